# Optimizing a Trainium2 kernel written in Bass

```python
import math
import jax, jax.numpy as jnp
from jax import lax
import numpy as np

D_MODEL = 1024
BATCH = 1
SEQ = 16384
DEPTH = 2
DEC_BATCH = 16
DEC_SEQ = 4096
PAST_LEN = 128

N_MIXERS = 2
N_A_LAYERS = (DEPTH + 1) // 2
N_B_LAYERS = DEPTH // 2

GDN_HEADS = 8
GDN_DK = 128
GDN_DV = 128
GDN_QK = GDN_HEADS * GDN_DK
GDN_V = GDN_HEADS * GDN_DV
GDN_QKV = 2 * GDN_QK + GDN_V
GDN_IN = GDN_QKV + GDN_V + 4 * GDN_HEADS
GDN_CONV = 5
GDN_CHUNK = 64

ATT_HEADS = 12
ATT_HEAD_DIM = 64
ATT_WIDTH = ATT_HEADS * ATT_HEAD_DIM
ATT_GROUPS = ((128, 1), (512, 4), (2048, 16))
ATT_HEADS_PER_GROUP = ATT_HEADS // len(ATT_GROUPS)
ATT_BLOCK = 64
REL_BUCKETS = 32
REL_MAX_DIST = 1024
NEG_INF = -1e30

D_FF = 2816
FFN_CONV = 3
EPS = 1e-6

kernel_name = "hybrid_gdn_dilated_encoder"


def rmsnorm(x, w):
    xf = x.astype(jnp.float32)
    y = xf * lax.rsqrt(jnp.mean(xf * xf, axis=-1, keepdims=True) + EPS)
    return (y * w.astype(jnp.float32)).astype(x.dtype)


def l2norm(x):
    xf = x.astype(jnp.float32)
    return xf * lax.rsqrt(jnp.sum(xf * xf, axis=-1, keepdims=True) + EPS)


def dwconv_centred(x, w, b=None):
    K = w.shape[0]
    p = K // 2
    T = x.shape[1]
    xp = jnp.pad(x, ((0, 0), (p, p), (0, 0)))
    y = xp[:, 0:T] * w[0]
    for i in range(1, K):
        y = y + xp[:, i:i + T] * w[i]
    if b is not None:
        y = y + b
    return y


def gated_delta_rule_chunked(q, k, v, g, beta):
    B, T, H, DK = q.shape
    DV = v.shape[-1]
    C = GDN_CHUNK
    N = T // C
    f32 = jnp.float32

    def to_chunks(t):
        return t.astype(f32).reshape(B, N, C, H, -1).transpose(0, 3, 1, 2, 4)

    q, k, v = to_chunks(q), to_chunks(k), to_chunks(v)
    g = g.astype(f32).reshape(B, N, C, H).transpose(0, 3, 1, 2)
    beta = beta.astype(f32).reshape(B, N, C, H).transpose(0, 3, 1, 2)
    gc = jnp.cumsum(g, axis=-1)
    incl = np.tril(np.ones((C, C), bool))
    strict = np.tril(np.ones((C, C), bool), -1)
    diff = gc[..., :, None] - gc[..., None, :]
    decay = jnp.where(incl, jnp.exp(jnp.where(incl, diff, 0.0)), 0.0)
    kb = k * beta[..., None]
    lower = jnp.where(strict, jnp.einsum("bhncd,bhnsd->bhncs", kb, k) * decay, 0.0)
    a_mat = lower + jnp.eye(C, dtype=f32)
    rhs = jnp.concatenate([v * beta[..., None], kb * jnp.exp(gc)[..., None]], axis=-1)
    sol = lax.linalg.triangular_solve(a_mat, rhs, left_side=True, lower=True, unit_diagonal=True)
    u, w = sol[..., :DV], sol[..., DV:]
    attn = jnp.where(incl, jnp.einsum("bhncd,bhnsd->bhncs", q, k) * decay, 0.0)
    g_last = gc[..., -1]
    q_dec = q * jnp.exp(gc)[..., None]
    k_dec = k * jnp.exp(g_last[..., None] - gc)[..., None]

    def step(S, inp):
        u_i, w_i, q_i, k_i, a_i, gl_i = inp
        v_new = u_i - jnp.einsum("bhck,bhkv->bhcv", w_i, S)
        o_i = jnp.einsum("bhck,bhkv->bhcv", q_i, S) + jnp.einsum("bhcs,bhsv->bhcv", a_i, v_new)
        S = S * jnp.exp(gl_i)[..., None, None] + jnp.einsum("bhck,bhcv->bhkv", k_i, v_new)
        return S, o_i

    def lead(t):
        return jnp.moveaxis(t, 2, 0)

    xs = (lead(u), lead(w), lead(q_dec), lead(k_dec), lead(attn), lead(g_last))
    _, o = lax.scan(step, jnp.zeros((B, H, DK, DV), f32), xs)
    return o.transpose(1, 0, 3, 2, 4).reshape(B, T, H, DV)


def gdn_mixer(x, w_in, conv_w, a_log, dt_bias, onorm_w, w_out):
    B, T, _ = x.shape
    proj = x @ w_in
    qkv = jax.nn.silu(dwconv_centred(proj[..., :GDN_QKV], conv_w))
    z = proj[..., GDN_QKV:GDN_QKV + GDN_V].reshape(B, T, GDN_HEADS, GDN_DV)
    gates = proj[..., GDN_QKV + GDN_V:].astype(jnp.float32).reshape(B, T, 4, GDN_HEADS)
    q = l2norm(qkv[..., :GDN_QK].reshape(B, T, GDN_HEADS, GDN_DK)) * (GDN_DK ** -0.5)
    k = l2norm(qkv[..., GDN_QK:2 * GDN_QK].reshape(B, T, GDN_HEADS, GDN_DK))
    v = qkv[..., 2 * GDN_QK:].reshape(B, T, GDN_HEADS, GDN_DV)
    a_log = a_log.astype(jnp.float32)
    dt_bias = dt_bias.astype(jnp.float32)
    g_f = -jnp.exp(a_log[0]) * jax.nn.softplus(gates[:, :, 0] + dt_bias[0])
    g_b = -jnp.exp(a_log[1]) * jax.nn.softplus(gates[:, :, 1] + dt_bias[1])
    beta_f = jax.nn.sigmoid(gates[:, :, 2])
    beta_b = jax.nn.sigmoid(gates[:, :, 3])
    o_f = gated_delta_rule_chunked(q, k, v, g_f, beta_f)
    flip = lambda t: t[:, ::-1]
    o_b = flip(gated_delta_rule_chunked(flip(q), flip(k), flip(v), flip(g_b), flip(beta_b)))
    o = (o_f + o_b).astype(x.dtype)
    o = rmsnorm(o, onorm_w) * jax.nn.silu(z)
    return o.reshape(B, T, GDN_V) @ w_out


def t5_bucket(rel):
    nb = REL_BUCKETS // 2
    max_exact = nb // 2
    n = np.abs(rel)
    large = max_exact + (np.log(np.maximum(n, max_exact) / max_exact) / math.log(REL_MAX_DIST / max_exact) * (nb - max_exact)).astype(np.int32)
    large = np.minimum(large, nb - 1)
    return (np.where(rel > 0, nb, 0) + np.where(n < max_exact, n, large)).astype(np.int32)


def dilated_window_attention(q, k, v, bias_tab, window, dilation):
    B, T, Hg, dh = q.shape
    d = dilation
    radius = window // (2 * d)
    L = T // d
    nblk = -(-L // ATT_BLOCK)
    Lp = nblk * ATT_BLOCK

    def to_sub(t):
        t = t.reshape(B, L, d, Hg, dh).transpose(0, 2, 1, 3, 4)
        return jnp.pad(t, ((0, 0), (0, 0), (0, Lp - L), (0, 0), (0, 0)))

    def neighbours(t):
        t = jnp.pad(to_sub(t), ((0, 0), (0, 0), (ATT_BLOCK, ATT_BLOCK), (0, 0), (0, 0)))
        t = t.reshape(B, d, nblk + 2, ATT_BLOCK, Hg, dh)
        return jnp.concatenate([t[:, :, :-2], t[:, :, 1:-1], t[:, :, 2:]], axis=3)

    qs = to_sub(q).reshape(B, d, nblk, ATT_BLOCK, Hg, dh)
    kw, vw = neighbours(k), neighbours(v)
    qi = np.arange(ATT_BLOCK)[:, None]
    ki = np.arange(3 * ATT_BLOCK)[None, :] - ATT_BLOCK
    rel = ki - qi
    in_band = np.abs(rel) <= radius
    s_k = np.arange(nblk)[:, None, None] * ATT_BLOCK + ki[None]
    mask = in_band[None] & (s_k >= 0) & (s_k < L)
    bias = bias_tab.astype(jnp.float32)[t5_bucket(rel * d)]
    bias = bias.transpose(2, 0, 1)
    logits = jnp.einsum("brnqhd,brnkhd->brnhqk", qs, kw).astype(jnp.float32) * (dh ** -0.5) + bias
    logits = jnp.where(mask[None, None, :, None], logits, NEG_INF)
    m = jnp.max(logits, axis=-1, keepdims=True)
    p = jnp.exp(logits - m)
    s = jnp.sum(p, axis=-1)
    o = jnp.einsum("brnhqk,brnkhd->brnqhd", p, vw.astype(jnp.float32)) / s.transpose(0, 1, 2, 4, 3)[..., None]
    lse = (m[..., 0] + jnp.log(s)).transpose(0, 1, 2, 4, 3)
    o = o.reshape(B, d, Lp, Hg, dh)[:, :, :L].transpose(0, 2, 1, 3, 4).reshape(B, T, Hg, dh)
    lse = lse.reshape(B, d, Lp, Hg)[:, :, :L].transpose(0, 2, 1, 3).reshape(B, T, Hg)
    return o, lse


def dilated_mixer(x, w_in, rel_bias, w_out):
    B, T, _ = x.shape
    proj = (x @ w_in).reshape(B, T, 3, ATT_HEADS, ATT_HEAD_DIM)
    q, k, v = proj[:, :, 0], proj[:, :, 1], proj[:, :, 2]
    outs, lses = [], []
    for gi, (window, dil) in enumerate(ATT_GROUPS):
        hs = slice(gi * ATT_HEADS_PER_GROUP, (gi + 1) * ATT_HEADS_PER_GROUP)
        o, lse = dilated_window_attention(q[:, :, hs], k[:, :, hs], v[:, :, hs], rel_bias[:, hs], window, dil)
        outs.append(o)
        lses.append(lse)
    alpha = jax.nn.softmax(jnp.stack(lses, axis=0), axis=0)
    o = jnp.concatenate([outs[gi] * alpha[gi][..., None] for gi in range(len(ATT_GROUPS))], axis=2)
    return o.astype(x.dtype).reshape(B, T, ATT_WIDTH) @ w_out


def conv_ffn(x, w_up, conv_w, conv_b, w_down):
    h = dwconv_centred(x @ w_up, conv_w, conv_b)
    val, gate = h[..., :D_FF], h[..., D_FF:]
    return (jax.nn.silu(gate) * val) @ w_down


def trunk(x, norm_mix, norm_ffn, norm_final, gdn_w_in, gdn_conv, gdn_a_log, gdn_dt_bias, gdn_norm, gdn_w_out,
          att_w_in, att_w_out, rel_bias, ffn_w_up, ffn_conv, ffn_conv_b, ffn_w_down):
    for i in range(DEPTH):
        h = rmsnorm(x, norm_mix[i])
        j = i // N_MIXERS
        if i % N_MIXERS == 0:
            h = gdn_mixer(h, gdn_w_in[j], gdn_conv[j], gdn_a_log[j], gdn_dt_bias[j], gdn_norm[j], gdn_w_out[j])
        else:
            h = dilated_mixer(h, att_w_in[j], rel_bias, att_w_out[j])
        x = x + h
        x = x + conv_ffn(rmsnorm(x, norm_ffn[i]), ffn_w_up[i], ffn_conv[i], ffn_conv_b[i], ffn_w_down[i])
    return rmsnorm(x, norm_final)


def setup_inputs(seed: int = 0) -> dict:
    key = jax.random.key(seed)
    ks = jax.random.split(key, 20)
    f32 = jnp.float32

    def dense(k, shape, fan_in):
        return jax.random.normal(k, shape, f32) * (fan_in ** -0.5)

    def gain(k, shape):
        return 1.0 + 0.02 * jax.random.normal(k, shape, f32)

    x_prompt = jax.random.normal(ks[0], (BATCH, SEQ, D_MODEL), f32)
    x_sample = jax.random.normal(ks[1], (DEC_BATCH, DEC_SEQ, D_MODEL), f32)
    norm_mix = gain(ks[2], (DEPTH, D_MODEL))
    norm_ffn = gain(ks[3], (DEPTH, D_MODEL))
    norm_final = gain(ks[4], (D_MODEL,))
    gdn_w_in = dense(ks[5], (N_A_LAYERS, D_MODEL, GDN_IN), D_MODEL)
    gdn_conv = dense(ks[6], (N_A_LAYERS, GDN_CONV, GDN_QKV), GDN_CONV)
    gdn_a_log = jnp.log(jax.random.uniform(ks[7], (N_A_LAYERS, 2, GDN_HEADS), f32, 1.0, 16.0))
    dt = jnp.exp(jax.random.uniform(ks[8], (N_A_LAYERS, 2, GDN_HEADS), f32, math.log(1e-3), math.log(1e-1)))
    gdn_dt_bias = dt + jnp.log(-jnp.expm1(-dt))
    gdn_norm = gain(ks[9], (N_A_LAYERS, GDN_DV))
    gdn_w_out = dense(ks[10], (N_A_LAYERS, GDN_V, D_MODEL), GDN_V)
    att_w_in = dense(ks[11], (N_B_LAYERS, D_MODEL, 3 * ATT_WIDTH), D_MODEL)
    att_w_out = dense(ks[12], (N_B_LAYERS, ATT_WIDTH, D_MODEL), ATT_WIDTH)
    rel_bias = 0.5 * jax.random.normal(ks[13], (REL_BUCKETS, ATT_HEADS), f32)
    ffn_w_up = dense(ks[14], (DEPTH, D_MODEL, 2 * D_FF), D_MODEL)
    ffn_conv = dense(ks[15], (DEPTH, FFN_CONV, 2 * D_FF), FFN_CONV)
    ffn_conv_b = 0.02 * jax.random.normal(ks[16], (DEPTH, 2 * D_FF), f32)
    ffn_w_down = dense(ks[17], (DEPTH, D_FF, D_MODEL), D_FF)
    return {"x_prompt": x_prompt, "x_sample": x_sample, "norm_mix": norm_mix, "norm_ffn": norm_ffn,
            "norm_final": norm_final, "gdn_w_in": gdn_w_in, "gdn_conv": gdn_conv, "gdn_a_log": gdn_a_log,
            "gdn_dt_bias": gdn_dt_bias, "gdn_norm": gdn_norm, "gdn_w_out": gdn_w_out, "att_w_in": att_w_in,
            "att_w_out": att_w_out, "rel_bias": rel_bias, "ffn_w_up": ffn_w_up, "ffn_conv": ffn_conv,
            "ffn_conv_b": ffn_conv_b, "ffn_w_down": ffn_w_down}


def reference(x_prompt, x_sample, norm_mix, norm_ffn, norm_final, gdn_w_in, gdn_conv, gdn_a_log, gdn_dt_bias,
              gdn_norm, gdn_w_out, att_w_in, att_w_out, rel_bias, ffn_w_up, ffn_conv, ffn_conv_b, ffn_w_down):
    y_prompt = trunk(x_prompt, norm_mix, norm_ffn, norm_final, gdn_w_in, gdn_conv, gdn_a_log, gdn_dt_bias, gdn_norm,
                     gdn_w_out, att_w_in, att_w_out, rel_bias, ffn_w_up, ffn_conv, ffn_conv_b, ffn_w_down)
    y_sample = trunk(x_sample, norm_mix, norm_ffn, norm_final, gdn_w_in, gdn_conv, gdn_a_log, gdn_dt_bias, gdn_norm,
                     gdn_w_out, att_w_in, att_w_out, rel_bias, ffn_w_up, ffn_conv, ffn_conv_b, ffn_w_down)
    return (y_prompt, y_sample)
```

```python
import math
import numpy as np
import ml_dtypes
import concourse.bass as bass
import concourse.mybir as mybir
from concourse.bass_utils import run_bass_kernel_spmd

F32 = mybir.dt.float32
BF16 = mybir.dt.bfloat16
AF = mybir.ActivationFunctionType
ALU = mybir.AluOpType
AX = mybir.AxisListType

D = 1024
EPS = 1e-6
GDN_H = 8
GDN_IN = 4128
D_FF = 2816
ATT_H = 12
ATT_DH = 64
ATT_W = 768
GROUPS = ((128, 1), (512, 4), (2048, 16))
NEG = -30000.0


class Sched:
    ENGS = ("pe", "dve", "act", "pool", "sp")
    EPOCH = 20000
    NDMA = {"sp": 24, "pool": 12, "act": 6}

    def __init__(self, nc):
        self.nc = nc
        self.ops = []
        self.res_w = {}
        self.res_r = {}

    def add(self, eng, fn, reads=(), writes=(), dma=False):
        deps = set()
        for k in reads:
            w = self.res_w.get(k)
            if w is not None:
                deps.add(w)
            if isinstance(k, tuple) and k[0] == "ps":
                for r in self.res_r.get(k, ()):
                    if self.ops[r][0] != eng:
                        deps.add(r)
        for k in writes:
            w = self.res_w.get(k)
            if w is not None:
                deps.add(w)
            for r in self.res_r.get(k, ()):
                deps.add(r)
        oid = len(self.ops)
        deps.discard(oid)
        self.ops.append([eng, fn, deps, dma])
        for k in reads:
            lst = self.res_r.setdefault(k, [])
            if not dma:
                lst[:] = [r for r in lst if self.ops[r][3] or self.ops[r][0] != eng]
            lst.append(oid)
        for k in writes:
            self.res_w[k] = oid
            self.res_r[k] = []
        return oid

    def pe(self, fn, reads=(), writes=()):
        return self.add("pe", fn, reads, writes)

    def dve(self, fn, reads=(), writes=()):
        return self.add("dve", fn, reads, writes)

    def act(self, fn, reads=(), writes=()):
        return self.add("act", fn, reads, writes)

    def pool(self, fn, reads=(), writes=()):
        return self.add("pool", fn, reads, writes)

    def dma(self, out, in_, reads=(), writes=(), eng="sp", transpose=False):
        if transpose:
            fn = lambda e: e.dma_start_transpose(out=out, in_=in_)
        else:
            fn = lambda e: e.dma_start(out=out, in_=in_)
        return self.add(eng, fn, reads, writes, dma=True)

    def emit(self, stack):
        nc = self.nc
        ops = self.ops
        n = len(ops)
        flagged = [False] * n
        for i, (eng, fn, deps, dma) in enumerate(ops):
            for d in deps:
                de, _, _, ddma = ops[d]
                if ddma:
                    continue
                if de == "pe" and eng == "pe" and not dma:
                    continue
                flagged[d] = True
        cnt = {e: 0 for e in self.ENGS}
        fidx = [None] * n
        for i, (eng, fn, deps, dma) in enumerate(ops):
            if flagged[i] and not dma:
                fidx[i] = cnt[eng]
                cnt[eng] += 1
        self.flag_counts = dict(cnt)
        csems = {}
        for e in self.ENGS:
            ne = cnt[e] // self.EPOCH + 1
            csems[e] = [stack.enter_context(nc.semaphore(f"c_{e}_{j}")) for j in range(ne)]
        dcnt = {e: 0 for e in self.ENGS}
        dinfo = [None] * n
        for i, (eng, fn, deps, dma) in enumerate(ops):
            if dma:
                dinfo[i] = (eng, dcnt[eng])
                dcnt[eng] += 1
        dsems = {}
        for e in self.ENGS:
            if dcnt[e]:
                nd = self.NDMA.get(e, 8)
                dsems[e] = [stack.enter_context(nc.semaphore(f"d_{e}_{j}")) for j in range(nd)]

        def dma_target(i):
            e, j = dinfo[i]
            nd = len(dsems[e])
            return dsems[e][j % nd], 16 * (j // nd + 1), (e, j % nd)

        per_eng = {e: [] for e in self.ENGS}
        for i, op in enumerate(ops):
            per_eng[op[0]].append(i)

        block = stack.enter_context(nc.Block())
        EPOCH = self.EPOCH

        def run_engine(ename, engobj):
            waited_c = {}
            waited_d = {}
            for i in per_eng[ename]:
                eng, fn, deps, dma = ops[i]
                need_c = {}
                need_d = {}
                for d in deps:
                    de, _, _, ddma = ops[d]
                    if ddma:
                        sem, val, key = dma_target(d)
                        if waited_d.get(key, 0) < val and need_d.get(key, (None, 0))[1] < val:
                            need_d[key] = (sem, val)
                    else:
                        if de == "pe" and eng == "pe" and not dma:
                            continue
                        fi = fidx[d]
                        if waited_c.get(de, -1) < fi and need_c.get(de, -1) < fi:
                            need_c[de] = fi
                if dma:
                    e, j = dinfo[i]
                    nd = len(dsems[e])
                    if j >= nd:
                        key = (e, j % nd)
                        val = 16 * (j // nd)
                        if waited_d.get(key, 0) < val and need_d.get(key, (None, 0))[1] < val:
                            need_d[key] = (dsems[e][j % nd], val)
                for de, fi in need_c.items():
                    engobj.wait_ge(csems[de][fi // EPOCH], fi % EPOCH + 1)
                    waited_c[de] = fi
                for key, (sem, val) in need_d.items():
                    engobj.wait_ge(sem, val)
                    waited_d[key] = val
                ins = fn(engobj)
                if dma:
                    sem, val, key = dma_target(i)
                    ins.then_inc(sem, 16)
                elif flagged[i]:
                    fi = fidx[i]
                    ins.then_inc(csems[eng][fi // EPOCH], 1)
            if ename == "sp":
                for e in self.ENGS:
                    if dcnt[e]:
                        nd = len(dsems[e])
                        for slot in range(min(nd, dcnt[e])):
                            last_j = ((dcnt[e] - 1 - slot) // nd) * nd + slot
                            val = 16 * (last_j // nd + 1)
                            if waited_d.get((e, slot), 0) < val:
                                engobj.wait_ge(dsems[e][slot], val)

        @block.tensor
        def _(e):
            run_engine("pe", e)

        @block.vector
        def _(e):
            run_engine("dve", e)

        @block.scalar
        def _(e):
            run_engine("act", e)

        @block.gpsimd
        def _(e):
            run_engine("pool", e)

        @block.sync
        def _(e):
            run_engine("sp", e)


class SBAlloc:
    def __init__(self, nc, nbytes=200 * 1024):
        self.t = nc.alloc_sbuf_tensor("sb_all", [128, nbytes // 2], BF16)
        self.nbytes = nbytes
        self.off = 0
        self.mark = 0

    def alloc(self, free_shape, dtype):
        esz = 4 if dtype == F32 else 2
        nel = int(np.prod(free_shape))
        nb = nel * esz
        self.off = (self.off + 63) // 64 * 64
        assert self.off + nb <= self.nbytes, f"SBUF overflow {self.off + nb}"
        v = self.t[:, self.off // 2:(self.off + nb) // 2]
        if dtype == F32:
            v = v.bitcast(F32)
        self.off += nb
        if len(free_shape) == 2:
            v = v.rearrange("p (a b) -> p a b", a=free_shape[0])
        elif len(free_shape) == 3:
            v = v.rearrange("p (a b c) -> p a b c", a=free_shape[0], b=free_shape[1])
        return v

    def set_mark(self):
        self.mark = self.off

    def reset(self):
        self.off = self.mark


class Ctx:
    def __init__(self, nc, seqs, debug=False):
        self.nc = nc
        self.S = Sched(nc)
        self.seqs = list(seqs)
        self.offs = [int(x) for x in np.cumsum([0] + self.seqs[:-1])]
        self.T = int(sum(self.seqs))
        self.debug = debug
        self.sb = SBAlloc(nc)
        self.ps = [nc.alloc_psum_tensor(f"psb{i}", [128, 512], F32).ap() for i in range(8)]
        self.ps_rr = 0
        self.ev_rr = 0
        self.dram = {}
        self.inputs = {}
        self.nw = [-(-t // 254) for t in self.seqs]
        self.pstride = [254 * n + 2 for n in self.nw]
        self.poffs = [int(x) for x in np.cumsum([0] + self.pstride[:-1])]
        self.TP = int(sum(self.pstride))
        self.bar_deps = {}

    def with_seqs(self, seqs):
        import copy
        c2 = copy.copy(self)
        c2.seqs = list(seqs)
        c2.offs = [int(x) for x in np.cumsum([0] + c2.seqs[:-1])]
        c2.T = int(sum(c2.seqs))
        c2.nw = [-(-t // 254) for t in c2.seqs]
        c2.pstride = [254 * n + 2 for n in c2.nw]
        c2.poffs = [int(x) for x in np.cumsum([0] + c2.pstride[:-1])]
        c2.TP = int(sum(c2.pstride))
        return c2

    def inp(self, name, shape, dtype=F32):
        if name in self.inputs:
            return self.inputs[name]
        t = self.nc.dram_tensor(name, list(shape), dtype, kind="ExternalInput").ap()
        self.inputs[name] = t
        return t

    def scratch(self, name, shape, dtype, out=False):
        kind = "ExternalOutput" if (out or self.debug) else "Internal"
        t = self.nc.dram_tensor(name, list(shape), dtype, kind=kind).ap()
        self.dram[name] = t
        return t

    def bank(self):
        i = self.ps_rr % 8
        self.ps_rr += 1
        return i

    def evac(self, out, in_, reads, writes, scale=None):
        S = self.S
        self.ev_rr += 1
        if self.ev_rr % 2 == 0:
            if scale is None:
                S.act(lambda e: e.activation(out=out, in_=in_, func=AF.Copy), reads, writes)
            else:
                S.act(lambda e: e.activation(out=out, in_=in_, func=AF.Copy, scale=scale), reads, writes)
        else:
            if scale is None:
                S.dve(lambda e: e.tensor_copy(out=out, in_=in_), reads, writes)
            else:
                S.dve(lambda e: e.tensor_scalar(out=out, in0=in_, scalar1=scale, scalar2=None, op0=ALU.mult),
                      reads, writes)

    def barrier(self):
        S = self.S
        last = {}
        dmas = []
        for i, op in enumerate(S.ops):
            if op[3]:
                dmas.append(i)
            else:
                last[op[0]] = i
        deps = set(last.values())
        for e in ("sp", "pool", "act"):
            de = [i for i in dmas if S.ops[i][0] == e]
            deps |= set(de[-S.NDMA.get(e, 8):])
        for e in ("pe", "dve", "act", "pool"):
            S.ops.append([e, (lambda en: en.nop()) if e != "pe" else (lambda en: en.nop()), set(deps), False])
        S.ops.append(["sp", lambda en: en.nop(), set(deps), False])
        S.res_w = {}
        S.res_r = {}


def load_w_bf16(C, dst, src, key, kt):
    for k in range(kt):
        C.S.dma(dst[:, k, :], src[k * 128:(k + 1) * 128, :], reads=(), writes=[(key, k)], eng="pool")


def load_bcast(C, dst, src_row, key):
    n = src_row.shape[-1]
    C.S.dma(dst, src_row.to_broadcast([128, n]), reads=(), writes=[key])


def norm_tile(C, x_t, xkeys, wbc, wkey, out_t, okeys, ss, sskey, junk, jkey, ns, func_out=None):
    S = C.S
    for s in range(ns):
        S.act(lambda e, s=s: e.activation(out=junk, in_=x_t[:, s, :], func=AF.Square, accum_out=ss[:, s:s + 1]),
              reads=[xkeys[s]], writes=[jkey, (sskey, s)])
    sk = [(sskey, s) for s in range(ns)]
    S.dve(lambda e: e.tensor_scalar(out=ss[:, 0:ns], in0=ss[:, 0:ns], scalar1=1.0 / D, scalar2=EPS,
                                    op0=ALU.mult, op1=ALU.add), reads=sk, writes=sk)
    S.act(lambda e: e.activation(out=ss[:, 0:ns], in_=ss[:, 0:ns], func=AF.Sqrt), reads=sk, writes=sk)
    S.dve(lambda e: e.reciprocal(out=ss[:, 0:ns], in_=ss[:, 0:ns]), reads=sk, writes=sk)
    for s in range(ns):
        S.dve(lambda e, s=s: e.scalar_tensor_tensor(out=out_t[:, s, :], in0=x_t[:, s, :], scalar=ss[:, s:s + 1],
                                                    in1=wbc, op0=ALU.mult, op1=ALU.mult),
              reads=[xkeys[s], (sskey, s), wkey], writes=[okeys[s]])


def phase_norm0(C, x_src, w_row, xn_dst):
    S, sb = C.S, C.sb
    sb.reset()
    wbc = sb.alloc((D,), F32)
    load_bcast(C, wbc, w_row, "n0w")
    xt = [sb.alloc((4, D), F32) for _ in range(2)]
    xn = [sb.alloc((4, D), BF16) for _ in range(2)]
    ss = [sb.alloc((4,), F32) for _ in range(2)]
    junk = sb.alloc((D,), BF16)
    nt = C.T // 512
    xs = x_src.rearrange("(n s p) d -> n p s d", s=4, p=128)
    xd = xn_dst.rearrange("(n s p) d -> n p s d", s=4, p=128)
    for i in range(nt):
        b = i % 2
        S.dma(xt[b], xs[i], reads=[], writes=[("n0x", b)])
        norm_tile(C, xt[b], [("n0x", b)] * 4, wbc, "n0w", xn[b], [("n0o", b, s) for s in range(4)], ss[b],
                  ("n0ss", b), junk, "n0j", 4)
        S.dma(xd[i], xn[b], reads=[("n0o", b, s) for s in range(4)], writes=[("xn_d", i)])
    C.barrier()


def phase_proj(C, xn_src, W_src, nout, fm, tm, tag):
    S, sb = C.S, C.sb
    sb.reset()
    W = sb.alloc((8, nout), BF16)
    load_w_bf16(C, W, W_src, tag + "W", 8)
    wkeys = [(tag + "W", k) for k in range(8)]
    xnT = [sb.alloc((8, 512), BF16) for _ in range(2)]
    st_fm = [sb.alloc((4, 512), BF16) for _ in range(2)]
    st_tm = {}
    for j, (c0, n, dst, dt) in enumerate(tm):
        st_tm[j] = [sb.alloc((4, n), dt) for _ in range(2)]
    nt = C.T // 512
    fm_rr = 0
    for i in range(nt):
        b = i % 2
        t0 = i * 512
        for k in range(8):
            S.dma(xnT[b][:, k, :], xn_src[t0:t0 + 512, k * 128:(k + 1) * 128], reads=[("xn_d", i)],
                  writes=[(tag + "xT", b, k)], transpose=True)
        xkeys = [(tag + "xT", b, k) for k in range(8)]
        for (c0, n, dst) in fm:
            nm = n // 128
            for m0 in range(0, nm, 4):
                sbuf = fm_rr % 2
                fm_rr += 1
                mm = min(4, nm - m0)
                for j in range(mm):
                    m = m0 + j
                    bk = C.bank()
                    for k in range(8):
                        S.pe(lambda e, bk=bk, k=k, m=m, c0=c0, b=b: e.matmul(
                            C.ps[bk], lhsT=W[:, k, c0 + m * 128:c0 + (m + 1) * 128], rhs=xnT[b][:, k, :],
                            start=(k == 0), stop=(k == 7)),
                            reads=[wkeys[k], xkeys[k]], writes=[("ps", bk)])
                    C.evac(st_fm[sbuf][:, j, :], C.ps[bk], reads=[("ps", bk)], writes=[(tag + "sf", sbuf, j)])
                d = dst[m0 * 128:(m0 + mm) * 128, t0:t0 + 512].rearrange("(m p) t -> p m t", p=128)
                S.dma(d, st_fm[sbuf][:, 0:mm, :], reads=[(tag + "sf", sbuf, j) for j in range(mm)],
                      writes=[(tag + "fm_d", id(dst), i)])
        for j, (c0, n, dst, dt) in enumerate(tm):
            stg = st_tm[j][b]
            for s in range(4):
                bk = C.bank()
                for k in range(8):
                    S.pe(lambda e, bk=bk, k=k, s=s, c0=c0, n=n, b=b: e.matmul(
                        C.ps[bk][:, 0:n], lhsT=xnT[b][:, k, s * 128:(s + 1) * 128], rhs=W[:, k, c0:c0 + n],
                        start=(k == 0), stop=(k == 7)),
                        reads=[wkeys[k], xkeys[k]], writes=[("ps", bk)])
                C.evac(stg[:, s, :], C.ps[bk][:, 0:n], reads=[("ps", bk)], writes=[(tag + "st", j, b, s)])
            d = dst[t0:t0 + 512, :].rearrange("(s p) c -> p s c", p=128)
            S.dma(d, stg, reads=[(tag + "st", j, b, s) for s in range(4)], writes=[(tag + "tm_d", j, i)])
    C.barrier()


def load_cols(C, dst, src_rows, nrow, ident32, tag):
    S, sb = C.S, C.sb
    tmp = sb.alloc((128,), F32)
    S.dma(tmp[0:nrow, :], src_rows, reads=[], writes=[("tmpc", tag)])
    bk = C.bank()
    S.pe(lambda e: e.transpose(C.ps[bk][:, 0:nrow], tmp[0:nrow, :], ident32[0:nrow, 0:nrow]),
         reads=[("tmpc", tag), "ident32"], writes=[("ps", bk)])
    S.dve(lambda e: e.tensor_copy(out=dst, in_=C.ps[bk][:, 0:nrow]), reads=[("ps", bk)], writes=[tag])


def phase_outproj(C, mix_src, kdim, Wo_src, x_src, w_row, xr_dst, xn_dst, tag, mix_key):
    S, sb = C.S, C.sb
    sb.reset()
    kt = kdim // 128
    Wo = sb.alloc((kt, D), BF16)
    load_w_bf16(C, Wo, Wo_src, tag + "W", kt)
    wbc = sb.alloc((D,), F32)
    load_bcast(C, wbc, w_row, tag + "nw")
    mixT = [sb.alloc((kt, 512), BF16) for _ in range(2)]
    xt = [sb.alloc((4, D), F32) for _ in range(2)]
    xn = [sb.alloc((4, D), BF16) for _ in range(2)]
    ss = [sb.alloc((4,), F32) for _ in range(2)]
    junk = sb.alloc((D,), BF16)
    zero = sb.alloc((D,), BF16)
    S.dve(lambda e: e.memset(zero, 0.0), reads=[], writes=[tag + "zero"])
    for si, T in enumerate(C.seqs):
        p0 = C.poffs[si]
        S.dma(xn_dst[p0:p0 + 1, :], zero[0:1, :], reads=[tag + "zero"], writes=[("xnp_pad", si, 0)])
        r0 = p0 + 1 + T
        r1 = p0 + C.pstride[si]
        while r0 < r1:
            n = min(128, r1 - r0)
            S.dma(xn_dst[r0:r0 + n, :], zero[0:n, :], reads=[tag + "zero"], writes=[("xnp_pad", si, r0)])
            r0 += n
    ti = 0
    for si, T in enumerate(C.seqs):
        for i in range(T // 512):
            b = ti % 2
            g0 = C.offs[si] + i * 512
            pr0 = C.poffs[si] + 1 + i * 512
            for k in range(kt):
                S.dma(mixT[b][:, k, :], mix_src[g0:g0 + 512, k * 128:(k + 1) * 128], reads=[(mix_key, g0 // 512)],
                      writes=[(tag + "mT", b, k)], transpose=True)
            S.dma(xt[b], x_src[g0:g0 + 512, :].rearrange("(s p) d -> p s d", p=128), reads=[("xres_d", g0 // 512)],
                  writes=[(tag + "x", b, s) for s in range(4)])
            for s_ in range(4):
                for h in range(2):
                    bk = C.bank()
                    for k in range(kt):
                        S.pe(lambda e, bk=bk, k=k, s_=s_, h=h, b=b: e.matmul(
                            C.ps[bk], lhsT=mixT[b][:, k, s_ * 128:(s_ + 1) * 128], rhs=Wo[:, k, h * 512:(h + 1) * 512],
                            start=(k == 0), stop=(k == kt - 1)),
                            reads=[(tag + "W", k), (tag + "mT", b, k)], writes=[("ps", bk)])
                    S.dve(lambda e, bk=bk, s_=s_, h=h, b=b: e.tensor_tensor(
                        out=xt[b][:, s_, h * 512:(h + 1) * 512], in0=C.ps[bk], in1=xt[b][:, s_, h * 512:(h + 1) * 512],
                        op=ALU.add), reads=[("ps", bk), (tag + "x", b, s_)], writes=[(tag + "x", b, s_)])
            norm_tile(C, xt[b], [(tag + "x", b, s) for s in range(4)], wbc, tag + "nw", xn[b],
                      [(tag + "xn", b, s) for s in range(4)], ss[b], (tag + "ss", b), junk, tag + "j", 4)
            S.dma(xr_dst[pr0:pr0 + 512, :].rearrange("(s p) d -> p s d", p=128), xt[b],
                  reads=[(tag + "x", b, s) for s in range(4)], writes=[("xrp_d", si, i)])
            S.dma(xn_dst[pr0:pr0 + 512, :].rearrange("(s p) d -> p s d", p=128), xn[b],
                  reads=[(tag + "xn", b, s) for s in range(4)], writes=[("xnp_d", si, i)])
            ti += 1
    C.barrier()


def phase_ffn(C, xnp_src, xrp_src, Wu_src, Wd_src, cw_src, cb_src, w_row, ident32, out_specs, tag, final):
    S, sb = C.S, C.sb
    sb.reset()
    Wu = sb.alloc((8, 2 * D_FF), BF16)
    Wd = sb.alloc((22, D), BF16)
    load_w_bf16(C, Wu, Wu_src, tag + "Wu", 8)
    load_w_bf16(C, Wd, Wd_src, tag + "Wd", 22)
    wbc = sb.alloc((D,), F32)
    load_bcast(C, wbc, w_row, tag + "nw")
    cw = sb.alloc((3, 44), F32)
    cb = sb.alloc((44,), F32)
    for i in range(3):
        load_cols(C, cw[:, i, :], cw_src[i].rearrange("(m p) -> m p", p=128), 44, ident32, (tag + "cw", i))
    load_cols(C, cb, cb_src.rearrange("(m p) -> m p", p=128), 44, ident32, tag + "cb")
    cwk = [(tag + "cw", i) for i in range(3)] + [tag + "cb"]
    xnT = [sb.alloc((8, 256), BF16) for _ in range(2)]
    xt = [sb.alloc((2, D), F32) for _ in range(2)]
    if final:
        xo1 = sb.alloc((2, D), F32)
        xo = [xo1, xo1]
    else:
        xo = [sb.alloc((2, D), BF16) for _ in range(2)]
    a_t1 = sb.alloc((22, 256), BF16)
    a_t = [a_t1, a_t1]
    S.pool(lambda e: e.memset(a_t1, 0.0), reads=[], writes=[(tag + "a", 0, m) for m in range(22)])
    tv = [sb.alloc((256,), F32) for _ in range(2)]
    tg = [sb.alloc((256,), F32) for _ in range(2)]
    ss = [sb.alloc((2,), F32) for _ in range(2)]
    junk = sb.alloc((D,), BF16)
    wi = 0
    for si, T in enumerate(C.seqs):
        for w in range(C.nw[si]):
            b = wi % 2
            u0 = 254 * w
            pr0 = C.poffs[si] + u0
            for k in range(8):
                S.dma(xnT[b][:, k, :], xnp_src[pr0:pr0 + 256, k * 128:(k + 1) * 128],
                      reads=[("xnp_d", si, j) for j in range(u0 // 512, min(T // 512, (u0 + 255) // 512 + 1))] +
                      [("xnp_pad", si, 0)], writes=[(tag + "xT", b, k)], transpose=True)
            S.dma(xt[b], xrp_src[pr0:pr0 + 256, :].rearrange("(s p) d -> p s d", p=128),
                  reads=[("xrp_d", si, j) for j in range(u0 // 512, min(T // 512, (u0 + 255) // 512 + 1))],
                  writes=[(tag + "x", b, s) for s in range(2)])
            xk = [(tag + "xT", b, k) for k in range(8)]
            for m in range(22):
                tb = m % 2
                bv = C.bank()
                for k in range(8):
                    S.pe(lambda e, bk=bv, k=k, m=m, b=b: e.matmul(
                        C.ps[bk][:, 0:256], lhsT=Wu[:, k, m * 128:(m + 1) * 128], rhs=xnT[b][:, k, :],
                        start=(k == 0), stop=(k == 7)), reads=[(tag + "Wu", k), xk[k]], writes=[("ps", bv)])
                bg = C.bank()
                for k in range(8):
                    S.pe(lambda e, bk=bg, k=k, m=m, b=b: e.matmul(
                        C.ps[bk][:, 0:256], lhsT=Wu[:, k, D_FF + m * 128:D_FF + (m + 1) * 128], rhs=xnT[b][:, k, :],
                        start=(k == 0), stop=(k == 7)), reads=[(tag + "Wu", k), xk[k]], writes=[("ps", bg)])
                for (bk, tt, mm, key) in ((bv, tv[tb], m, (tag + "tv", tb)), (bg, tg[tb], 22 + m, (tag + "tg", tb))):
                    S.act(lambda e, bk=bk, tt=tt, mm=mm: e.activation(
                        out=tt[:, 1:255], in_=C.ps[bk][:, 1:255], func=AF.Identity, scale=cw[:, 1, mm:mm + 1],
                        bias=cb[:, mm:mm + 1]), reads=[("ps", bk)] + cwk, writes=[key])
                    S.dve(lambda e, bk=bk, tt=tt, mm=mm: e.scalar_tensor_tensor(
                        out=tt[:, 1:255], in0=C.ps[bk][:, 0:254], scalar=cw[:, 0, mm:mm + 1], in1=tt[:, 1:255],
                        op0=ALU.mult, op1=ALU.add), reads=[("ps", bk), key] + cwk, writes=[key])
                    S.dve(lambda e, bk=bk, tt=tt, mm=mm: e.scalar_tensor_tensor(
                        out=tt[:, 1:255], in0=C.ps[bk][:, 2:256], scalar=cw[:, 2, mm:mm + 1], in1=tt[:, 1:255],
                        op0=ALU.mult, op1=ALU.add), reads=[("ps", bk), key] + cwk, writes=[key])
                S.act(lambda e, tb=tb: e.activation(out=tg[tb][:, 1:255], in_=tg[tb][:, 1:255], func=AF.Silu),
                      reads=[(tag + "tg", tb)], writes=[(tag + "tg", tb)])
                S.pool(lambda e, tb=tb, m=m, b=b: e.tensor_tensor(
                    out=a_t[b][:, m, 1:255], in0=tg[tb][:, 1:255], in1=tv[tb][:, 1:255], op=ALU.mult),
                    reads=[(tag + "tg", tb), (tag + "tv", tb)], writes=[(tag + "a", 0, m)])
            for s_ in range(2):
                for h in range(2):
                    bk = C.bank()
                    for kk in range(22):
                        S.pe(lambda e, bk=bk, kk=kk, s_=s_, h=h, b=b: e.matmul(
                            C.ps[bk], lhsT=a_t[b][:, kk, s_ * 128:(s_ + 1) * 128], rhs=Wd[:, kk, h * 512:(h + 1) * 512],
                            start=(kk == 0), stop=(kk == 21)),
                            reads=[(tag + "Wd", kk), (tag + "a", 0, kk)], writes=[("ps", bk)])
                    S.dve(lambda e, bk=bk, s_=s_, h=h, b=b: e.tensor_tensor(
                        out=xt[b][:, s_, h * 512:(h + 1) * 512], in0=C.ps[bk], in1=xt[b][:, s_, h * 512:(h + 1) * 512],
                        op=ALU.add), reads=[("ps", bk), (tag + "x", b, s_)], writes=[(tag + "x", b, s_)])
            norm_tile(C, xt[b], [(tag + "x", b, s) for s in range(2)], wbc, tag + "nw", xo[b],
                      [(tag + "xo", b if not final else 0, s) for s in range(2)], ss[b], (tag + "ss", b), junk, tag + "j", 2)
            jhi = min(254, T - u0)
            for s_ in range(2):
                plo = max(0, 1 - 128 * s_)
                phi = min(127, jhi - 128 * s_)
                if phi < plo:
                    continue
                t_lo = u0 + 128 * s_ + plo - 1
                n = phi - plo + 1
                out_specs(si, t_lo, n, xt[b][plo:plo + n, s_, :], xo[b][plo:plo + n, s_, :],
                          [(tag + "x", b, s_)], [(tag + "xo", b if not final else 0, s_)])
            wi += 1
    C.barrier()


def phase_gdn(C, projT, ztok, gates, of_s, og, K_, conv_src, alog_src, dtb_src, gnorm_src, NH=8, HB=4):
    S, sb = C.S, C.sb
    sb.reset()
    NC2 = 2 * NH
    ident_bf, ident32 = K_["ident_bf"], K_["ident32"]
    ck = list(K_["keys"])
    A16 = sb.alloc((NC2,), F32)
    DTB = sb.alloc((NC2,), F32)
    load_bcast(C, A16, alog_src, "gA16")
    load_bcast(C, DTB, dtb_src, "gDTB")
    S.act(lambda e: e.activation(out=A16, in_=A16, func=AF.Exp), reads=["gA16"], writes=["gA16"])
    S.dve(lambda e: e.tensor_scalar(out=A16, in0=A16, scalar1=-1.0, scalar2=None, op0=ALU.mult),
          reads=["gA16"], writes=["gA16"])
    sel_d = C.inp("csel", [16, 2048], F32)
    SELt = sb.alloc((16, 128), F32)
    S.dma(SELt[0:16, :, :], sel_d.rearrange("p (a b) -> p a b", a=16), reads=[], writes=["csel"])
    onec = sb.alloc((1,), F32)
    epsc = sb.alloc((1,), F32)
    S.dve(lambda e: e.memset(onec, 1.0), reads=[], writes=["gonec"])
    S.dve(lambda e: e.memset(epsc, EPS), reads=[], writes=["gepsc"])
    gnw = sb.alloc((128,), F32)
    load_bcast(C, gnw, gnorm_src, "gnw")
    cwg = sb.alloc((5, 3 * NH), F32)
    for i in range(5):
        load_cols(C, cwg[:, i, :], conv_src[i].rearrange("(m p) -> m p", p=128), 3 * NH, ident32, ("gcw", i))
    DG = sb.alloc((3 * HB * 5, 128), BF16)

    def build_DG(hg):
        for w3 in range(3):
            for hh in range(HB):
                ct = w3 * NH + hg * HB + hh
                lc = w3 * HB + hh
                for i in range(5):
                    S.dve(lambda e, ct=ct, lc=lc, i=i: e.tensor_scalar(out=DG[:, lc * 5 + i, :], in0=ident_bf,
                                                                      scalar1=cwg[:, i, ct:ct + 1], scalar2=None, op0=ALU.mult),
                          reads=[("gcw", i), "ident_bf"], writes=[("gDG", lc)])
    SEGT = 32
    NTmax = SEGT
    GA = sb.alloc((NTmax, 2 * NC2), F32)
    G = sb.alloc((NTmax, NC2), F32)
    GH = sb.alloc((NTmax, NC2), BF16)
    GLo = sb.alloc((NTmax, NC2), BF16)
    BETA = sb.alloc((NTmax, NC2), F32)
    NBETA = sb.alloc((NTmax, NC2), F32)
    GC = sb.alloc((NTmax, NC2), F32)
    EGC = sb.alloc((NTmax, NC2), F32)
    EKD = sb.alloc((NTmax, NC2), F32)
    DEC = sb.alloc((NTmax, 2, NC2), F32)
    X = [[sb.alloc((516,), BF16) for _ in range(3 * HB)] for _ in range(1)]
    qc = [sb.alloc((512,), F32) for _ in range(2)]
    sq = [sb.alloc((512,), BF16) for _ in range(2)]
    rn = [sb.alloc((512,), F32) for _ in range(2)]
    qT = [sb.alloc((HB, 512), BF16) for _ in range(2)]
    kT = [sb.alloc((HB, 512), BF16) for _ in range(2)]
    vT = [sb.alloc((HB, 512), BF16) for _ in range(2)]
    qdT = [sb.alloc((HB, 512), BF16) for _ in range(2)]
    ktok = [sb.alloc((HB, 4, 128), BF16) for _ in range(2)]
    vtok = [sb.alloc((HB, 4, 128), BF16) for _ in range(2)]
    EGT = [sb.alloc((512,), F32) for _ in range(2)]
    NB2 = 2
    gUh = [sb.alloc((HB, 128), BF16) for _ in range(NB2)]
    gUl = [sb.alloc((HB, 128), BF16) for _ in range(NB2)]
    Dm = [sb.alloc((HB, 128), BF16) for _ in range(NB2)]
    DmTi = [sb.alloc((HB, 128), BF16) for _ in range(NB2)]
    attnT = [sb.alloc((HB, 128), BF16) for _ in range(NB2)]
    Pm = [[sb.alloc((HB, 128), BF16) for _ in range(2)] for _ in range(NB2)]
    Qm = [[sb.alloc((HB, 128), BF16) for _ in range(2)] for _ in range(NB2)]
    Ym = [[sb.alloc((HB, 128), BF16) for _ in range(2)] for _ in range(NB2)]
    TbT = [sb.alloc((HB, 128), BF16) for _ in range(NB2)]
    keg = [sb.alloc((HB, 128), BF16) for _ in range(NB2)]
    kdec = [sb.alloc((HB, 128), BF16) for _ in range(NB2)]
    negwT = [sb.alloc((HB, 128), BF16) for _ in range(NB2)]
    vnew = [sb.alloc((HB, 128), BF16) for _ in range(NB2)]
    S32 = sb.alloc((HB, 128), F32)
    S16 = sb.alloc((HB, 128), BF16)
    ob = [sb.alloc((HB, 128), BF16) for _ in range(2)]
    o32 = [sb.alloc((HB, 128), F32) for _ in range(2)]
    ofl = [sb.alloc((HB, 128), BF16) for _ in range(2)]
    zt = [sb.alloc((HB, 128), BF16) for _ in range(2)]
    ogt = [sb.alloc((HB, 128), BF16) for _ in range(2)]
    o2 = sb.alloc((HB, 128), F32)
    ssq = [sb.alloc((HB,), F32) for _ in range(2)]
    junkg = sb.alloc((128,), BF16)

    def psb(bk):
        return C.ps[bk].bitcast(BF16)

    def ps4(bk):
        return C.ps[bk][:, 0:HB * 128].rearrange("p (h c) -> p h c", h=HB)

    def ps4b(bk):
        return psb(bk)[:, 0:HB * 128].rearrange("p (h c) -> p h c", h=HB)

    tcount = [0]
    for si, T in enumerate(C.seqs):
        NTall = T // 128
        NBk = T // 512
        g0 = C.offs[si]
        def gates_prep(tile0, NT, g0=g0):
            S.dma(GA[:, 0:NT, :], gates[g0 + tile0 * 128:g0 + (tile0 + NT) * 128, :].rearrange("(n p) c -> p n c", p=128), reads=[], writes=["GA"])
            Ga = GA[:, 0:NT, 0:NC2]
            Gb = GA[:, 0:NT, NC2:2 * NC2]
            Gv = G[:, 0:NT, :]
            S.dve(lambda e, Ga=Ga, Gv=Gv, NT=NT: e.tensor_tensor(out=Gv, in0=Ga, in1=DTB.unsqueeze(1).to_broadcast([128, NT, NC2]),
                                                                 op=ALU.add), reads=["GA", "gDTB"], writes=["G"])
            S.act(lambda e, Gv=Gv: e.activation(out=Gv, in_=Gv, func=AF.Exp), reads=["G"], writes=["G"])
            S.act(lambda e, Gv=Gv: e.activation(out=Gv, in_=Gv, func=AF.Ln, bias=onec[:, 0:1]), reads=["G", "gonec"], writes=["G"])
            S.dve(lambda e, Gv=Gv, NT=NT: e.tensor_tensor(out=Gv, in0=Gv, in1=A16.unsqueeze(1).to_broadcast([128, NT, NC2]),
                                                         op=ALU.mult), reads=["G", "gA16"], writes=["G"])
            S.act(lambda e, Gv=Gv, NT=NT: e.activation(out=GH[:, 0:NT, :], in_=Gv, func=AF.Copy), reads=["G"], writes=["GH"])
            S.dve(lambda e, Gv=Gv, NT=NT: e.tensor_tensor(out=GLo[:, 0:NT, :], in0=Gv, in1=GH[:, 0:NT, :], op=ALU.subtract),
                  reads=["G", "GH"], writes=["GLo"])
            Bv = BETA[:, 0:NT, :]
            S.act(lambda e, Gb=Gb, Bv=Bv: e.activation(out=Bv, in_=Gb, func=AF.Exp, scale=-1.0), reads=["GA"], writes=["BETA"])
            S.dve(lambda e, Bv=Bv: e.tensor_scalar(out=Bv, in0=Bv, scalar1=1.0, scalar2=None, op0=ALU.add),
                  reads=["BETA"], writes=["BETA"])
            S.dve(lambda e, Bv=Bv: e.reciprocal(out=Bv, in_=Bv), reads=["BETA"], writes=["BETA"])
            S.dve(lambda e, Bv=Bv, NT=NT: e.tensor_scalar(out=NBETA[:, 0:NT, :], in0=Bv, scalar1=-1.0, scalar2=None,
                                                          op0=ALU.mult), reads=["BETA"], writes=["NBETA"])
            for t0 in range(0, NT, 32):
                n = min(32, NT - t0)
                bk = C.bank()
                for t in range(t0, t0 + n):
                    c0 = (t - t0) * NC2
                    S.pe(lambda e, bk=bk, t=t, c0=c0: e.matmul(C.ps[bk][:, c0:c0 + NH], lhsT=K_["UF32"], rhs=G[:, t, 0:NH],
                                                               start=True, stop=True), reads=["G"] + ck, writes=[("ps", bk)])
                    S.pe(lambda e, bk=bk, t=t, c0=c0: e.matmul(C.ps[bk][:, c0 + NH:c0 + NC2], lhsT=K_["UB32"], rhs=G[:, t, NH:NC2],
                                                               start=True, stop=True), reads=["G"] + ck, writes=[("ps", bk)])
                S.dve(lambda e, bk=bk, t0=t0, n=n: e.tensor_copy(
                    out=GC[:, t0:t0 + n, :], in_=C.ps[bk][:, 0:n * NC2].rearrange("p (n c) -> p n c", c=NC2)),
                    reads=[("ps", bk)], writes=["GC"])
            S.act(lambda e, NT=NT: e.activation(out=EGC[:, 0:NT, :], in_=GC[:, 0:NT, :], func=AF.Exp), reads=["GC"], writes=["EGC"])
            for t0 in range(0, NT, 16):
                n = min(16, NT - t0)
                bk = C.bank()
                for t in range(t0, t0 + n):
                    for j in range(2):
                        c0 = ((t - t0) * 2 + j) * NC2
                        S.pe(lambda e, bk=bk, t=t, c0=c0, j=j: e.matmul(C.ps[bk][:, c0:c0 + NC2], lhsT=K_["CH%d" % j],
                                                                        rhs=G[:, t, :], start=True, stop=True),
                             reads=["G"] + ck, writes=[("ps", bk)])
                pv = C.ps[bk][:, 0:n * 2 * NC2].rearrange("p (n j c) -> p n j c", j=2, c=NC2)
                S.act(lambda e, pv=pv, t0=t0, n=n: e.activation(out=DEC[:, t0:t0 + n, :, :], in_=pv, func=AF.Exp),
                      reads=[("ps", bk)], writes=["DEC"])
                for j in range(2):
                    S.dve(lambda e, pv=pv, t0=t0, n=n, j=j: e.tensor_tensor(
                        out=EKD[64 * j:64 * j + 64, t0:t0 + n, :], in0=pv[64 * j:64 * j + 64, :, j, :],
                        in1=GC[64 * j:64 * j + 64, t0:t0 + n, :], op=ALU.subtract),
                        reads=[("ps", bk), "GC"], writes=["EKD"])
            S.act(lambda e, NT=NT: e.activation(out=EKD[:, 0:NT, :], in_=EKD[:, 0:NT, :], func=AF.Exp), reads=["EKD"], writes=["EKD"])
            gk = ["G", "GH", "GLo", "BETA", "NBETA", "GC", "EGC", "EKD", "DEC"]

        for hg in range(NH // HB):
            for di in range(2):
                Ud = K_["UF32"] if di == 0 else K_["UB32"]
                MA = K_["MA_f"] if di == 0 else K_["MA_b"]
                S.dve(lambda e: e.memset(S32, 0.0), reads=[], writes=["S32"])
                S.dve(lambda e: e.memset(S16, 0.0), reads=[], writes=["S16"])
                if di == 0:
                    build_DG(hg)
                blocks = list(range(NBk)) if di == 0 else list(range(NBk - 1, -1, -1))
                cur_seg = None
                for bi, blk in enumerate(blocks):
                    bb = bi % 2
                    t0 = blk * 512
                    seg = (blk * 4) // SEGT
                    if seg != cur_seg:
                        cur_seg = seg
                        gates_prep(seg * SEGT, min(SEGT, NTall - seg * SEGT))
                    tl0 = seg * SEGT
                    for hh in range(HB):
                        h = hg * HB + hh
                        for w3 in range(3):
                            ct = w3 * NH + h
                            xb = X[0][w3 * HB + hh]
                            xkey = ("gX", 0, w3, hh)
                            lo, hi = t0 - 2, t0 + 514
                            dlo, dhi = max(lo, 0), min(hi, T)
                            if dlo > lo:
                                S.pool(lambda e, xb=xb: e.memset(xb[:, 0:2], 0.0), reads=[], writes=[xkey])
                            if dhi < hi:
                                S.pool(lambda e, xb=xb: e.memset(xb[:, 514:516], 0.0), reads=[], writes=[xkey])
                            S.dma(xb[:, dlo - lo:dhi - lo], projT[ct * 128:(ct + 1) * 128, g0 + dlo:g0 + dhi],
                                  reads=[], writes=[xkey])
                            bk = C.bank()
                            for i in range(5):
                                S.pe(lambda e, bk=bk, i=i, xb=xb, w3=w3, hh=hh: e.matmul(
                                    C.ps[bk], lhsT=DG[:, (w3 * HB + hh) * 5 + i, :], rhs=xb[:, i:i + 512], start=(i == 0), stop=(i == 4)),
                                    reads=[xkey, ("gDG", w3 * HB + hh)], writes=[("ps", bk)])
                            if w3 == 2:
                                S.act(lambda e, bk=bk, bb=bb, hh=hh: e.activation(out=vT[bb][:, hh, :], in_=C.ps[bk], func=AF.Silu),
                                      reads=[("ps", bk)], writes=[("gvT", bb, hh)])
                                continue
                            q2 = tcount[0] % 2
                            tcount[0] += 1
                            S.act(lambda e, bk=bk, q2=q2: e.activation(out=qc[q2], in_=C.ps[bk], func=AF.Silu),
                                  reads=[("ps", bk)], writes=[("gqc", q2)])
                            S.act(lambda e, q2=q2: e.activation(out=sq[q2], in_=qc[q2], func=AF.Square),
                                  reads=[("gqc", q2)], writes=[("gsq", q2)])
                            bk2 = C.bank()
                            S.pe(lambda e, bk2=bk2, q2=q2: e.matmul(C.ps[bk2], lhsT=K_["ones_bf"], rhs=sq[q2], start=True, stop=True),
                                 reads=[("gsq", q2)] + ck, writes=[("ps", bk2)])
                            S.act(lambda e, bk2=bk2, q2=q2: e.activation(out=rn[q2], in_=C.ps[bk2], func=AF.Sqrt, bias=epsc[:, 0:1]),
                                  reads=[("ps", bk2), "gepsc"], writes=[("grn", q2)])
                            S.dve(lambda e, q2=q2: e.reciprocal(out=rn[q2], in_=rn[q2]), reads=[("grn", q2)], writes=[("grn", q2)])
                            dst = qT[bb][:, hh, :] if w3 == 0 else kT[bb][:, hh, :]
                            dkey = ("gqT", bb, hh) if w3 == 0 else ("gkT", bb, hh)
                            sc = (128.0 ** -0.5) if w3 == 0 else 1.0
                            S.dve(lambda e, q2=q2, dst=dst, sc=sc: e.scalar_tensor_tensor(
                                out=dst, in0=qc[q2], scalar=sc, in1=rn[q2], op0=ALU.mult, op1=ALU.mult),
                                reads=[("gqc", q2), ("grn", q2)], writes=[dkey])
                        for (srcT, dstk, skey, dkey) in ((kT[bb], ktok[bb], ("gkT", bb, hh), ("gktok", bb, hh)),
                                                         (vT[bb], vtok[bb], ("gvT", bb, hh), ("gvtok", bb, hh))):
                            bk = C.bank()
                            for t in range(4):
                                S.pe(lambda e, bk=bk, t=t, srcT=srcT, hh=hh: e.transpose(
                                    psb(bk)[:, t * 128:(t + 1) * 128], srcT[:, hh, t * 128:(t + 1) * 128], ident_bf),
                                    reads=[skey, "ident_bf"], writes=[("ps", bk)])
                            C.evac(dstk[:, hh, :, :], psb(bk)[:, 0:512].rearrange("p (t c) -> p t c", t=4),
                                   reads=[("ps", bk)], writes=[dkey])
                    bk = C.bank()
                    for t in range(4):
                        S.pe(lambda e, bk=bk, t=t, blk=blk, tl0=tl0: e.transpose(
                            C.ps[bk][0:NC2, t * 128:(t + 1) * 128], EGC[:, blk * 4 + t - tl0, :], ident32),
                            reads=["EGC", "ident32"], writes=[("ps", bk)])
                    S.dve(lambda e, bk=bk, bb=bb: e.tensor_copy(out=EGT[bb][0:NC2, :], in_=C.ps[bk][0:NC2, :]),
                          reads=[("ps", bk)], writes=[("gEGT", bb)])
                    for hh in range(HB):
                        hd = di * NH + hg * HB + hh
                        bk = C.bank()
                        S.pe(lambda e, bk=bk, hd=hd, bb=bb: e.matmul(C.ps[bk], lhsT=SELt[0:NC2, hd, :], rhs=EGT[bb][0:NC2, :],
                                                                     start=True, stop=True),
                             reads=[("gEGT", bb), "csel"] + ck, writes=[("ps", bk)])
                        S.dve(lambda e, bk=bk, hh=hh, bb=bb: e.tensor_tensor(out=qdT[bb][:, hh, :], in0=C.ps[bk], in1=qT[bb][:, hh, :],
                                                                             op=ALU.mult),
                              reads=[("ps", bk), ("gqT", bb, hh)], writes=[("gqdT", bb, hh)])
                    tiles = list(range(4)) if di == 0 else [3, 2, 1, 0]
                    for tl in tiles:
                        tt = blk * 4 + tl
                        lt = tt - tl0
                        wb = tt % NB2
                        ts_ = slice(tl * 128, (tl + 1) * 128)
                        hd0 = di * NH + hg * HB
                        Ub = Ud.unsqueeze(1).to_broadcast([128, HB, 128])
                        for (dst, src, key, skey) in ((gUh[wb], GH, ("gUh", wb), "GH"), (gUl[wb], GLo, ("gUl", wb), "GLo")):
                            S.pool(lambda e, dst=dst, src=src, lt=lt, hd0=hd0, Ub=Ub: e.tensor_tensor(
                                out=dst, in0=Ub, in1=src[:, lt, hd0:hd0 + HB].unsqueeze(2).to_broadcast([128, HB, 128]),
                                op=ALU.mult), reads=[skey] + ck, writes=[key])
                        bkE = C.bank()
                        for hh in range(HB):
                            o_ = ps4(bkE)[:, hh, :]
                            S.pe(lambda e, o_=o_, wb=wb, hh=hh: e.matmul(o_, lhsT=gUh[wb][:, hh, :], rhs=K_["ones_bf"], start=True, stop=False),
                                 reads=[("gUh", wb)] + ck, writes=[("ps", bkE)])
                            S.pe(lambda e, o_=o_, wb=wb, hh=hh: e.matmul(o_, lhsT=gUl[wb][:, hh, :], rhs=K_["ones_bf"], start=False, stop=False),
                                 reads=[("gUl", wb)] + ck, writes=[("ps", bkE)])
                            S.pe(lambda e, o_=o_, wb=wb, hh=hh: e.matmul(o_, lhsT=K_["negones_bf"], rhs=gUh[wb][:, hh, :], start=False, stop=False),
                                 reads=[("gUh", wb)] + ck, writes=[("ps", bkE)])
                            S.pe(lambda e, o_=o_, wb=wb, hh=hh: e.matmul(o_, lhsT=K_["negones_bf"], rhs=gUl[wb][:, hh, :], start=False, stop=False),
                                 reads=[("gUl", wb)] + ck, writes=[("ps", bkE)])
                            S.pe(lambda e, o_=o_, MA=MA: e.matmul(o_, lhsT=ident_bf, rhs=MA, start=False, stop=True),
                                 reads=ck, writes=[("ps", bkE)])
                        S.act(lambda e, bkE=bkE, wb=wb: e.activation(out=Dm[wb], in_=ps4(bkE), func=AF.Exp),
                              reads=[("ps", bkE)], writes=[("gDm", wb)])
                        bkT = C.bank()
                        for hh in range(HB):
                            S.pe(lambda e, bkT=bkT, wb=wb, hh=hh: e.transpose(ps4b(bkT)[:, hh, :], Dm[wb][:, hh, :], ident_bf),
                                 reads=[("gDm", wb), "ident_bf"], writes=[("ps", bkT)])
                        S.dve(lambda e, bkT=bkT, wb=wb: e.tensor_tensor(
                            out=DmTi[wb], in0=ps4b(bkT), in1=ident_bf.unsqueeze(1).to_broadcast([128, HB, 128]), op=ALU.add),
                            reads=[("ps", bkT), "ident_bf"], writes=[("gDmTi", wb)])
                        bkK = C.bank()
                        bkQ = C.bank()
                        for hh in range(HB):
                            S.pe(lambda e, bkK=bkK, hh=hh, bb=bb, ts_=ts_: e.matmul(ps4(bkK)[:, hh, :], lhsT=kT[bb][:, hh, ts_], rhs=kT[bb][:, hh, ts_],
                                                                                   start=True, stop=True),
                                 reads=[("gkT", bb, hh)], writes=[("ps", bkK)])
                            S.pe(lambda e, bkQ=bkQ, hh=hh, bb=bb, ts_=ts_: e.matmul(ps4(bkQ)[:, hh, :], lhsT=kT[bb][:, hh, ts_], rhs=qT[bb][:, hh, ts_],
                                                                                   start=True, stop=True),
                                 reads=[("gkT", bb, hh), ("gqT", bb, hh)], writes=[("ps", bkQ)])
                        Q0 = Qm[wb][0]
                        for hh in range(HB):
                            S.dve(lambda e, bkK=bkK, hh=hh, wb=wb, lt=lt, hd0=hd0, Q0=Q0: e.scalar_tensor_tensor(
                                out=Q0[:, hh, :], in0=ps4(bkK)[:, hh, :], scalar=NBETA[:, lt, hd0 + hh:hd0 + hh + 1], in1=Dm[wb][:, hh, :],
                                op0=ALU.mult, op1=ALU.mult), reads=[("ps", bkK), "NBETA", ("gDm", wb)], writes=[("gQ", wb, 0)])
                        S.dve(lambda e, bkQ=bkQ, wb=wb: e.tensor_tensor(out=attnT[wb], in0=ps4(bkQ), in1=DmTi[wb], op=ALU.mult),
                              reads=[("ps", bkQ), ("gDmTi", wb)], writes=[("gattnT", wb)])
                        bkN = C.bank()
                        for hh in range(HB):
                            S.pe(lambda e, bkN=bkN, hh=hh, Q0=Q0: e.transpose(ps4b(bkN)[:, hh, :], Q0[:, hh, :], ident_bf),
                                 reads=[("gQ", wb, 0), "ident_bf"], writes=[("ps", bkN)])
                        P0 = Pm[wb][0]
                        C.evac(P0, ps4b(bkN), reads=[("ps", bkN)], writes=[("gP", wb, 0)])
                        Y0 = Ym[wb][0]
                        S.dve(lambda e, bkN=bkN, Y0=Y0: e.tensor_tensor(
                            out=Y0, in0=ps4b(bkN), in1=ident_bf.unsqueeze(1).to_broadcast([128, HB, 128]), op=ALU.add),
                            reads=[("ps", bkN), "ident_bf"], writes=[("gY", wb, 0)])
                        for k in range(1, 6):
                            pp, pc = (k - 1) % 2, k % 2
                            Pp, Qp, Yp = Pm[wb][pp], Qm[wb][pp], Ym[wb][pp]
                            Pc, Qc, Yc = Pm[wb][pc], Qm[wb][pc], Ym[wb][pc]
                            if k <= 4:
                                bkP = C.bank()
                                for hh in range(HB):
                                    S.pe(lambda e, bkP=bkP, hh=hh, Qp=Qp, Pp=Pp: e.matmul(ps4(bkP)[:, hh, :], lhsT=Qp[:, hh, :], rhs=Pp[:, hh, :],
                                                                                         start=True, stop=True),
                                         reads=[("gQ", wb, pp), ("gP", wb, pp)], writes=[("ps", bkP)])
                            bkQ2 = C.bank()
                            for hh in range(HB):
                                S.pe(lambda e, bkQ2=bkQ2, hh=hh, Qp=Qp, Pp=Pp: e.matmul(ps4(bkQ2)[:, hh, :], lhsT=Pp[:, hh, :], rhs=Qp[:, hh, :],
                                                                                       start=True, stop=True),
                                     reads=[("gQ", wb, pp), ("gP", wb, pp)], writes=[("ps", bkQ2)])
                            if k <= 4:
                                C.evac(Pc, ps4(bkP), reads=[("ps", bkP)], writes=[("gP", wb, pc)])
                            C.evac(Qc, ps4(bkQ2), reads=[("ps", bkQ2)], writes=[("gQ", wb, pc)])
                            bkY = C.bank()
                            for hh in range(HB):
                                S.pe(lambda e, bkY=bkY, hh=hh, Qc=Qc, Yp=Yp: e.matmul(ps4(bkY)[:, hh, :], lhsT=Qc[:, hh, :], rhs=Yp[:, hh, :],
                                                                                     start=True, stop=True),
                                     reads=[("gQ", wb, pc), ("gY", wb, pp)], writes=[("ps", bkY)])
                            S.dve(lambda e, bkY=bkY, Yc=Yc, Yp=Yp: e.tensor_tensor(out=Yc, in0=ps4(bkY), in1=Yp, op=ALU.add),
                                  reads=[("ps", bkY), ("gY", wb, pp)], writes=[("gY", wb, pc)])
                        Y5 = Ym[wb][1]
                        for hh in range(HB):
                            sc_b = BETA[:, lt, hd0 + hh:hd0 + hh + 1]
                            sc_e = EGC[:, lt, hd0 + hh:hd0 + hh + 1]
                            sc_k = EKD[:, lt, hd0 + hh:hd0 + hh + 1]
                            S.act(lambda e, hh=hh, wb=wb, sc_b=sc_b, Y5=Y5: e.activation(out=TbT[wb][:, hh, :], in_=Y5[:, hh, :], func=AF.Copy, scale=sc_b),
                                  reads=[("gY", wb, 1), "BETA"], writes=[("gTbT", wb)])
                            S.pool(lambda e, hh=hh, wb=wb, sc_e=sc_e, bb=bb, tl=tl: e.tensor_scalar(
                                out=keg[wb][:, hh, :], in0=ktok[bb][:, hh, tl, :], scalar1=sc_e, scalar2=None, op0=ALU.mult),
                                reads=[("gktok", bb, hh), "EGC"], writes=[("gkeg", wb)])
                            S.pool(lambda e, hh=hh, wb=wb, sc_k=sc_k, bb=bb, tl=tl: e.tensor_scalar(
                                out=kdec[wb][:, hh, :], in0=ktok[bb][:, hh, tl, :], scalar1=sc_k, scalar2=None, op0=ALU.mult),
                                reads=[("gktok", bb, hh), "EKD"], writes=[("gkdec", wb)])
                        bkW = C.bank()
                        for hh in range(HB):
                            S.pe(lambda e, bkW=bkW, hh=hh, wb=wb: e.matmul(ps4(bkW)[:, hh, :], lhsT=keg[wb][:, hh, :], rhs=TbT[wb][:, hh, :],
                                                                           start=True, stop=True),
                                 reads=[("gkeg", wb), ("gTbT", wb)], writes=[("ps", bkW)])
                        S.act(lambda e, bkW=bkW, wb=wb: e.activation(out=negwT[wb], in_=ps4(bkW), func=AF.Copy, scale=-1.0),
                              reads=[("ps", bkW)], writes=[("gnegwT", wb)])
                        ob_i = tt % 2
                        if di == 1:
                            grow = g0 + tt * 128
                            S.dma(ofl[ob_i], of_s[grow:grow + 128, hg * HB * 128:(hg + 1) * HB * 128].rearrange("p (h c) -> p h c", h=HB),
                                  reads=[("of_d", si, hg, tt)], writes=[("gofl", ob_i)])
                            S.dma(zt[ob_i], ztok[grow:grow + 128, hg * HB * 128:(hg + 1) * HB * 128].rearrange("p (h c) -> p h c", h=HB),
                                  reads=[], writes=[("gzt", ob_i)])
                        for j in ((0, 1) if di == 0 else (1, 0)):
                            pr = slice(64 * j, 64 * j + 64)
                            bkV = C.bank()
                            for hh in range(HB):
                                S.pe(lambda e, bkV=bkV, hh=hh, wb=wb, bb=bb, tl=tl: e.matmul(ps4(bkV)[:, hh, :], lhsT=TbT[wb][:, hh, :], rhs=vtok[bb][:, hh, tl, :],
                                                                                             start=True, stop=False),
                                     reads=[("gTbT", wb), ("gvtok", bb, hh)], writes=[("ps", bkV)])
                                S.pe(lambda e, bkV=bkV, hh=hh, wb=wb: e.matmul(ps4(bkV)[:, hh, :], lhsT=negwT[wb][:, hh, :], rhs=S16[:, hh, :],
                                                                               start=False, stop=True),
                                     reads=[("gnegwT", wb), "S16"], writes=[("ps", bkV)])
                            S.dve(lambda e, bkV=bkV, wb=wb, pr=pr: e.tensor_copy(out=vnew[wb][pr, :, :], in_=ps4(bkV)[pr, :, :]),
                                  reads=[("ps", bkV)], writes=[("gvnew", wb, j)])
                            bkO = C.bank()
                            for hh in range(HB):
                                S.pe(lambda e, bkO=bkO, hh=hh, bb=bb, ts_=ts_: e.matmul(ps4(bkO)[:, hh, :], lhsT=qdT[bb][:, hh, ts_], rhs=S16[:, hh, :],
                                                                                       start=True, stop=False),
                                     reads=[("gqdT", bb, hh), "S16"], writes=[("ps", bkO)])
                                S.pe(lambda e, bkO=bkO, hh=hh, wb=wb, pr=pr: e.matmul(ps4(bkO)[:, hh, :], lhsT=attnT[wb][pr, hh, :], rhs=vnew[wb][pr, hh, :],
                                                                                     start=False, stop=True),
                                     reads=[("gattnT", wb), ("gvnew", wb, j)], writes=[("ps", bkO)])
                            if di == 0:
                                S.act(lambda e, bkO=bkO, ob_i=ob_i, pr=pr: e.activation(out=ob[ob_i][pr, :, :], in_=ps4(bkO)[pr, :, :], func=AF.Copy),
                                      reads=[("ps", bkO)], writes=[("gob", ob_i, j)])
                            else:
                                S.dve(lambda e, bkO=bkO, ob_i=ob_i, pr=pr: e.tensor_tensor(out=o32[ob_i][pr, :, :], in0=ps4(bkO)[pr, :, :],
                                                                                          in1=ofl[ob_i][pr, :, :], op=ALU.add),
                                      reads=[("ps", bkO), ("gofl", ob_i)], writes=[("go32", ob_i, j)])
                            bkS = C.bank()
                            for hh in range(HB):
                                S.pe(lambda e, bkS=bkS, hh=hh, wb=wb, pr=pr: e.matmul(ps4(bkS)[:, hh, :], lhsT=kdec[wb][pr, hh, :], rhs=vnew[wb][pr, hh, :],
                                                                                     start=True, stop=True),
                                     reads=[("gkdec", wb), ("gvnew", wb, j)], writes=[("ps", bkS)])
                            for hh in range(HB):
                                S.dve(lambda e, bkS=bkS, hh=hh, lt=lt, j=j, hd0=hd0: e.scalar_tensor_tensor(
                                    out=S32[:, hh, :], in0=S32[:, hh, :], scalar=DEC[:, lt, j, hd0 + hh:hd0 + hh + 1], in1=ps4(bkS)[:, hh, :],
                                    op0=ALU.mult, op1=ALU.add), reads=[("ps", bkS), "S32", "DEC"], writes=["S32"])
                            S.act(lambda e: e.activation(out=S16, in_=S32, func=AF.Copy), reads=["S32"], writes=["S16"])
                        grow = g0 + tt * 128
                        if di == 0:
                            S.dma(of_s[grow:grow + 128, hg * HB * 128:(hg + 1) * HB * 128].rearrange("p (h c) -> p h c", h=HB), ob[ob_i],
                                  reads=[("gob", ob_i, 0), ("gob", ob_i, 1)], writes=[("of_d", si, hg, tt)])
                        else:
                            o_in = o32[ob_i]
                            okeys = [("go32", ob_i, 0), ("go32", ob_i, 1)]
                            for hh in range(HB):
                                S.act(lambda e, hh=hh, o_in=o_in, ob_i=ob_i: e.activation(out=junkg, in_=o_in[:, hh, :], func=AF.Square,
                                                                                         accum_out=ssq[ob_i][:, hh:hh + 1]),
                                      reads=okeys, writes=["gjunk", ("gssq", ob_i)])
                            S.dve(lambda e, ob_i=ob_i: e.tensor_scalar(out=ssq[ob_i], in0=ssq[ob_i], scalar1=1.0 / 128, scalar2=EPS,
                                                                       op0=ALU.mult, op1=ALU.add), reads=[("gssq", ob_i)], writes=[("gssq", ob_i)])
                            S.act(lambda e, ob_i=ob_i: e.activation(out=ssq[ob_i], in_=ssq[ob_i], func=AF.Sqrt),
                                  reads=[("gssq", ob_i)], writes=[("gssq", ob_i)])
                            S.dve(lambda e, ob_i=ob_i: e.reciprocal(out=ssq[ob_i], in_=ssq[ob_i]), reads=[("gssq", ob_i)], writes=[("gssq", ob_i)])
                            S.dve(lambda e, o_in=o_in, ob_i=ob_i: e.tensor_tensor(
                                out=o2, in0=o_in, in1=ssq[ob_i].unsqueeze(2).to_broadcast([128, HB, 128]), op=ALU.mult),
                                reads=okeys + [("gssq", ob_i)], writes=["go2"])
                            S.pool(lambda e: e.tensor_tensor(out=o2, in0=o2, in1=gnw.unsqueeze(1).to_broadcast([128, HB, 128]), op=ALU.mult),
                                   reads=["go2", "gnw"], writes=["go2"])
                            S.act(lambda e, ob_i=ob_i: e.activation(out=zt[ob_i], in_=zt[ob_i], func=AF.Silu),
                                  reads=[("gzt", ob_i)], writes=[("gzt", ob_i)])
                            S.dve(lambda e, ob_i=ob_i: e.tensor_tensor(out=ogt[ob_i], in0=o2, in1=zt[ob_i], op=ALU.mult),
                                  reads=["go2", ("gzt", ob_i)], writes=[("gogt", ob_i)])
                            S.dma(og[grow:grow + 128, hg * HB * 128:(hg + 1) * HB * 128].rearrange("p (h c) -> p h c", h=HB), ogt[ob_i],
                                  reads=[("gogt", ob_i)], writes=[("og_d", grow // 512, hg, tt)])
    C.barrier()


ATT_DBG = 0


def t5_bucket_np(rel):
    nb = 16
    max_exact = 8
    n = np.abs(rel)
    large = max_exact + (np.log(np.maximum(n, max_exact) / max_exact) / math.log(1024 / max_exact) * (nb - max_exact)).astype(np.int32)
    large = np.minimum(large, nb - 1)
    return (np.where(rel > 0, nb, 0) + np.where(n < max_exact, n, large)).astype(np.int32)


def host_att_consts():
    ohr = np.zeros((3, 33, 384), np.float32)
    for g, (win, d) in enumerate(GROUPS):
        for u in range(383):
            rel = (382 - u) - 191
            if abs(rel) <= 64:
                ohr[g, t5_bucket_np(np.array(rel * d)), u] = 1.0
            else:
                ohr[g, 32, u] = NEG
        ohr[g, 32, 383] = NEG
    return {"cohr": ohr.reshape(99, 384)}


def phase_att(C, qkT, vtok_a, ao_un, relb_src, zr_d, K_, dbg=None):
    S, sb = C.S, C.sb
    sb.reset()
    SEG = 4096
    if all(t % 4096 for t in C.seqs):
        SEG = min(C.seqs)
    assert all(t % SEG == 0 for t in C.seqs)
    ohr_d = C.inp("cohr", [99, 384], F32)
    tab = sb.alloc((12,), F32)
    S.dma(tab[0:32, :], relb_src, reads=[], writes=["atab"])
    TABB = sb.alloc((12, 128), F32)
    S.dve(lambda e: e.memset(TABB[32:33, :, :], 1.0), reads=[], writes=["aTABBo"])
    S.dve(lambda e: e.tensor_copy(out=TABB[0:32, :, :], in_=tab[0:32, :].unsqueeze(2).to_broadcast([32, 12, 128])),
          reads=["atab"], writes=["aTABB"])
    OHR = sb.alloc((3, 384), F32)
    S.dma(OHR[0:33, :, :], ohr_d.rearrange("(g b) u -> b g u", g=3), reads=[], writes=["aOHR"])
    BMT = sb.alloc((12, 2, 128), F32)
    frep = sb.alloc((384,), F32)
    for h in range(12):
        g = h // 4
        bk = C.bank()
        S.pe(lambda e, bk=bk, h=h, g=g: e.matmul(C.ps[bk][:, 0:384], lhsT=TABB[0:33, h, :], rhs=OHR[0:33, g, :], start=True, stop=True),
             reads=["aTABB", "aTABBo", "aOHR"], writes=[("ps", bk)])
        S.dve(lambda e, bk=bk: e.tensor_copy(out=frep, in_=C.ps[bk][:, 0:384]), reads=[("ps", bk)], writes=["afrep"])
        zr = zr_d[h]
        S.dma(zr.rearrange("(p u) -> p u", u=384), frep, reads=["afrep"], writes=[("azr", h)])
        for slot, off in ((0, 127), (1, 255)):
            src = bass.AP(zr.tensor, zr.offset + off, [[383, 128], [1, 128]])
            S.dma(BMT[:, h, slot, :], src, reads=[("azr", h)], writes=[("aBMT", h)])
    if dbg is not None:
        S.dma(dbg.rearrange("p (a b c) -> p a b c", a=12, b=2), BMT, reads=[("aBMT", h) for h in range(12)], writes=["dbg"])
        if dbg.shape[-1] == 3072:
            C.barrier()
            return
    PADM = 64 * 16
    qb = [sb.alloc((SEG,), BF16) for _ in range(2)]
    kb = [sb.alloc((PADM + SEG + PADM,), BF16) for _ in range(2)]
    qd = [sb.alloc((SEG,), BF16) for _ in range(2)]
    kd = [sb.alloc((PADM + SEG + PADM,), BF16) for _ in range(2)]
    NV = 3
    v4 = [sb.alloc((4, 65), BF16) for _ in range(NV)]
    for i in range(NV):
        S.dve(lambda e, i=i: e.memset(v4[i], 1.0), reads=[], writes=[("av4", i)])
    lg = [sb.alloc((256,), F32) for _ in range(3)]
    PT = [[sb.alloc((256,), BF16) for _ in range(4)] for _ in range(2)]
    ost = [sb.alloc((264,), BF16) for _ in range(2)]
    lrr = [0]
    vrr = [0]
    orr = [0]
    for si, T in enumerate(C.seqs):
        g0 = C.offs[si]
        for so in range(0, T, SEG):
            for g, (win, d) in enumerate(GROUPS):
                if (ATT_DBG & 4) and g > 0:
                    continue
                if (ATT_DBG & 32) and g != 2:
                    continue
                if (ATT_DBG & 8) and g != 1:
                    continue
                pad = 64 * d
                for pt in range(2):
                    rq = g * 256 + pt * 128
                    S.dma(qb[pt], qkT[rq:rq + 128, g0 + so:g0 + so + SEG], reads=[], writes=[("aqb", pt)])
                    lo, hi = so - pad, so + SEG + pad
                    dlo, dhi = max(lo, 0), min(hi, T)
                    if dlo > lo:
                        S.pool(lambda e, pt=pt, n=dlo - lo: e.memset(kb[pt][:, 0:n], 0.0), reads=[], writes=[("akb", pt)])
                    if dhi < hi:
                        S.pool(lambda e, pt=pt, a=dhi - lo, b=hi - lo: e.memset(kb[pt][:, a:b], 0.0), reads=[], writes=[("akb", pt)])
                    S.dma(kb[pt][:, dlo - lo:dhi - lo], qkT[768 + rq:768 + rq + 128, g0 + dlo:g0 + dhi], reads=[], writes=[("akb", pt)])
                Ls = SEG // d
                nq = Ls // 128
                assert nq >= 2
                if True:
                    for pt in range(2):
                        S.dve(lambda e, pt=pt, d=d: e.tensor_copy(
                            out=qd[pt][:, 0:SEG].rearrange("p (r s) -> p r s", r=d),
                            in_=qb[pt][:, 0:SEG].rearrange("p (s r) -> p r s", r=d)), reads=[("aqb", pt)], writes=[("aqd", pt)])
                        nk = SEG + 2 * pad
                        S.pool(lambda e, pt=pt, d=d, nk=nk: e.tensor_copy(
                            out=kd[pt][:, 0:nk].rearrange("p (r s) -> p r s", r=d),
                            in_=kb[pt][:, 0:nk].rearrange("p (s r) -> p r s", r=d)), reads=[("akb", pt)], writes=[("akd", pt)])
                L = T // d
                for r in range(d):
                    prev = None
                    for m in range(nq + 1):
                        if (ATT_DBG >> 8) and m >= (ATT_DBG >> 8):
                            break
                        gen = m % 2
                        sg0 = (so // d) + 128 * m - 64
                        jlo = 0 if sg0 >= 0 else 64
                        jhi = 128 if sg0 + 128 <= L else 64
                        vi = vrr[0] % NV
                        vrr[0] += 1
                        trow = g0 + (sg0 + jlo) * d + r
                        nrow = jhi - jlo
                        vsrc = bass.AP(vtok_a.tensor, vtok_a.offset + trow * 768 + g * 256, [[768 * d, nrow], [64, 4], [1, 64]])
                        if not (ATT_DBG & 1):
                            S.dma(v4[vi][jlo:jhi, :, 0:64], vsrc, reads=[], writes=[("av4", vi)])
                        qt_lo = max(m - 1, 0)
                        qt_hi = min(m, nq - 1)
                        nqc = (qt_hi - qt_lo + 1) * 128
                        bslot0 = 0 if m - 1 >= 0 else 1
                        kc0 = (128 * m - 64) * d + r + pad
                        qc0 = (128 * qt_lo) * d + r
                        for hh in range(4):
                            pt, prt = hh // 2, 64 * (hh % 2)
                            h = g * 4 + hh
                            bk = C.bank()
                            if False:
                                kop = kb[pt][prt:prt + 64, kc0:kc0 + 128]
                                qop = qb[pt][prt:prt + 64, qc0:qc0 + nqc]
                                rk = [("aqb", pt), ("akb", pt)]
                            else:
                                kop = kd[pt][prt:prt + 64, r * (Ls + 128) + 128 * m:r * (Ls + 128) + 128 * m + 128]
                                qop = qd[pt][prt:prt + 64, r * Ls + 128 * qt_lo:r * Ls + 128 * qt_lo + nqc]
                                rk = [("aqd", pt), ("akd", pt)]
                            S.pe(lambda e, bk=bk, kop=kop, qop=qop, nqc=nqc: e.matmul(
                                C.ps[bk][:, 0:nqc], lhsT=kop, rhs=qop, start=True, stop=True),
                                reads=rk, writes=[("ps", bk)])
                            li = lrr[0] % 3
                            lrr[0] += 1
                            bsl = BMT[:, h, bslot0:bslot0 + nqc // 128, :]
                            S.dve(lambda e, bk=bk, li=li, nqc=nqc, bsl=bsl: e.scalar_tensor_tensor(
                                out=lg[li][:, 0:nqc], in0=C.ps[bk][:, 0:nqc], scalar=0.125, in1=bsl.rearrange("p a b -> p (a b)"),
                                op0=ALU.mult, op1=ALU.add), reads=[("ps", bk), ("aBMT", h)], writes=[("alg", li)])
                            S.act(lambda e, li=li, nqc=nqc, gen=gen, hh=hh: e.activation(out=PT[gen][hh][:, 0:nqc], in_=lg[li][:, 0:nqc], func=AF.Exp),
                                  reads=[("alg", li)], writes=[("aPT", gen, hh)])
                        if m >= 1:
                            qi = m - 1
                            pjlo, pjhi, pvi = prev
                            oi = orr[0] % 2
                            orr[0] += 1
                            bkO = C.bank()
                            pc0 = 128 if qi >= 1 else 0
                            for hh in range(4):
                                oap = C.ps[bkO][:, hh * 128:hh * 128 + 65]
                                S.pe(lambda e, oap=oap, gen=gen, hh=hh, pc0=pc0, pjlo=pjlo, pjhi=pjhi, pvi=pvi: e.matmul(
                                    oap, lhsT=PT[1 - gen][hh][pjlo:pjhi, pc0:pc0 + 128], rhs=v4[pvi][pjlo:pjhi, hh, :], start=True, stop=False),
                                    reads=[("aPT", 1 - gen, hh), ("av4", pvi)], writes=[("ps", bkO)])
                                S.pe(lambda e, oap=oap, gen=gen, hh=hh, jlo=jlo, jhi=jhi, vi=vi: e.matmul(
                                    oap, lhsT=PT[gen][hh][jlo:jhi, 0:128], rhs=v4[vi][jlo:jhi, hh, :], start=False, stop=True),
                                    reads=[("aPT", gen, hh), ("av4", vi)], writes=[("ps", bkO)])
                            o4 = C.ps[bkO].rearrange("p (h c) -> p h c", h=4)
                            S.act(lambda e, o4=o4, oi=oi: e.activation(out=ost[oi][:, 0:256].rearrange("p (h c) -> p h c", h=4),
                                                                       in_=o4[:, :, 0:64], func=AF.Copy),
                                  reads=[("ps", bkO)], writes=[("aost", oi)])
                            S.dve(lambda e, o4=o4, oi=oi: e.tensor_copy(out=ost[oi][:, 256:260], in_=o4[:, :, 64]),
                                  reads=[("ps", bkO)], writes=[("adst", oi)])
                            S.dve(lambda e, o4=o4, oi=oi: e.tensor_tensor(out=ost[oi][:, 260:264], in0=o4[:, :, 64], in1=ost[oi][:, 256:260],
                                                                          op=ALU.subtract),
                                  reads=[("ps", bkO), ("adst", oi)], writes=[("adst2", oi)])
                            trow = g0 + so + (128 * qi) * d + r
                            odst = bass.AP(ao_un.tensor, ao_un.offset + trow * 792 + g * 264, [[792 * d, 128], [1, 264]])
                            if not (ATT_DBG & 2):
                                S.dma(odst, ost[oi], reads=[("aost", oi), ("adst", oi), ("adst2", oi)], writes=[("ao_d", trow, g)])
                        prev = (jlo, jhi, vi)
    C.barrier()


def phase_attnorm(C, ao_un, ao_n):
    S, sb = C.S, C.sb
    sb.reset()
    at = [sb.alloc((4, 792), BF16) for _ in range(2)]
    ao = [sb.alloc((4, 768), BF16) for _ in range(2)]
    tot = [sb.alloc((4, 4), F32) for _ in range(2)]
    nt = C.T // 512
    for i in range(nt):
        b = i % 2
        t0 = i * 512
        S.dma(at[b], ao_un[t0:t0 + 512, :].rearrange("(s p) c -> p s c", p=128), reads=[], writes=[("nat", b)])
        dvs = [at[b][:, :, g * 264 + 256 + 4 * k:g * 264 + 260 + 4 * k] for g in range(3) for k in range(2)]
        S.dve(lambda e, b=b, dvs=dvs: e.tensor_tensor(out=tot[b], in0=dvs[0], in1=dvs[1], op=ALU.add),
              reads=[("nat", b)], writes=[("ntot", b)])
        for kk in range(2, 6):
            S.dve(lambda e, b=b, dvs=dvs, kk=kk: e.tensor_tensor(out=tot[b], in0=tot[b], in1=dvs[kk], op=ALU.add),
                  reads=[("nat", b), ("ntot", b)], writes=[("ntot", b)])
        S.dve(lambda e, b=b: e.reciprocal(out=tot[b], in_=tot[b]), reads=[("ntot", b)], writes=[("ntot", b)])
        for s_ in range(4):
            for g in range(3):
                eng = S.dve if (s_ * 3 + g) % 2 == 0 else S.pool
                eng(lambda e, b=b, s_=s_, g=g: e.tensor_tensor(
                    out=ao[b][:, s_, g * 256:(g + 1) * 256].rearrange("p (j c) -> p j c", j=4),
                    in0=at[b][:, s_, g * 264:g * 264 + 256].rearrange("p (j c) -> p j c", j=4),
                    in1=tot[b][:, s_, :].unsqueeze(2).to_broadcast([128, 4, 64]), op=ALU.mult),
                    reads=[("nat", b), ("ntot", b)], writes=[("nao", b, s_, g)])
        S.dma(ao_n[t0:t0 + 512, :].rearrange("(s p) c -> p s c", p=128), ao[b],
              reads=[("nao", b, s_, g) for s_ in range(4) for g in range(3)], writes=[("aon_d", i)])
    C.barrier()


def host_consts():
    idx = np.arange(128)
    same = (idx[:, None] // 64) == (idx[None, :] // 64)
    UF = (same & (idx[:, None] <= idx[None, :])).astype(np.float32)
    UB = (same & (idx[:, None] >= idx[None, :])).astype(np.float32)
    CH0 = np.repeat((idx < 64).astype(np.float32)[:, None], 128, 1)
    CH1 = np.repeat((idx >= 64).astype(np.float32)[:, None], 128, 1)
    MA_f = np.where(same & (idx[None, :] < idx[:, None]), 0.0, NEG).astype(np.float32)
    MA_b = np.where(same & (idx[None, :] > idx[:, None]), 0.0, NEG).astype(np.float32)
    SEL = np.zeros((16, 16, 128), np.float32)
    for h in range(16):
        SEL[h, h, :] = 1.0
    c32 = np.concatenate([np.eye(128, dtype=np.float32), UF, UB, CH0, CH1], axis=1)
    cbf = np.concatenate([np.eye(128, dtype=np.float32), np.ones((128, 128), np.float32),
                          -np.ones((128, 128), np.float32), MA_f, MA_b], axis=1).astype(ml_dtypes.bfloat16)
    return {"c32": c32, "cbf": cbf, "csel": SEL.reshape(16, 2048)}


def load_consts(C):
    S, sb = C.S, C.sb
    c32_d = C.inp("c32", [128, 640], F32)
    cbf_d = C.inp("cbf", [128, 640], BF16)
    c32 = sb.alloc((5, 128), F32)
    cbf = sb.alloc((5, 128), BF16)
    S.dma(c32, c32_d.rearrange("p (a b) -> p a b", a=5), reads=[], writes=["c32", "ident32"])
    S.dma(cbf, cbf_d.rearrange("p (a b) -> p a b", a=5), reads=[], writes=["cbf", "ident_bf"])
    K_ = {"ident32": c32[:, 0, :], "UF32": c32[:, 1, :], "UB32": c32[:, 2, :], "CH0": c32[:, 3, :], "CH1": c32[:, 4, :],
          "ident_bf": cbf[:, 0, :], "ones_bf": cbf[:, 1, :], "negones_bf": cbf[:, 2, :], "MA_f": cbf[:, 3, :],
          "MA_b": cbf[:, 4, :], "keys": ["c32", "cbf"]}
    sb.set_mark()
    return K_


W_NAMES = ["norm_mix", "norm_ffn", "norm_final", "gdn_w_in", "gdn_conv", "gdn_a_log", "gdn_dt_bias", "gdn_norm",
           "gdn_w_out", "att_w_in", "att_w_out", "rel_bias", "ffn_w_up", "ffn_conv", "ffn_conv_b", "ffn_w_down"]

WIN = 6144


def declare_weights(C):
    W = {}
    W["norm_mix"] = C.inp("norm_mix", [2, D]); W["norm_ffn"] = C.inp("norm_ffn", [2, D]); W["norm_final"] = C.inp("norm_final", [1, D])
    W["gdn_w_in"] = C.inp("gdn_w_in", [D, GDN_IN]); W["gdn_conv"] = C.inp("gdn_conv", [5, 3072])
    W["gdn_a_log"] = C.inp("gdn_a_log", [1, 16]); W["gdn_dt_bias"] = C.inp("gdn_dt_bias", [1, 16]); W["gdn_norm"] = C.inp("gdn_norm", [1, 128])
    W["gdn_w_out"] = C.inp("gdn_w_out", [D, D]); W["att_w_in"] = C.inp("att_w_in", [D, 2304]); W["att_w_out"] = C.inp("att_w_out", [768, D])
    W["rel_bias"] = C.inp("rel_bias", [32, 12]); W["ffn_w_up"] = C.inp("ffn_w_up", [2, D, 2 * D_FF]); W["ffn_conv"] = C.inp("ffn_conv", [2, 3, 2 * D_FF])
    W["ffn_conv_b"] = C.inp("ffn_conv_b", [2, 2 * D_FF]); W["ffn_w_down"] = C.inp("ffn_w_down", [2, D_FF, D])
    return W


def chain_gdn_layer(C, W, K_, x_in, og, pfx):
    T = C.T
    xn0 = C.scratch(pfx + "xn0", [T, D], BF16)
    projT = C.scratch(pfx + "projT", [3072, T], BF16)
    ztok = C.scratch(pfx + "ztok", [T, 1024], BF16)
    gates = C.scratch(pfx + "gates", [T, 32], F32)
    of_s = C.scratch(pfx + "of_s", [T, 1024], BF16)
    phase_norm0(C, x_in, W["norm_mix"][0:1, :], xn0)
    phase_proj(C, xn0, W["gdn_w_in"], GDN_IN, fm=[(0, 3072, projT)],
               tm=[(3072, 512, ztok[:, 0:512], BF16), (3584, 512, ztok[:, 512:1024], BF16), (4096, 32, gates, F32)], tag=pfx + "gp")
    phase_gdn(C, projT, ztok, gates, of_s, og, K_, W["gdn_conv"], W["gdn_a_log"], W["gdn_dt_bias"], W["gdn_norm"])


def chain_rest(C, W, K_, x_in, og, y_out, pfx):
    S = C.S
    T, TP = C.T, C.TP
    xrp = C.scratch(pfx + "xrp", [TP, D], F32)
    xnp = C.scratch(pfx + "xnp", [TP, D], BF16)
    x1 = C.scratch(pfx + "x1", [T, D], F32)
    xn1 = C.scratch(pfx + "xn1", [T, D], BF16)
    qkT = C.scratch(pfx + "qkT", [1536, T], BF16)
    vtok = C.scratch(pfx + "vtok", [T, 768], BF16)
    ao_un = C.scratch(pfx + "ao_un", [T, 792], BF16)
    ao_n = C.scratch(pfx + "ao_n", [T, 768], BF16)
    zr = C.scratch(pfx + "zr", [12, 128 * 384], F32)
    phase_outproj(C, og, D, W["gdn_w_out"], x_in, W["norm_ffn"][0:1, :], xrp, xnp, pfx + "op0", "og_d")

    def outs0(si, t_lo, n, xt_ap, xo_ap, kx, ko):
        g = C.offs[si] + t_lo
        S.dma(x1[g:g + n, :], xt_ap, reads=kx, writes=[("x1d", g)])
        S.dma(xn1[g:g + n, :], xo_ap, reads=ko, writes=[("xn1d", g)])
    phase_ffn(C, xnp, xrp, W["ffn_w_up"][0], W["ffn_w_down"][0], W["ffn_conv"][0], W["ffn_conv_b"][0], W["norm_mix"][1:2, :],
              K_["ident32"], outs0, pfx + "f0", final=False)
    phase_proj(C, xn1, W["att_w_in"], 2304, fm=[(0, 1536, qkT)],
               tm=[(1536, 512, vtok[:, 0:512], BF16), (2048, 256, vtok[:, 512:768], BF16)], tag=pfx + "ap")
    phase_att(C, qkT, vtok, ao_un, W["rel_bias"], zr, K_)
    phase_attnorm(C, ao_un, ao_n)
    phase_outproj(C, ao_n, 768, W["att_w_out"], x1, W["norm_ffn"][1:2, :], xrp, xnp, pfx + "op1", "aon_d")

    def outs1(si, t_lo, n, xt_ap, xo_ap, kx, ko):
        g = C.offs[si] + t_lo
        S.dma(y_out[g:g + n, :], xo_ap, reads=ko, writes=[("yd", g)])
    phase_ffn(C, xnp, xrp, W["ffn_w_up"][1], W["ffn_w_down"][1], W["ffn_conv"][1], W["ffn_conv_b"][1], W["norm_final"],
              K_["ident32"], outs1, pfx + "f1", final=True)


def build_program(seqs, debug=False):
    import contextlib
    nc = bass.Bass("TRN2", target_bir_lowering=False)
    C = Ctx(nc, seqs, debug=debug)
    x_all = C.inp("x_all", [C.T, D])
    W = declare_weights(C)
    y_all = C.scratch("y_all", [C.T, D], F32, out=True)
    og = C.scratch("og", [C.T, 1024], BF16)
    K_ = load_consts(C)
    chain_gdn_layer(C, W, K_, x_all, og, "")
    chain_rest(C, W, K_, x_all, og, y_all, "")
    stack = contextlib.ExitStack()
    C.S.emit(stack)
    return nc, C, stack


def build_program_A(sample_seqs, Tp, debug=False):
    import contextlib
    nc = bass.Bass("TRN2", target_bir_lowering=False)
    C = Ctx(nc, sample_seqs, debug=debug)
    x_all = C.inp("x_all", [C.T, D])
    W = declare_weights(C)
    y_all = C.scratch("y_all", [C.T, D], F32, out=True)
    og = C.scratch("og", [C.T, 1024], BF16)
    K_ = load_consts(C)
    Cp = C.with_seqs([Tp])
    x_p = C.inp("x_p", [Tp, D])
    w_h = C.inp("gdn_w_in_h", [D, 516]); conv_h = C.inp("gdn_conv_h", [5, 384])
    alog_h = C.inp("gdn_a_log_h", [1, 2]); dtb_h = C.inp("gdn_dt_bias_h", [1, 2])
    og_p = C.scratch("og_p", [Tp, 128], BF16, out=True)
    xn0p = C.scratch("p_xn0", [Tp, D], BF16)
    projTp = C.scratch("p_projT", [384, Tp], BF16)
    ztokp = C.scratch("p_ztok", [Tp, 128], BF16)
    gatesp = C.scratch("p_gates", [Tp, 4], F32)
    ofp = C.scratch("p_of", [Tp, 128], BF16)
    phase_norm0(Cp, x_p, W["norm_mix"][0:1, :], xn0p)
    phase_proj(Cp, xn0p, w_h, 516, fm=[(0, 384, projTp)], tm=[(384, 128, ztokp, BF16), (512, 4, gatesp, F32)], tag="pgp")
    phase_gdn(Cp, projTp, ztokp, gatesp, ofp, og_p, K_, conv_h, alog_h, dtb_h, W["gdn_norm"], NH=1, HB=1)
    chain_gdn_layer(C, W, K_, x_all, og, "")
    chain_rest(C, W, K_, x_all, og, y_all, "")
    stack = contextlib.ExitStack()
    C.S.emit(stack)
    return nc, C, stack


def build_program_B(win, debug=False):
    import contextlib
    nc = bass.Bass("TRN2", target_bir_lowering=False)
    C = Ctx(nc, [win], debug=debug)
    x_w = C.inp("x_w", [win, D])
    og_w = C.inp("og_w", [win, 1024], BF16)
    W = declare_weights(C)
    y_w = C.scratch("y_w", [win, D], F32, out=True)
    K_ = load_consts(C)
    chain_rest(C, W, K_, x_w, og_w, y_w, "")
    stack = contextlib.ExitStack()
    C.S.emit(stack)
    return nc, C, stack


def weight_map(w):
    m = {}
    m["norm_mix"] = np.asarray(w["norm_mix"], np.float32)
    m["norm_ffn"] = np.asarray(w["norm_ffn"], np.float32)
    m["norm_final"] = np.asarray(w["norm_final"], np.float32).reshape(1, D)
    m["gdn_w_in"] = np.asarray(w["gdn_w_in"], np.float32)[0]
    m["gdn_conv"] = np.asarray(w["gdn_conv"], np.float32)[0]
    m["gdn_a_log"] = np.asarray(w["gdn_a_log"], np.float32)[0].reshape(1, 16)
    m["gdn_dt_bias"] = np.asarray(w["gdn_dt_bias"], np.float32)[0].reshape(1, 16)
    m["gdn_norm"] = np.asarray(w["gdn_norm"], np.float32)[0].reshape(1, 128)
    m["gdn_w_out"] = np.asarray(w["gdn_w_out"], np.float32)[0]
    m["att_w_in"] = np.asarray(w["att_w_in"], np.float32)[0]
    m["att_w_out"] = np.asarray(w["att_w_out"], np.float32)[0]
    m["rel_bias"] = np.asarray(w["rel_bias"], np.float32)
    m["ffn_w_up"] = np.asarray(w["ffn_w_up"], np.float32)
    m["ffn_conv"] = np.asarray(w["ffn_conv"], np.float32)
    m["ffn_conv_b"] = np.asarray(w["ffn_conv_b"], np.float32)
    m["ffn_w_down"] = np.asarray(w["ffn_w_down"], np.float32)
    m.update(host_consts())
    m.update(host_att_consts())
    return m


def make_in_map(x_rows, w):
    m = weight_map(w)
    m["x_all"] = np.ascontiguousarray(x_rows, dtype=np.float32)
    return m


def head_slices(wm, h):
    w_in = wm["gdn_w_in"]
    cols = ([h * 128 + i for i in range(128)] + [1024 + h * 128 + i for i in range(128)] + [2048 + h * 128 + i for i in range(128)]
            + [3072 + h * 128 + i for i in range(128)] + [4096 + k * 8 + h for k in range(4)])
    ccols = [h * 128 + i for i in range(128)] + [1024 + h * 128 + i for i in range(128)] + [2048 + h * 128 + i for i in range(128)]
    return {"gdn_w_in_h": np.ascontiguousarray(w_in[:, cols]),
            "gdn_conv_h": np.ascontiguousarray(wm["gdn_conv"][:, ccols]),
            "gdn_a_log_h": np.ascontiguousarray(wm["gdn_a_log"][:, [h, 8 + h]]),
            "gdn_dt_bias_h": np.ascontiguousarray(wm["gdn_dt_bias"][:, [h, 8 + h]])}


def kernel(**inputs):
    x_prompt = np.asarray(inputs["x_prompt"], np.float32)
    x_sample = np.asarray(inputs["x_sample"], np.float32)
    ncore = 8
    ns = x_sample.shape[0] // ncore
    Ts = x_sample.shape[1]
    Tp = x_prompt.shape[1]
    wm = weight_map(inputs)
    nc, C, stack = build_program_A([Ts] * ns, Tp)
    with stack:
        in_maps = []
        for c in range(ncore):
            m = dict(wm)
            m["x_all"] = np.ascontiguousarray(x_sample[c * ns:(c + 1) * ns].reshape(ns * Ts, D))
            m["x_p"] = x_prompt[0]
            m.update(head_slices(wm, c))
            in_maps.append({k: m[k] for k in C.inputs})
        resA = run_bass_kernel_spmd(nc, in_maps, core_ids=list(range(ncore)))
    y_sample = np.empty_like(x_sample)
    og_full = np.empty((Tp, 1024), dtype=ml_dtypes.bfloat16)
    for c in range(ncore):
        y_sample[c * ns:(c + 1) * ns] = resA.results[c]["y_all"].reshape(ns, Ts, D)
        og_full[:, c * 128:(c + 1) * 128] = resA.results[c]["og_p"]
    share = Tp // ncore
    nc2, C2, stack2 = build_program_B(WIN)
    starts = [min(max(c * share - (WIN - share) // 2, 0), Tp - WIN) for c in range(ncore)]
    with stack2:
        in_maps = []
        for c in range(ncore):
            m = dict(wm)
            m["x_w"] = np.ascontiguousarray(x_prompt[0, starts[c]:starts[c] + WIN])
            m["og_w"] = np.ascontiguousarray(og_full[starts[c]:starts[c] + WIN])
            in_maps.append({k: m[k] for k in C2.inputs})
        resB = run_bass_kernel_spmd(nc2, in_maps, core_ids=list(range(ncore)))
    y_prompt = np.empty_like(x_prompt)
    for c in range(ncore):
        o = c * share - starts[c]
        y_prompt[0, c * share:(c + 1) * share] = resB.results[c]["y_w"][o:o + share]
    return (y_prompt, y_sample)
```

```python
import math
import numpy as np
import ml_dtypes
import concourse.bass as bass
import concourse.mybir as mybir
from concourse.bass_utils import run_bass_kernel_spmd

F32 = mybir.dt.float32
BF16 = mybir.dt.bfloat16
AF = mybir.ActivationFunctionType
ALU = mybir.AluOpType
AX = mybir.AxisListType

D = 1024
EPS = 1e-6
GDN_H = 8
GDN_IN = 4128
D_FF = 2816
ATT_H = 12
ATT_DH = 64
ATT_W = 768
GROUPS = ((128, 1), (512, 4), (2048, 16))
NEG = -30000.0


class Sched:
    ENGS = ("pe", "dve", "act", "pool", "sp")
    EPOCH = 20000
    NDMA = {"sp": 24, "pool": 12, "act": 6}

    def __init__(self, nc):
        self.nc = nc
        self.ops = []
        self.res_w = {}
        self.res_r = {}

    def add(self, eng, fn, reads=(), writes=(), dma=False):
        deps = set()
        for k in reads:
            w = self.res_w.get(k)
            if w is not None:
                deps.add(w)
            if isinstance(k, tuple) and k[0] == "ps":
                for r in self.res_r.get(k, ()):
                    if self.ops[r][0] != eng:
                        deps.add(r)
        for k in writes:
            w = self.res_w.get(k)
            if w is not None:
                deps.add(w)
            for r in self.res_r.get(k, ()):
                deps.add(r)
        oid = len(self.ops)
        deps.discard(oid)
        self.ops.append([eng, fn, deps, dma])
        for k in reads:
            lst = self.res_r.setdefault(k, [])
            if not dma:
                lst[:] = [r for r in lst if self.ops[r][3] or self.ops[r][0] != eng]
            lst.append(oid)
        for k in writes:
            self.res_w[k] = oid
            self.res_r[k] = []
        return oid

    def pe(self, fn, reads=(), writes=()):
        return self.add("pe", fn, reads, writes)

    def dve(self, fn, reads=(), writes=()):
        return self.add("dve", fn, reads, writes)

    def act(self, fn, reads=(), writes=()):
        return self.add("act", fn, reads, writes)

    def pool(self, fn, reads=(), writes=()):
        return self.add("pool", fn, reads, writes)

    def dma(self, out, in_, reads=(), writes=(), eng="sp", transpose=False):
        if transpose:
            fn = lambda e: e.dma_start_transpose(out=out, in_=in_)
        else:
            fn = lambda e: e.dma_start(out=out, in_=in_)
        return self.add(eng, fn, reads, writes, dma=True)

    def emit(self, stack):
        nc = self.nc
        ops = self.ops
        n = len(ops)
        flagged = [False] * n
        for i, (eng, fn, deps, dma) in enumerate(ops):
            for d in deps:
                de, _, _, ddma = ops[d]
                if ddma:
                    continue
                if de == "pe" and eng == "pe" and not dma:
                    continue
                flagged[d] = True
        cnt = {e: 0 for e in self.ENGS}
        fidx = [None] * n
        for i, (eng, fn, deps, dma) in enumerate(ops):
            if flagged[i] and not dma:
                fidx[i] = cnt[eng]
                cnt[eng] += 1
        self.flag_counts = dict(cnt)
        csems = {}
        for e in self.ENGS:
            ne = cnt[e] // self.EPOCH + 1
            csems[e] = [stack.enter_context(nc.semaphore(f"c_{e}_{j}")) for j in range(ne)]
        dcnt = {e: 0 for e in self.ENGS}
        dinfo = [None] * n
        for i, (eng, fn, deps, dma) in enumerate(ops):
            if dma:
                dinfo[i] = (eng, dcnt[eng])
                dcnt[eng] += 1
        dsems = {}
        for e in self.ENGS:
            if dcnt[e]:
                nd = self.NDMA.get(e, 8)
                dsems[e] = [stack.enter_context(nc.semaphore(f"d_{e}_{j}")) for j in range(nd)]

        def dma_target(i):
            e, j = dinfo[i]
            nd = len(dsems[e])
            return dsems[e][j % nd], 16 * (j // nd + 1), (e, j % nd)

        per_eng = {e: [] for e in self.ENGS}
        for i, op in enumerate(ops):
            per_eng[op[0]].append(i)

        block = stack.enter_context(nc.Block())
        EPOCH = self.EPOCH

        def run_engine(ename, engobj):
            waited_c = {}
            waited_d = {}
            for i in per_eng[ename]:
                eng, fn, deps, dma = ops[i]
                need_c = {}
                need_d = {}
                for d in deps:
                    de, _, _, ddma = ops[d]
                    if ddma:
                        sem, val, key = dma_target(d)
                        if waited_d.get(key, 0) < val and need_d.get(key, (None, 0))[1] < val:
                            need_d[key] = (sem, val)
                    else:
                        if de == "pe" and eng == "pe" and not dma:
                            continue
                        fi = fidx[d]
                        if waited_c.get(de, -1) < fi and need_c.get(de, -1) < fi:
                            need_c[de] = fi
                if dma:
                    e, j = dinfo[i]
                    nd = len(dsems[e])
                    if j >= nd:
                        key = (e, j % nd)
                        val = 16 * (j // nd)
                        if waited_d.get(key, 0) < val and need_d.get(key, (None, 0))[1] < val:
                            need_d[key] = (dsems[e][j % nd], val)
                for de, fi in need_c.items():
                    engobj.wait_ge(csems[de][fi // EPOCH], fi % EPOCH + 1)
                    waited_c[de] = fi
                for key, (sem, val) in need_d.items():
                    engobj.wait_ge(sem, val)
                    waited_d[key] = val
                ins = fn(engobj)
                if dma:
                    sem, val, key = dma_target(i)
                    ins.then_inc(sem, 16)
                elif flagged[i]:
                    fi = fidx[i]
                    ins.then_inc(csems[eng][fi // EPOCH], 1)
            if ename == "sp":
                for e in self.ENGS:
                    if dcnt[e]:
                        nd = len(dsems[e])
                        for slot in range(min(nd, dcnt[e])):
                            last_j = ((dcnt[e] - 1 - slot) // nd) * nd + slot
                            val = 16 * (last_j // nd + 1)
                            if waited_d.get((e, slot), 0) < val:
                                engobj.wait_ge(dsems[e][slot], val)

        @block.tensor
        def _(e):
            run_engine("pe", e)

        @block.vector
        def _(e):
            run_engine("dve", e)

        @block.scalar
        def _(e):
            run_engine("act", e)

        @block.gpsimd
        def _(e):
            run_engine("pool", e)

        @block.sync
        def _(e):
            run_engine("sp", e)


class Rec(Sched):
    def __init__(self):
        self.calls = []

    def add(self, eng, fn, reads=(), writes=(), dma=False):
        self.calls.append((eng, fn, list(reads), list(writes), dma))


def replay_merged(S, a, b):
    ca = a.calls if a is not None else []
    cb = b.calls if b is not None else []
    na, nb = len(ca), len(cb)
    i = j = 0
    while i < na or j < nb:
        if j >= nb or (i < na and i * max(nb, 1) <= j * max(na, 1)):
            S.add(*ca[i])
            i += 1
        else:
            S.add(*cb[j])
            j += 1


class SBAlloc:
    def __init__(self, nc, nbytes=200 * 1024):
        self.t = nc.alloc_sbuf_tensor("sb_all", [128, nbytes // 2], BF16)
        self.nbytes = nbytes
        self.off = 0
        self.mark = 0

    def alloc(self, free_shape, dtype):
        esz = 4 if dtype == F32 else 2
        nel = int(np.prod(free_shape))
        nb = nel * esz
        self.off = (self.off + 63) // 64 * 64
        assert self.off + nb <= self.nbytes, f"SBUF overflow {self.off + nb}"
        v = self.t[:, self.off // 2:(self.off + nb) // 2]
        if dtype == F32:
            v = v.bitcast(F32)
        self.off += nb
        if len(free_shape) == 2:
            v = v.rearrange("p (a b) -> p a b", a=free_shape[0])
        elif len(free_shape) == 3:
            v = v.rearrange("p (a b c) -> p a b c", a=free_shape[0], b=free_shape[1])
        return v

    def set_mark(self):
        self.mark = self.off

    def reset(self):
        self.off = self.mark


class Ctx:
    def __init__(self, nc, seqs, debug=False):
        self.nc = nc
        self.S = Sched(nc)
        self.seqs = list(seqs)
        self.offs = [int(x) for x in np.cumsum([0] + self.seqs[:-1])]
        self.T = int(sum(self.seqs))
        self.debug = debug
        self.sb = SBAlloc(nc)
        self.ps = [nc.alloc_psum_tensor(f"psb{i}", [128, 512], F32).ap() for i in range(8)]
        self.ps_rr = 0
        self.ev_rr = 0
        self.dram = {}
        self.inputs = {}
        self.nw = [-(-t // 254) for t in self.seqs]
        self.pstride = [254 * n + 2 for n in self.nw]
        self.poffs = [int(x) for x in np.cumsum([0] + self.pstride[:-1])]
        self.TP = int(sum(self.pstride))
        self.bar_deps = {}

    def with_seqs(self, seqs):
        import copy
        c2 = copy.copy(self)
        c2.seqs = list(seqs)
        c2.offs = [int(x) for x in np.cumsum([0] + c2.seqs[:-1])]
        c2.T = int(sum(c2.seqs))
        c2.nw = [-(-t // 254) for t in c2.seqs]
        c2.pstride = [254 * n + 2 for n in c2.nw]
        c2.poffs = [int(x) for x in np.cumsum([0] + c2.pstride[:-1])]
        c2.TP = int(sum(c2.pstride))
        return c2

    def inp(self, name, shape, dtype=F32):
        if name in self.inputs:
            return self.inputs[name]
        t = self.nc.dram_tensor(name, list(shape), dtype, kind="ExternalInput").ap()
        self.inputs[name] = t
        return t

    def scratch(self, name, shape, dtype, out=False):
        kind = "ExternalOutput" if (out or self.debug) else "Internal"
        t = self.nc.dram_tensor(name, list(shape), dtype, kind=kind).ap()
        self.dram[name] = t
        return t

    bank_pool = None

    def bank(self):
        pool = self.bank_pool or (0, 1, 2, 3, 4, 5, 6, 7)
        i = pool[self.ps_rr % len(pool)]
        self.ps_rr += 1
        return i

    def evac(self, out, in_, reads, writes, scale=None):
        S = self.S
        self.ev_rr += 1
        if self.ev_rr % 2 == 0:
            if scale is None:
                S.act(lambda e: e.activation(out=out, in_=in_, func=AF.Copy), reads, writes)
            else:
                S.act(lambda e: e.activation(out=out, in_=in_, func=AF.Copy, scale=scale), reads, writes)
        else:
            if scale is None:
                S.dve(lambda e: e.tensor_copy(out=out, in_=in_), reads, writes)
            else:
                S.dve(lambda e: e.tensor_scalar(out=out, in0=in_, scalar1=scale, scalar2=None, op0=ALU.mult),
                      reads, writes)

    def barrier(self):
        S = self.S
        last = {}
        dmas = []
        for i, op in enumerate(S.ops):
            if op[3]:
                dmas.append(i)
            else:
                last[op[0]] = i
        deps = set(last.values())
        for e in ("sp", "pool", "act"):
            de = [i for i in dmas if S.ops[i][0] == e]
            deps |= set(de[-S.NDMA.get(e, 8):])
        for e in ("pe", "dve", "act", "pool"):
            S.ops.append([e, (lambda en: en.nop()) if e != "pe" else (lambda en: en.nop()), set(deps), False])
        S.ops.append(["sp", lambda en: en.nop(), set(deps), False])
        S.res_w = {}
        S.res_r = {}


def load_w_bf16(C, dst, src, key, kt):
    for k in range(kt):
        C.S.dma(dst[:, k, :], src[k * 128:(k + 1) * 128, :], reads=(), writes=[(key, k)], eng="pool")


def load_bcast(C, dst, src_row, key):
    n = src_row.shape[-1]
    C.S.dma(dst, src_row.to_broadcast([128, n]), reads=(), writes=[key])


def norm_tile(C, x_t, xkeys, wbc, wkey, out_t, okeys, ss, sskey, junk, jkey, ns, func_out=None):
    S = C.S
    for s in range(ns):
        S.act(lambda e, s=s: e.activation(out=junk, in_=x_t[:, s, :], func=AF.Square, accum_out=ss[:, s:s + 1]),
              reads=[xkeys[s]], writes=[jkey, (sskey, s)])
    sk = [(sskey, s) for s in range(ns)]
    S.dve(lambda e: e.tensor_scalar(out=ss[:, 0:ns], in0=ss[:, 0:ns], scalar1=1.0 / D, scalar2=EPS,
                                    op0=ALU.mult, op1=ALU.add), reads=sk, writes=sk)
    S.act(lambda e: e.activation(out=ss[:, 0:ns], in_=ss[:, 0:ns], func=AF.Sqrt), reads=sk, writes=sk)
    S.dve(lambda e: e.reciprocal(out=ss[:, 0:ns], in_=ss[:, 0:ns]), reads=sk, writes=sk)
    for s in range(ns):
        S.dve(lambda e, s=s: e.scalar_tensor_tensor(out=out_t[:, s, :], in0=x_t[:, s, :], scalar=ss[:, s:s + 1],
                                                    in1=wbc, op0=ALU.mult, op1=ALU.mult),
              reads=[xkeys[s], (sskey, s), wkey], writes=[okeys[s]])


def phase_norm0(C, x_src, w_row, xn_dst):
    S, sb = C.S, C.sb
    sb.reset()
    wbc = sb.alloc((D,), F32)
    load_bcast(C, wbc, w_row, "n0w")
    xt = [sb.alloc((4, D), F32) for _ in range(2)]
    xn = [sb.alloc((4, D), BF16) for _ in range(2)]
    ss = [sb.alloc((4,), F32) for _ in range(2)]
    junk = sb.alloc((D,), BF16)
    nt = C.T // 512
    xs = x_src.rearrange("(n s p) d -> n p s d", s=4, p=128)
    xd = xn_dst.rearrange("(n s p) d -> n p s d", s=4, p=128)
    for i in range(nt):
        b = i % 2
        S.dma(xt[b], xs[i], reads=[], writes=[("n0x", b)])
        norm_tile(C, xt[b], [("n0x", b)] * 4, wbc, "n0w", xn[b], [("n0o", b, s) for s in range(4)], ss[b],
                  ("n0ss", b), junk, "n0j", 4)
        S.dma(xd[i], xn[b], reads=[("n0o", b, s) for s in range(4)], writes=[("xn_d", i)])
    C.barrier()


def phase_proj(C, xn_src, W_src, nout, fm, tm, tag):
    S, sb = C.S, C.sb
    sb.reset()
    W = sb.alloc((8, nout), BF16)
    load_w_bf16(C, W, W_src, tag + "W", 8)
    wkeys = [(tag + "W", k) for k in range(8)]
    xnT = [sb.alloc((8, 512), BF16) for _ in range(2)]
    st_fm = [sb.alloc((4, 512), BF16) for _ in range(2)]
    st_tm = {}
    for j, (c0, n, dst, dt) in enumerate(tm):
        st_tm[j] = [sb.alloc((4, n), dt) for _ in range(2)]
    nt = C.T // 512
    fm_rr = 0
    for i in range(nt):
        b = i % 2
        t0 = i * 512
        for k in range(8):
            S.dma(xnT[b][:, k, :], xn_src[t0:t0 + 512, k * 128:(k + 1) * 128], reads=[("xn_d", i)],
                  writes=[(tag + "xT", b, k)], transpose=True)
        xkeys = [(tag + "xT", b, k) for k in range(8)]
        for (c0, n, dst) in fm:
            nm = n // 128
            for m0 in range(0, nm, 4):
                sbuf = fm_rr % 2
                fm_rr += 1
                mm = min(4, nm - m0)
                for j in range(mm):
                    m = m0 + j
                    bk = C.bank()
                    for k in range(8):
                        S.pe(lambda e, bk=bk, k=k, m=m, c0=c0, b=b: e.matmul(
                            C.ps[bk], lhsT=W[:, k, c0 + m * 128:c0 + (m + 1) * 128], rhs=xnT[b][:, k, :],
                            start=(k == 0), stop=(k == 7)),
                            reads=[wkeys[k], xkeys[k]], writes=[("ps", bk)])
                    C.evac(st_fm[sbuf][:, j, :], C.ps[bk], reads=[("ps", bk)], writes=[(tag + "sf", sbuf, j)])
                d = dst[m0 * 128:(m0 + mm) * 128, t0:t0 + 512].rearrange("(m p) t -> p m t", p=128)
                S.dma(d, st_fm[sbuf][:, 0:mm, :], reads=[(tag + "sf", sbuf, j) for j in range(mm)],
                      writes=[(tag + "fm_d", id(dst), i)])
        for j, (c0, n, dst, dt) in enumerate(tm):
            stg = st_tm[j][b]
            for s in range(4):
                bk = C.bank()
                for k in range(8):
                    S.pe(lambda e, bk=bk, k=k, s=s, c0=c0, n=n, b=b: e.matmul(
                        C.ps[bk][:, 0:n], lhsT=xnT[b][:, k, s * 128:(s + 1) * 128], rhs=W[:, k, c0:c0 + n],
                        start=(k == 0), stop=(k == 7)),
                        reads=[wkeys[k], xkeys[k]], writes=[("ps", bk)])
                C.evac(stg[:, s, :], C.ps[bk][:, 0:n], reads=[("ps", bk)], writes=[(tag + "st", j, b, s)])
            d = dst[t0:t0 + 512, :].rearrange("(s p) c -> p s c", p=128)
            S.dma(d, stg, reads=[(tag + "st", j, b, s) for s in range(4)], writes=[(tag + "tm_d", j, i)])
    C.barrier()


def load_cols(C, dst, src_rows, nrow, ident32, tag):
    S, sb = C.S, C.sb
    tmp = sb.alloc((128,), F32)
    S.dma(tmp[0:nrow, :], src_rows, reads=[], writes=[("tmpc", tag)])
    bk = C.bank()
    S.pe(lambda e: e.transpose(C.ps[bk][:, 0:nrow], tmp[0:nrow, :], ident32[0:nrow, 0:nrow]),
         reads=[("tmpc", tag), "ident32"], writes=[("ps", bk)])
    S.dve(lambda e: e.tensor_copy(out=dst, in_=C.ps[bk][:, 0:nrow]), reads=[("ps", bk)], writes=[tag])


def phase_outproj(C, mix_src, kdim, Wo_src, x_src, w_row, xr_dst, xn_dst, tag, mix_key):
    S, sb = C.S, C.sb
    sb.reset()
    kt = kdim // 128
    Wo = sb.alloc((kt, D), BF16)
    load_w_bf16(C, Wo, Wo_src, tag + "W", kt)
    wbc = sb.alloc((D,), F32)
    load_bcast(C, wbc, w_row, tag + "nw")
    mixT = [sb.alloc((kt, 512), BF16) for _ in range(2)]
    xt = [sb.alloc((4, D), F32) for _ in range(2)]
    xn = [sb.alloc((4, D), BF16) for _ in range(2)]
    ss = [sb.alloc((4,), F32) for _ in range(2)]
    junk = sb.alloc((D,), BF16)
    zero = sb.alloc((D,), BF16)
    S.dve(lambda e: e.memset(zero, 0.0), reads=[], writes=[tag + "zero"])
    for si, T in enumerate(C.seqs):
        p0 = C.poffs[si]
        S.dma(xn_dst[p0:p0 + 1, :], zero[0:1, :], reads=[tag + "zero"], writes=[("xnp_pad", si, 0)])
        r0 = p0 + 1 + T
        r1 = p0 + C.pstride[si]
        while r0 < r1:
            n = min(128, r1 - r0)
            S.dma(xn_dst[r0:r0 + n, :], zero[0:n, :], reads=[tag + "zero"], writes=[("xnp_pad", si, r0)])
            r0 += n
    ti = 0
    for si, T in enumerate(C.seqs):
        for i in range(T // 512):
            b = ti % 2
            g0 = C.offs[si] + i * 512
            pr0 = C.poffs[si] + 1 + i * 512
            for k in range(kt):
                S.dma(mixT[b][:, k, :], mix_src[g0:g0 + 512, k * 128:(k + 1) * 128], reads=[(mix_key, g0 // 512)],
                      writes=[(tag + "mT", b, k)], transpose=True)
            S.dma(xt[b], x_src[g0:g0 + 512, :].rearrange("(s p) d -> p s d", p=128), reads=[("xres_d", g0 // 512)],
                  writes=[(tag + "x", b, s) for s in range(4)])
            for s_ in range(4):
                for h in range(2):
                    bk = C.bank()
                    for k in range(kt):
                        S.pe(lambda e, bk=bk, k=k, s_=s_, h=h, b=b: e.matmul(
                            C.ps[bk], lhsT=mixT[b][:, k, s_ * 128:(s_ + 1) * 128], rhs=Wo[:, k, h * 512:(h + 1) * 512],
                            start=(k == 0), stop=(k == kt - 1)),
                            reads=[(tag + "W", k), (tag + "mT", b, k)], writes=[("ps", bk)])
                    S.dve(lambda e, bk=bk, s_=s_, h=h, b=b: e.tensor_tensor(
                        out=xt[b][:, s_, h * 512:(h + 1) * 512], in0=C.ps[bk], in1=xt[b][:, s_, h * 512:(h + 1) * 512],
                        op=ALU.add), reads=[("ps", bk), (tag + "x", b, s_)], writes=[(tag + "x", b, s_)])
            norm_tile(C, xt[b], [(tag + "x", b, s) for s in range(4)], wbc, tag + "nw", xn[b],
                      [(tag + "xn", b, s) for s in range(4)], ss[b], (tag + "ss", b), junk, tag + "j", 4)
            S.dma(xr_dst[pr0:pr0 + 512, :].rearrange("(s p) d -> p s d", p=128), xt[b],
                  reads=[(tag + "x", b, s) for s in range(4)], writes=[("xrp_d", si, i)])
            S.dma(xn_dst[pr0:pr0 + 512, :].rearrange("(s p) d -> p s d", p=128), xn[b],
                  reads=[(tag + "xn", b, s) for s in range(4)], writes=[("xnp_d", si, i)])
            ti += 1
    C.barrier()


def phase_ffn(C, xnp_src, xrp_src, Wu_src, Wd_src, cw_src, cb_src, w_row, ident32, out_specs, tag, final):
    S, sb = C.S, C.sb
    sb.reset()
    Wu = sb.alloc((8, 2 * D_FF), BF16)
    Wd = sb.alloc((22, D), BF16)
    load_w_bf16(C, Wu, Wu_src, tag + "Wu", 8)
    load_w_bf16(C, Wd, Wd_src, tag + "Wd", 22)
    wbc = sb.alloc((D,), F32)
    load_bcast(C, wbc, w_row, tag + "nw")
    cw = sb.alloc((3, 44), F32)
    cb = sb.alloc((44,), F32)
    for i in range(3):
        load_cols(C, cw[:, i, :], cw_src[i].rearrange("(m p) -> m p", p=128), 44, ident32, (tag + "cw", i))
    load_cols(C, cb, cb_src.rearrange("(m p) -> m p", p=128), 44, ident32, tag + "cb")
    cwk = [(tag + "cw", i) for i in range(3)] + [tag + "cb"]
    xnT = [sb.alloc((8, 256), BF16) for _ in range(2)]
    xt = [sb.alloc((2, D), F32) for _ in range(2)]
    if final:
        xo1 = sb.alloc((2, D), F32)
        xo = [xo1, xo1]
    else:
        xo = [sb.alloc((2, D), BF16) for _ in range(2)]
    a_t1 = sb.alloc((22, 256), BF16)
    a_t = [a_t1, a_t1]
    S.pool(lambda e: e.memset(a_t1, 0.0), reads=[], writes=[(tag + "a", 0, m) for m in range(22)])
    tv = [sb.alloc((256,), F32) for _ in range(2)]
    tg = [sb.alloc((256,), F32) for _ in range(2)]
    ss = [sb.alloc((2,), F32) for _ in range(2)]
    junk = sb.alloc((D,), BF16)
    wi = 0
    for si, T in enumerate(C.seqs):
        for w in range(C.nw[si]):
            b = wi % 2
            u0 = 254 * w
            pr0 = C.poffs[si] + u0
            for k in range(8):
                S.dma(xnT[b][:, k, :], xnp_src[pr0:pr0 + 256, k * 128:(k + 1) * 128],
                      reads=[("xnp_d", si, j) for j in range(u0 // 512, min(T // 512, (u0 + 255) // 512 + 1))] +
                      [("xnp_pad", si, 0)], writes=[(tag + "xT", b, k)], transpose=True)
            S.dma(xt[b], xrp_src[pr0:pr0 + 256, :].rearrange("(s p) d -> p s d", p=128),
                  reads=[("xrp_d", si, j) for j in range(u0 // 512, min(T // 512, (u0 + 255) // 512 + 1))],
                  writes=[(tag + "x", b, s) for s in range(2)])
            xk = [(tag + "xT", b, k) for k in range(8)]
            for m in range(22):
                tb = m % 2
                bv = C.bank()
                for k in range(8):
                    S.pe(lambda e, bk=bv, k=k, m=m, b=b: e.matmul(
                        C.ps[bk][:, 0:256], lhsT=Wu[:, k, m * 128:(m + 1) * 128], rhs=xnT[b][:, k, :],
                        start=(k == 0), stop=(k == 7)), reads=[(tag + "Wu", k), xk[k]], writes=[("ps", bv)])
                bg = C.bank()
                for k in range(8):
                    S.pe(lambda e, bk=bg, k=k, m=m, b=b: e.matmul(
                        C.ps[bk][:, 0:256], lhsT=Wu[:, k, D_FF + m * 128:D_FF + (m + 1) * 128], rhs=xnT[b][:, k, :],
                        start=(k == 0), stop=(k == 7)), reads=[(tag + "Wu", k), xk[k]], writes=[("ps", bg)])
                for (bk, tt, mm, key) in ((bv, tv[tb], m, (tag + "tv", tb)), (bg, tg[tb], 22 + m, (tag + "tg", tb))):
                    S.act(lambda e, bk=bk, tt=tt, mm=mm: e.activation(
                        out=tt[:, 1:255], in_=C.ps[bk][:, 1:255], func=AF.Identity, scale=cw[:, 1, mm:mm + 1],
                        bias=cb[:, mm:mm + 1]), reads=[("ps", bk)] + cwk, writes=[key])
                    S.dve(lambda e, bk=bk, tt=tt, mm=mm: e.scalar_tensor_tensor(
                        out=tt[:, 1:255], in0=C.ps[bk][:, 0:254], scalar=cw[:, 0, mm:mm + 1], in1=tt[:, 1:255],
                        op0=ALU.mult, op1=ALU.add), reads=[("ps", bk), key] + cwk, writes=[key])
                    S.dve(lambda e, bk=bk, tt=tt, mm=mm: e.scalar_tensor_tensor(
                        out=tt[:, 1:255], in0=C.ps[bk][:, 2:256], scalar=cw[:, 2, mm:mm + 1], in1=tt[:, 1:255],
                        op0=ALU.mult, op1=ALU.add), reads=[("ps", bk), key] + cwk, writes=[key])
                S.act(lambda e, tb=tb: e.activation(out=tg[tb][:, 1:255], in_=tg[tb][:, 1:255], func=AF.Silu),
                      reads=[(tag + "tg", tb)], writes=[(tag + "tg", tb)])
                S.pool(lambda e, tb=tb, m=m, b=b: e.tensor_tensor(
                    out=a_t[b][:, m, 1:255], in0=tg[tb][:, 1:255], in1=tv[tb][:, 1:255], op=ALU.mult),
                    reads=[(tag + "tg", tb), (tag + "tv", tb)], writes=[(tag + "a", 0, m)])
            for s_ in range(2):
                for h in range(2):
                    bk = C.bank()
                    for kk in range(22):
                        S.pe(lambda e, bk=bk, kk=kk, s_=s_, h=h, b=b: e.matmul(
                            C.ps[bk], lhsT=a_t[b][:, kk, s_ * 128:(s_ + 1) * 128], rhs=Wd[:, kk, h * 512:(h + 1) * 512],
                            start=(kk == 0), stop=(kk == 21)),
                            reads=[(tag + "Wd", kk), (tag + "a", 0, kk)], writes=[("ps", bk)])
                    S.dve(lambda e, bk=bk, s_=s_, h=h, b=b: e.tensor_tensor(
                        out=xt[b][:, s_, h * 512:(h + 1) * 512], in0=C.ps[bk], in1=xt[b][:, s_, h * 512:(h + 1) * 512],
                        op=ALU.add), reads=[("ps", bk), (tag + "x", b, s_)], writes=[(tag + "x", b, s_)])
            norm_tile(C, xt[b], [(tag + "x", b, s) for s in range(2)], wbc, tag + "nw", xo[b],
                      [(tag + "xo", b if not final else 0, s) for s in range(2)], ss[b], (tag + "ss", b), junk, tag + "j", 2)
            jhi = min(254, T - u0)
            for s_ in range(2):
                plo = max(0, 1 - 128 * s_)
                phi = min(127, jhi - 128 * s_)
                if phi < plo:
                    continue
                t_lo = u0 + 128 * s_ + plo - 1
                n = phi - plo + 1
                out_specs(si, t_lo, n, xt[b][plo:plo + n, s_, :], xo[b][plo:plo + n, s_, :],
                          [(tag + "x", b, s_)], [(tag + "xo", b if not final else 0, s_)])
            wi += 1
    C.barrier()


def phase_gdn(C, projT, ztok, gates, of_s, og, K_, conv_src, alog_src, dtb_src, gnorm_src, NH=8, HB=4):
    S, sb = C.S, C.sb
    S_real = S
    sb.reset()
    NC2 = 2 * NH
    ident_bf, ident32 = K_["ident_bf"], K_["ident32"]
    ck = list(K_["keys"])
    A16 = sb.alloc((NC2,), F32)
    DTB = sb.alloc((NC2,), F32)
    load_bcast(C, A16, alog_src, "gA16")
    load_bcast(C, DTB, dtb_src, "gDTB")
    S.act(lambda e: e.activation(out=A16, in_=A16, func=AF.Exp), reads=["gA16"], writes=["gA16"])
    S.dve(lambda e: e.tensor_scalar(out=A16, in0=A16, scalar1=-1.0, scalar2=None, op0=ALU.mult),
          reads=["gA16"], writes=["gA16"])
    sel_d = C.inp("csel", [16, 2048], F32)
    SELt = sb.alloc((16, 128), F32)
    S.dma(SELt[0:16, :, :], sel_d.rearrange("p (a b) -> p a b", a=16), reads=[], writes=["csel"])
    onec = sb.alloc((1,), F32)
    epsc = sb.alloc((1,), F32)
    S.dve(lambda e: e.memset(onec, 1.0), reads=[], writes=["gonec"])
    S.dve(lambda e: e.memset(epsc, EPS), reads=[], writes=["gepsc"])
    gnw = sb.alloc((128,), F32)
    load_bcast(C, gnw, gnorm_src, "gnw")
    cwg = sb.alloc((5, 3 * NH), F32)
    for i in range(5):
        load_cols(C, cwg[:, i, :], conv_src[i].rearrange("(m p) -> m p", p=128), 3 * NH, ident32, ("gcw", i))
    DG = sb.alloc((3 * HB * 5, 128), BF16)

    def build_DG(hg):
        for w3 in range(3):
            for hh in range(HB):
                ct = w3 * NH + hg * HB + hh
                lc = w3 * HB + hh
                for i in range(5):
                    S.dve(lambda e, ct=ct, lc=lc, i=i: e.tensor_scalar(out=DG[:, lc * 5 + i, :], in0=ident_bf,
                                                                      scalar1=cwg[:, i, ct:ct + 1], scalar2=None, op0=ALU.mult),
                          reads=[("gcw", i), "ident_bf"], writes=[("gDG", lc)])
    SEGT = 32
    NTmax = SEGT
    GA = sb.alloc((NTmax, 2 * NC2), F32)
    G = sb.alloc((NTmax, NC2), F32)
    GH = sb.alloc((NTmax, NC2), BF16)
    GLo = sb.alloc((NTmax, NC2), BF16)
    BETA = sb.alloc((NTmax, NC2), F32)
    NBETA = sb.alloc((NTmax, NC2), F32)
    GC = sb.alloc((NTmax, NC2), F32)
    EGC = sb.alloc((NTmax, NC2), F32)
    EKD = sb.alloc((NTmax, NC2), F32)
    DEC = sb.alloc((NTmax, 2, NC2), F32)
    X = [[sb.alloc((516,), BF16) for _ in range(3 * HB)] for _ in range(1)]
    qc = [sb.alloc((512,), F32) for _ in range(2)]
    sq = [sb.alloc((512,), BF16) for _ in range(2)]
    rn = [sb.alloc((512,), F32) for _ in range(2)]
    qT = [sb.alloc((HB, 512), BF16) for _ in range(2)]
    kT = [sb.alloc((HB, 512), BF16) for _ in range(2)]
    vT = [sb.alloc((HB, 512), BF16) for _ in range(2)]
    qdT = [sb.alloc((HB, 512), BF16) for _ in range(2)]
    ktok = [sb.alloc((HB, 4, 128), BF16) for _ in range(2)]
    vtok = [sb.alloc((HB, 4, 128), BF16) for _ in range(2)]
    EGT = [sb.alloc((512,), F32) for _ in range(2)]
    NB2 = 2
    gUh = [sb.alloc((HB, 128), BF16) for _ in range(NB2)]
    gUl = [sb.alloc((HB, 128), BF16) for _ in range(NB2)]
    Dm = [sb.alloc((HB, 128), BF16) for _ in range(NB2)]
    DmTi = [sb.alloc((HB, 128), BF16) for _ in range(NB2)]
    attnT = [sb.alloc((HB, 128), BF16) for _ in range(NB2)]
    Pm = [[sb.alloc((HB, 128), BF16) for _ in range(2)] for _ in range(NB2)]
    Qm = [[sb.alloc((HB, 128), BF16) for _ in range(2)] for _ in range(NB2)]
    Ym = [[sb.alloc((HB, 128), BF16) for _ in range(2)] for _ in range(NB2)]
    TbT = [sb.alloc((HB, 128), BF16) for _ in range(NB2)]
    keg = [sb.alloc((HB, 128), BF16) for _ in range(NB2)]
    kdec = [sb.alloc((HB, 128), BF16) for _ in range(NB2)]
    negwT = [sb.alloc((HB, 128), BF16) for _ in range(NB2)]
    vnew = [sb.alloc((HB, 128), BF16) for _ in range(NB2)]
    S32 = sb.alloc((HB, 128), F32)
    S16 = sb.alloc((HB, 128), BF16)
    ob = [sb.alloc((HB, 128), BF16) for _ in range(2)]
    o32 = [sb.alloc((HB, 128), F32) for _ in range(2)]
    ofl = [sb.alloc((HB, 128), BF16) for _ in range(2)]
    zt = [sb.alloc((HB, 128), BF16) for _ in range(2)]
    ogt = [sb.alloc((HB, 128), BF16) for _ in range(2)]
    o2 = sb.alloc((HB, 128), F32)
    ssq = [sb.alloc((HB,), F32) for _ in range(2)]
    junkg = sb.alloc((128,), BF16)

    def psb(bk):
        return C.ps[bk].bitcast(BF16)

    def ps4(bk):
        return C.ps[bk][:, 0:HB * 128].rearrange("p (h c) -> p h c", h=HB)

    def ps4b(bk):
        return psb(bk)[:, 0:HB * 128].rearrange("p (h c) -> p h c", h=HB)

    tcount = [0]
    for si, T in enumerate(C.seqs):
        NTall = T // 128
        NBk = T // 512
        g0 = C.offs[si]
        def gates_prep(tile0, NT, g0=g0):
            S.dma(GA[:, 0:NT, :], gates[g0 + tile0 * 128:g0 + (tile0 + NT) * 128, :].rearrange("(n p) c -> p n c", p=128), reads=[], writes=["GA"])
            Ga = GA[:, 0:NT, 0:NC2]
            Gb = GA[:, 0:NT, NC2:2 * NC2]
            Gv = G[:, 0:NT, :]
            S.dve(lambda e, Ga=Ga, Gv=Gv, NT=NT: e.tensor_tensor(out=Gv, in0=Ga, in1=DTB.unsqueeze(1).to_broadcast([128, NT, NC2]),
                                                                 op=ALU.add), reads=["GA", "gDTB"], writes=["G"])
            S.act(lambda e, Gv=Gv: e.activation(out=Gv, in_=Gv, func=AF.Exp), reads=["G"], writes=["G"])
            S.act(lambda e, Gv=Gv: e.activation(out=Gv, in_=Gv, func=AF.Ln, bias=onec[:, 0:1]), reads=["G", "gonec"], writes=["G"])
            S.dve(lambda e, Gv=Gv, NT=NT: e.tensor_tensor(out=Gv, in0=Gv, in1=A16.unsqueeze(1).to_broadcast([128, NT, NC2]),
                                                         op=ALU.mult), reads=["G", "gA16"], writes=["G"])
            S.act(lambda e, Gv=Gv, NT=NT: e.activation(out=GH[:, 0:NT, :], in_=Gv, func=AF.Copy), reads=["G"], writes=["GH"])
            S.dve(lambda e, Gv=Gv, NT=NT: e.tensor_tensor(out=GLo[:, 0:NT, :], in0=Gv, in1=GH[:, 0:NT, :], op=ALU.subtract),
                  reads=["G", "GH"], writes=["GLo"])
            Bv = BETA[:, 0:NT, :]
            S.act(lambda e, Gb=Gb, Bv=Bv: e.activation(out=Bv, in_=Gb, func=AF.Exp, scale=-1.0), reads=["GA"], writes=["BETA"])
            S.dve(lambda e, Bv=Bv: e.tensor_scalar(out=Bv, in0=Bv, scalar1=1.0, scalar2=None, op0=ALU.add),
                  reads=["BETA"], writes=["BETA"])
            S.dve(lambda e, Bv=Bv: e.reciprocal(out=Bv, in_=Bv), reads=["BETA"], writes=["BETA"])
            S.dve(lambda e, Bv=Bv, NT=NT: e.tensor_scalar(out=NBETA[:, 0:NT, :], in0=Bv, scalar1=-1.0, scalar2=None,
                                                          op0=ALU.mult), reads=["BETA"], writes=["NBETA"])
            for t0 in range(0, NT, 32):
                n = min(32, NT - t0)
                bk = C.bank()
                for t in range(t0, t0 + n):
                    c0 = (t - t0) * NC2
                    S.pe(lambda e, bk=bk, t=t, c0=c0: e.matmul(C.ps[bk][:, c0:c0 + NH], lhsT=K_["UF32"], rhs=G[:, t, 0:NH],
                                                               start=True, stop=True), reads=["G"] + ck, writes=[("ps", bk)])
                    S.pe(lambda e, bk=bk, t=t, c0=c0: e.matmul(C.ps[bk][:, c0 + NH:c0 + NC2], lhsT=K_["UB32"], rhs=G[:, t, NH:NC2],
                                                               start=True, stop=True), reads=["G"] + ck, writes=[("ps", bk)])
                S.dve(lambda e, bk=bk, t0=t0, n=n: e.tensor_copy(
                    out=GC[:, t0:t0 + n, :], in_=C.ps[bk][:, 0:n * NC2].rearrange("p (n c) -> p n c", c=NC2)),
                    reads=[("ps", bk)], writes=["GC"])
            S.act(lambda e, NT=NT: e.activation(out=EGC[:, 0:NT, :], in_=GC[:, 0:NT, :], func=AF.Exp), reads=["GC"], writes=["EGC"])
            for t0 in range(0, NT, 16):
                n = min(16, NT - t0)
                bk = C.bank()
                for t in range(t0, t0 + n):
                    for j in range(2):
                        c0 = ((t - t0) * 2 + j) * NC2
                        S.pe(lambda e, bk=bk, t=t, c0=c0, j=j: e.matmul(C.ps[bk][:, c0:c0 + NC2], lhsT=K_["CH%d" % j],
                                                                        rhs=G[:, t, :], start=True, stop=True),
                             reads=["G"] + ck, writes=[("ps", bk)])
                pv = C.ps[bk][:, 0:n * 2 * NC2].rearrange("p (n j c) -> p n j c", j=2, c=NC2)
                S.act(lambda e, pv=pv, t0=t0, n=n: e.activation(out=DEC[:, t0:t0 + n, :, :], in_=pv, func=AF.Exp),
                      reads=[("ps", bk)], writes=["DEC"])
                for j in range(2):
                    S.dve(lambda e, pv=pv, t0=t0, n=n, j=j: e.tensor_tensor(
                        out=EKD[64 * j:64 * j + 64, t0:t0 + n, :], in0=pv[64 * j:64 * j + 64, :, j, :],
                        in1=GC[64 * j:64 * j + 64, t0:t0 + n, :], op=ALU.subtract),
                        reads=[("ps", bk), "GC"], writes=["EKD"])
            S.act(lambda e, NT=NT: e.activation(out=EKD[:, 0:NT, :], in_=EKD[:, 0:NT, :], func=AF.Exp), reads=["EKD"], writes=["EKD"])
            gk = ["G", "GH", "GLo", "BETA", "NBETA", "GC", "EGC", "EKD", "DEC"]

        for hg in range(NH // HB):
            for di in range(2):
                Ud = K_["UF32"] if di == 0 else K_["UB32"]
                MA = K_["MA_f"] if di == 0 else K_["MA_b"]
                S.dve(lambda e: e.memset(S32, 0.0), reads=[], writes=["S32"])
                S.dve(lambda e: e.memset(S16, 0.0), reads=[], writes=["S16"])
                if di == 0:
                    build_DG(hg)
                blocks = list(range(NBk)) if di == 0 else list(range(NBk - 1, -1, -1))
                cur_seg = None
                pending_rc = None
                for bi, blk in enumerate(blocks):
                    bb = bi % 2
                    t0 = blk * 512
                    seg = (blk * 4) // SEGT
                    if seg != cur_seg:
                        cur_seg = seg
                        replay_merged(S_real, pending_rc, None)
                        pending_rc = None
                        gates_prep(seg * SEGT, min(SEGT, NTall - seg * SEGT))
                    tl0 = seg * SEGT
                    S = C.S = Rec()
                    C.bank_pool = (0, 1, 2, 3, 4)
                    first_tile = True
                    for hh in range(HB):
                        h = hg * HB + hh
                        for w3 in range(3):
                            ct = w3 * NH + h
                            xb = X[0][w3 * HB + hh]
                            xkey = ("gX", 0, w3, hh)
                            lo, hi = t0 - 2, t0 + 514
                            dlo, dhi = max(lo, 0), min(hi, T)
                            if dlo > lo:
                                S.pool(lambda e, xb=xb: e.memset(xb[:, 0:2], 0.0), reads=[], writes=[xkey])
                            if dhi < hi:
                                S.pool(lambda e, xb=xb: e.memset(xb[:, 514:516], 0.0), reads=[], writes=[xkey])
                            S.dma(xb[:, dlo - lo:dhi - lo], projT[ct * 128:(ct + 1) * 128, g0 + dlo:g0 + dhi],
                                  reads=[], writes=[xkey])
                            bk = C.bank()
                            for i in range(5):
                                S.pe(lambda e, bk=bk, i=i, xb=xb, w3=w3, hh=hh: e.matmul(
                                    C.ps[bk], lhsT=DG[:, (w3 * HB + hh) * 5 + i, :], rhs=xb[:, i:i + 512], start=(i == 0), stop=(i == 4)),
                                    reads=[xkey, ("gDG", w3 * HB + hh)], writes=[("ps", bk)])
                            if w3 == 2:
                                S.act(lambda e, bk=bk, bb=bb, hh=hh: e.activation(out=vT[bb][:, hh, :], in_=C.ps[bk], func=AF.Silu),
                                      reads=[("ps", bk)], writes=[("gvT", bb, hh)])
                                continue
                            q2 = tcount[0] % 2
                            tcount[0] += 1
                            S.act(lambda e, bk=bk, q2=q2: e.activation(out=qc[q2], in_=C.ps[bk], func=AF.Silu),
                                  reads=[("ps", bk)], writes=[("gqc", q2)])
                            S.act(lambda e, q2=q2: e.activation(out=sq[q2], in_=qc[q2], func=AF.Square),
                                  reads=[("gqc", q2)], writes=[("gsq", q2)])
                            bk2 = C.bank()
                            S.pe(lambda e, bk2=bk2, q2=q2: e.matmul(C.ps[bk2], lhsT=K_["ones_bf"], rhs=sq[q2], start=True, stop=True),
                                 reads=[("gsq", q2)] + ck, writes=[("ps", bk2)])
                            S.act(lambda e, bk2=bk2, q2=q2: e.activation(out=rn[q2], in_=C.ps[bk2], func=AF.Sqrt, bias=epsc[:, 0:1]),
                                  reads=[("ps", bk2), "gepsc"], writes=[("grn", q2)])
                            S.dve(lambda e, q2=q2: e.reciprocal(out=rn[q2], in_=rn[q2]), reads=[("grn", q2)], writes=[("grn", q2)])
                            dst = qT[bb][:, hh, :] if w3 == 0 else kT[bb][:, hh, :]
                            dkey = ("gqT", bb, hh) if w3 == 0 else ("gkT", bb, hh)
                            sc = (128.0 ** -0.5) if w3 == 0 else 1.0
                            S.dve(lambda e, q2=q2, dst=dst, sc=sc: e.scalar_tensor_tensor(
                                out=dst, in0=qc[q2], scalar=sc, in1=rn[q2], op0=ALU.mult, op1=ALU.mult),
                                reads=[("gqc", q2), ("grn", q2)], writes=[dkey])
                        for (srcT, dstk, skey, dkey) in ((kT[bb], ktok[bb], ("gkT", bb, hh), ("gktok", bb, hh)),
                                                         (vT[bb], vtok[bb], ("gvT", bb, hh), ("gvtok", bb, hh))):
                            bk = C.bank()
                            for t in range(4):
                                S.pe(lambda e, bk=bk, t=t, srcT=srcT, hh=hh: e.transpose(
                                    psb(bk)[:, t * 128:(t + 1) * 128], srcT[:, hh, t * 128:(t + 1) * 128], ident_bf),
                                    reads=[skey, "ident_bf"], writes=[("ps", bk)])
                            C.evac(dstk[:, hh, :, :], psb(bk)[:, 0:512].rearrange("p (t c) -> p t c", t=4),
                                   reads=[("ps", bk)], writes=[dkey])
                    bk = C.bank()
                    for t in range(4):
                        S.pe(lambda e, bk=bk, t=t, blk=blk, tl0=tl0: e.transpose(
                            C.ps[bk][0:NC2, t * 128:(t + 1) * 128], EGC[:, blk * 4 + t - tl0, :], ident32),
                            reads=["EGC", "ident32"], writes=[("ps", bk)])
                    S.dve(lambda e, bk=bk, bb=bb: e.tensor_copy(out=EGT[bb][0:NC2, :], in_=C.ps[bk][0:NC2, :]),
                          reads=[("ps", bk)], writes=[("gEGT", bb)])
                    for hh in range(HB):
                        hd = di * NH + hg * HB + hh
                        bk = C.bank()
                        S.pe(lambda e, bk=bk, hd=hd, bb=bb: e.matmul(C.ps[bk], lhsT=SELt[0:NC2, hd, :], rhs=EGT[bb][0:NC2, :],
                                                                     start=True, stop=True),
                             reads=[("gEGT", bb), "csel"] + ck, writes=[("ps", bk)])
                        S.dve(lambda e, bk=bk, hh=hh, bb=bb: e.tensor_tensor(out=qdT[bb][:, hh, :], in0=C.ps[bk], in1=qT[bb][:, hh, :],
                                                                             op=ALU.mult),
                              reads=[("ps", bk), ("gqT", bb, hh)], writes=[("gqdT", bb, hh)])
                    tiles = list(range(4)) if di == 0 else [3, 2, 1, 0]
                    for tl in tiles:
                        if not first_tile:
                            S = C.S = Rec()
                            C.bank_pool = (0, 1, 2, 3, 4)
                        first_tile = False
                        tt = blk * 4 + tl
                        lt = tt - tl0
                        wb = tt % NB2
                        ts_ = slice(tl * 128, (tl + 1) * 128)
                        hd0 = di * NH + hg * HB
                        Ub = Ud.unsqueeze(1).to_broadcast([128, HB, 128])
                        for (dst, src, key, skey) in ((gUh[wb], GH, ("gUh", wb), "GH"), (gUl[wb], GLo, ("gUl", wb), "GLo")):
                            S.pool(lambda e, dst=dst, src=src, lt=lt, hd0=hd0, Ub=Ub: e.tensor_tensor(
                                out=dst, in0=Ub, in1=src[:, lt, hd0:hd0 + HB].unsqueeze(2).to_broadcast([128, HB, 128]),
                                op=ALU.mult), reads=[skey] + ck, writes=[key])
                        bkE = C.bank()
                        for hh in range(HB):
                            o_ = ps4(bkE)[:, hh, :]
                            S.pe(lambda e, o_=o_, wb=wb, hh=hh: e.matmul(o_, lhsT=gUh[wb][:, hh, :], rhs=K_["ones_bf"], start=True, stop=False),
                                 reads=[("gUh", wb)] + ck, writes=[("ps", bkE)])
                            S.pe(lambda e, o_=o_, wb=wb, hh=hh: e.matmul(o_, lhsT=gUl[wb][:, hh, :], rhs=K_["ones_bf"], start=False, stop=False),
                                 reads=[("gUl", wb)] + ck, writes=[("ps", bkE)])
                            S.pe(lambda e, o_=o_, wb=wb, hh=hh: e.matmul(o_, lhsT=K_["negones_bf"], rhs=gUh[wb][:, hh, :], start=False, stop=False),
                                 reads=[("gUh", wb)] + ck, writes=[("ps", bkE)])
                            S.pe(lambda e, o_=o_, wb=wb, hh=hh: e.matmul(o_, lhsT=K_["negones_bf"], rhs=gUl[wb][:, hh, :], start=False, stop=False),
                                 reads=[("gUl", wb)] + ck, writes=[("ps", bkE)])
                            S.pe(lambda e, o_=o_, MA=MA: e.matmul(o_, lhsT=ident_bf, rhs=MA, start=False, stop=True),
                                 reads=ck, writes=[("ps", bkE)])
                        S.act(lambda e, bkE=bkE, wb=wb: e.activation(out=Dm[wb], in_=ps4(bkE), func=AF.Exp),
                              reads=[("ps", bkE)], writes=[("gDm", wb)])
                        bkT = C.bank()
                        for hh in range(HB):
                            S.pe(lambda e, bkT=bkT, wb=wb, hh=hh: e.transpose(ps4b(bkT)[:, hh, :], Dm[wb][:, hh, :], ident_bf),
                                 reads=[("gDm", wb), "ident_bf"], writes=[("ps", bkT)])
                        S.dve(lambda e, bkT=bkT, wb=wb: e.tensor_tensor(
                            out=DmTi[wb], in0=ps4b(bkT), in1=ident_bf.unsqueeze(1).to_broadcast([128, HB, 128]), op=ALU.add),
                            reads=[("ps", bkT), "ident_bf"], writes=[("gDmTi", wb)])
                        bkK = C.bank()
                        bkQ = C.bank()
                        for hh in range(HB):
                            S.pe(lambda e, bkK=bkK, hh=hh, bb=bb, ts_=ts_: e.matmul(ps4(bkK)[:, hh, :], lhsT=kT[bb][:, hh, ts_], rhs=kT[bb][:, hh, ts_],
                                                                                   start=True, stop=True),
                                 reads=[("gkT", bb, hh)], writes=[("ps", bkK)])
                            S.pe(lambda e, bkQ=bkQ, hh=hh, bb=bb, ts_=ts_: e.matmul(ps4(bkQ)[:, hh, :], lhsT=kT[bb][:, hh, ts_], rhs=qT[bb][:, hh, ts_],
                                                                                   start=True, stop=True),
                                 reads=[("gkT", bb, hh), ("gqT", bb, hh)], writes=[("ps", bkQ)])
                        Q0 = Qm[wb][0]
                        for hh in range(HB):
                            S.dve(lambda e, bkK=bkK, hh=hh, wb=wb, lt=lt, hd0=hd0, Q0=Q0: e.scalar_tensor_tensor(
                                out=Q0[:, hh, :], in0=ps4(bkK)[:, hh, :], scalar=NBETA[:, lt, hd0 + hh:hd0 + hh + 1], in1=Dm[wb][:, hh, :],
                                op0=ALU.mult, op1=ALU.mult), reads=[("ps", bkK), "NBETA", ("gDm", wb)], writes=[("gQ", wb, 0)])
                        S.dve(lambda e, bkQ=bkQ, wb=wb: e.tensor_tensor(out=attnT[wb], in0=ps4(bkQ), in1=DmTi[wb], op=ALU.mult),
                              reads=[("ps", bkQ), ("gDmTi", wb)], writes=[("gattnT", wb)])
                        bkN = C.bank()
                        for hh in range(HB):
                            S.pe(lambda e, bkN=bkN, hh=hh, Q0=Q0: e.transpose(ps4b(bkN)[:, hh, :], Q0[:, hh, :], ident_bf),
                                 reads=[("gQ", wb, 0), "ident_bf"], writes=[("ps", bkN)])
                        P0 = Pm[wb][0]
                        C.evac(P0, ps4b(bkN), reads=[("ps", bkN)], writes=[("gP", wb, 0)])
                        Y0 = Ym[wb][0]
                        S.dve(lambda e, bkN=bkN, Y0=Y0: e.tensor_tensor(
                            out=Y0, in0=ps4b(bkN), in1=ident_bf.unsqueeze(1).to_broadcast([128, HB, 128]), op=ALU.add),
                            reads=[("ps", bkN), "ident_bf"], writes=[("gY", wb, 0)])
                        for k in range(1, 6):
                            pp, pc = (k - 1) % 2, k % 2
                            Pp, Qp, Yp = Pm[wb][pp], Qm[wb][pp], Ym[wb][pp]
                            Pc, Qc, Yc = Pm[wb][pc], Qm[wb][pc], Ym[wb][pc]
                            if k <= 4:
                                bkP = C.bank()
                                for hh in range(HB):
                                    S.pe(lambda e, bkP=bkP, hh=hh, Qp=Qp, Pp=Pp: e.matmul(ps4(bkP)[:, hh, :], lhsT=Qp[:, hh, :], rhs=Pp[:, hh, :],
                                                                                         start=True, stop=True),
                                         reads=[("gQ", wb, pp), ("gP", wb, pp)], writes=[("ps", bkP)])
                            bkQ2 = C.bank()
                            for hh in range(HB):
                                S.pe(lambda e, bkQ2=bkQ2, hh=hh, Qp=Qp, Pp=Pp: e.matmul(ps4(bkQ2)[:, hh, :], lhsT=Pp[:, hh, :], rhs=Qp[:, hh, :],
                                                                                       start=True, stop=True),
                                     reads=[("gQ", wb, pp), ("gP", wb, pp)], writes=[("ps", bkQ2)])
                            if k <= 4:
                                C.evac(Pc, ps4(bkP), reads=[("ps", bkP)], writes=[("gP", wb, pc)])
                            C.evac(Qc, ps4(bkQ2), reads=[("ps", bkQ2)], writes=[("gQ", wb, pc)])
                            bkY = C.bank()
                            for hh in range(HB):
                                S.pe(lambda e, bkY=bkY, hh=hh, Qc=Qc, Yp=Yp: e.matmul(ps4(bkY)[:, hh, :], lhsT=Qc[:, hh, :], rhs=Yp[:, hh, :],
                                                                                     start=True, stop=True),
                                     reads=[("gQ", wb, pc), ("gY", wb, pp)], writes=[("ps", bkY)])
                            S.dve(lambda e, bkY=bkY, Yc=Yc, Yp=Yp: e.tensor_tensor(out=Yc, in0=ps4(bkY), in1=Yp, op=ALU.add),
                                  reads=[("ps", bkY), ("gY", wb, pp)], writes=[("gY", wb, pc)])
                        Y5 = Ym[wb][1]
                        for hh in range(HB):
                            sc_b = BETA[:, lt, hd0 + hh:hd0 + hh + 1]
                            sc_e = EGC[:, lt, hd0 + hh:hd0 + hh + 1]
                            sc_k = EKD[:, lt, hd0 + hh:hd0 + hh + 1]
                            S.act(lambda e, hh=hh, wb=wb, sc_b=sc_b, Y5=Y5: e.activation(out=TbT[wb][:, hh, :], in_=Y5[:, hh, :], func=AF.Copy, scale=sc_b),
                                  reads=[("gY", wb, 1), "BETA"], writes=[("gTbT", wb)])
                            S.pool(lambda e, hh=hh, wb=wb, sc_e=sc_e, bb=bb, tl=tl: e.tensor_scalar(
                                out=keg[wb][:, hh, :], in0=ktok[bb][:, hh, tl, :], scalar1=sc_e, scalar2=None, op0=ALU.mult),
                                reads=[("gktok", bb, hh), "EGC"], writes=[("gkeg", wb)])
                            S.pool(lambda e, hh=hh, wb=wb, sc_k=sc_k, bb=bb, tl=tl: e.tensor_scalar(
                                out=kdec[wb][:, hh, :], in0=ktok[bb][:, hh, tl, :], scalar1=sc_k, scalar2=None, op0=ALU.mult),
                                reads=[("gktok", bb, hh), "EKD"], writes=[("gkdec", wb)])
                        bkW = C.bank()
                        for hh in range(HB):
                            S.pe(lambda e, bkW=bkW, hh=hh, wb=wb: e.matmul(ps4(bkW)[:, hh, :], lhsT=keg[wb][:, hh, :], rhs=TbT[wb][:, hh, :],
                                                                           start=True, stop=True),
                                 reads=[("gkeg", wb), ("gTbT", wb)], writes=[("ps", bkW)])
                        S.act(lambda e, bkW=bkW, wb=wb: e.activation(out=negwT[wb], in_=ps4(bkW), func=AF.Copy, scale=-1.0),
                              reads=[("ps", bkW)], writes=[("gnegwT", wb)])
                        ob_i = tt % 2
                        if di == 1:
                            grow = g0 + tt * 128
                            S.dma(ofl[ob_i], of_s[grow:grow + 128, hg * HB * 128:(hg + 1) * HB * 128].rearrange("p (h c) -> p h c", h=HB),
                                  reads=[("of_d", si, hg, tt)], writes=[("gofl", ob_i)])
                            S.dma(zt[ob_i], ztok[grow:grow + 128, hg * HB * 128:(hg + 1) * HB * 128].rearrange("p (h c) -> p h c", h=HB),
                                  reads=[], writes=[("gzt", ob_i)])
                        rec_pi = S
                        S = C.S = Rec()
                        C.bank_pool = (5, 6, 7)
                        for j in ((0, 1) if di == 0 else (1, 0)):
                            pr = slice(64 * j, 64 * j + 64)
                            bkV = C.bank()
                            for hh in range(HB):
                                S.pe(lambda e, bkV=bkV, hh=hh, wb=wb, bb=bb, tl=tl: e.matmul(ps4(bkV)[:, hh, :], lhsT=TbT[wb][:, hh, :], rhs=vtok[bb][:, hh, tl, :],
                                                                                             start=True, stop=False),
                                     reads=[("gTbT", wb), ("gvtok", bb, hh)], writes=[("ps", bkV)])
                                S.pe(lambda e, bkV=bkV, hh=hh, wb=wb: e.matmul(ps4(bkV)[:, hh, :], lhsT=negwT[wb][:, hh, :], rhs=S16[:, hh, :],
                                                                               start=False, stop=True),
                                     reads=[("gnegwT", wb), "S16"], writes=[("ps", bkV)])
                            S.dve(lambda e, bkV=bkV, wb=wb, pr=pr: e.tensor_copy(out=vnew[wb][pr, :, :], in_=ps4(bkV)[pr, :, :]),
                                  reads=[("ps", bkV)], writes=[("gvnew", wb, j)])
                            bkO = C.bank()
                            for hh in range(HB):
                                S.pe(lambda e, bkO=bkO, hh=hh, bb=bb, ts_=ts_: e.matmul(ps4(bkO)[:, hh, :], lhsT=qdT[bb][:, hh, ts_], rhs=S16[:, hh, :],
                                                                                       start=True, stop=False),
                                     reads=[("gqdT", bb, hh), "S16"], writes=[("ps", bkO)])
                                S.pe(lambda e, bkO=bkO, hh=hh, wb=wb, pr=pr: e.matmul(ps4(bkO)[:, hh, :], lhsT=attnT[wb][pr, hh, :], rhs=vnew[wb][pr, hh, :],
                                                                                     start=False, stop=True),
                                     reads=[("gattnT", wb), ("gvnew", wb, j)], writes=[("ps", bkO)])
                            if di == 0:
                                S.act(lambda e, bkO=bkO, ob_i=ob_i, pr=pr: e.activation(out=ob[ob_i][pr, :, :], in_=ps4(bkO)[pr, :, :], func=AF.Copy),
                                      reads=[("ps", bkO)], writes=[("gob", ob_i, j)])
                            else:
                                S.dve(lambda e, bkO=bkO, ob_i=ob_i, pr=pr: e.tensor_tensor(out=o32[ob_i][pr, :, :], in0=ps4(bkO)[pr, :, :],
                                                                                          in1=ofl[ob_i][pr, :, :], op=ALU.add),
                                      reads=[("ps", bkO), ("gofl", ob_i)], writes=[("go32", ob_i, j)])
                            bkS = C.bank()
                            for hh in range(HB):
                                S.pe(lambda e, bkS=bkS, hh=hh, wb=wb, pr=pr: e.matmul(ps4(bkS)[:, hh, :], lhsT=kdec[wb][pr, hh, :], rhs=vnew[wb][pr, hh, :],
                                                                                     start=True, stop=True),
                                     reads=[("gkdec", wb), ("gvnew", wb, j)], writes=[("ps", bkS)])
                            for hh in range(HB):
                                S.dve(lambda e, bkS=bkS, hh=hh, lt=lt, j=j, hd0=hd0: e.scalar_tensor_tensor(
                                    out=S32[:, hh, :], in0=S32[:, hh, :], scalar=DEC[:, lt, j, hd0 + hh:hd0 + hh + 1], in1=ps4(bkS)[:, hh, :],
                                    op0=ALU.mult, op1=ALU.add), reads=[("ps", bkS), "S32", "DEC"], writes=["S32"])
                            S.act(lambda e: e.activation(out=S16, in_=S32, func=AF.Copy), reads=["S32"], writes=["S16"])
                        grow = g0 + tt * 128
                        if di == 0:
                            S.dma(of_s[grow:grow + 128, hg * HB * 128:(hg + 1) * HB * 128].rearrange("p (h c) -> p h c", h=HB), ob[ob_i],
                                  reads=[("gob", ob_i, 0), ("gob", ob_i, 1)], writes=[("of_d", si, hg, tt)])
                        else:
                            o_in = o32[ob_i]
                            okeys = [("go32", ob_i, 0), ("go32", ob_i, 1)]
                            for hh in range(HB):
                                S.act(lambda e, hh=hh, o_in=o_in, ob_i=ob_i: e.activation(out=junkg, in_=o_in[:, hh, :], func=AF.Square,
                                                                                         accum_out=ssq[ob_i][:, hh:hh + 1]),
                                      reads=okeys, writes=["gjunk", ("gssq", ob_i)])
                            S.dve(lambda e, ob_i=ob_i: e.tensor_scalar(out=ssq[ob_i], in0=ssq[ob_i], scalar1=1.0 / 128, scalar2=EPS,
                                                                       op0=ALU.mult, op1=ALU.add), reads=[("gssq", ob_i)], writes=[("gssq", ob_i)])
                            S.act(lambda e, ob_i=ob_i: e.activation(out=ssq[ob_i], in_=ssq[ob_i], func=AF.Sqrt),
                                  reads=[("gssq", ob_i)], writes=[("gssq", ob_i)])
                            S.dve(lambda e, ob_i=ob_i: e.reciprocal(out=ssq[ob_i], in_=ssq[ob_i]), reads=[("gssq", ob_i)], writes=[("gssq", ob_i)])
                            S.dve(lambda e, o_in=o_in, ob_i=ob_i: e.tensor_tensor(
                                out=o2, in0=o_in, in1=ssq[ob_i].unsqueeze(2).to_broadcast([128, HB, 128]), op=ALU.mult),
                                reads=okeys + [("gssq", ob_i)], writes=["go2"])
                            S.pool(lambda e: e.tensor_tensor(out=o2, in0=o2, in1=gnw.unsqueeze(1).to_broadcast([128, HB, 128]), op=ALU.mult),
                                   reads=["go2", "gnw"], writes=["go2"])
                            S.act(lambda e, ob_i=ob_i: e.activation(out=zt[ob_i], in_=zt[ob_i], func=AF.Silu),
                                  reads=[("gzt", ob_i)], writes=[("gzt", ob_i)])
                            S.dve(lambda e, ob_i=ob_i: e.tensor_tensor(out=ogt[ob_i], in0=o2, in1=zt[ob_i], op=ALU.mult),
                                  reads=["go2", ("gzt", ob_i)], writes=[("gogt", ob_i)])
                            S.dma(og[grow:grow + 128, hg * HB * 128:(hg + 1) * HB * 128].rearrange("p (h c) -> p h c", h=HB), ogt[ob_i],
                                  reads=[("gogt", ob_i)], writes=[("og_d", grow // 512, hg, tt)])
                        rec_rc = S
                        S = C.S = S_real
                        C.bank_pool = None
                        replay_merged(S_real, pending_rc, rec_pi)
                        pending_rc = rec_rc
                replay_merged(S_real, pending_rc, None)
                pending_rc = None
    C.barrier()


ATT_DBG = 0


def t5_bucket_np(rel):
    nb = 16
    max_exact = 8
    n = np.abs(rel)
    large = max_exact + (np.log(np.maximum(n, max_exact) / max_exact) / math.log(1024 / max_exact) * (nb - max_exact)).astype(np.int32)
    large = np.minimum(large, nb - 1)
    return (np.where(rel > 0, nb, 0) + np.where(n < max_exact, n, large)).astype(np.int32)


def host_att_consts():
    ohr = np.zeros((3, 33, 384), np.float32)
    for g, (win, d) in enumerate(GROUPS):
        for u in range(383):
            rel = (382 - u) - 191
            if abs(rel) <= 64:
                ohr[g, t5_bucket_np(np.array(rel * d)), u] = 1.0
            else:
                ohr[g, 32, u] = NEG
        ohr[g, 32, 383] = NEG
    return {"cohr": ohr.reshape(99, 384)}


def phase_att(C, qkT, vtok_a, ao_un, relb_src, zr_d, K_, dbg=None):
    S, sb = C.S, C.sb
    sb.reset()
    SEG = 4096
    if all(t % 4096 for t in C.seqs):
        SEG = min(C.seqs)
    assert all(t % SEG == 0 for t in C.seqs)
    ohr_d = C.inp("cohr", [99, 384], F32)
    tab = sb.alloc((12,), F32)
    S.dma(tab[0:32, :], relb_src, reads=[], writes=["atab"])
    TABB = sb.alloc((12, 128), F32)
    S.dve(lambda e: e.memset(TABB[32:33, :, :], 1.0), reads=[], writes=["aTABBo"])
    S.dve(lambda e: e.tensor_copy(out=TABB[0:32, :, :], in_=tab[0:32, :].unsqueeze(2).to_broadcast([32, 12, 128])),
          reads=["atab"], writes=["aTABB"])
    OHR = sb.alloc((3, 384), F32)
    S.dma(OHR[0:33, :, :], ohr_d.rearrange("(g b) u -> b g u", g=3), reads=[], writes=["aOHR"])
    BMT = sb.alloc((12, 2, 128), F32)
    frep = sb.alloc((384,), F32)
    for h in range(12):
        g = h // 4
        bk = C.bank()
        S.pe(lambda e, bk=bk, h=h, g=g: e.matmul(C.ps[bk][:, 0:384], lhsT=TABB[0:33, h, :], rhs=OHR[0:33, g, :], start=True, stop=True),
             reads=["aTABB", "aTABBo", "aOHR"], writes=[("ps", bk)])
        S.dve(lambda e, bk=bk: e.tensor_copy(out=frep, in_=C.ps[bk][:, 0:384]), reads=[("ps", bk)], writes=["afrep"])
        zr = zr_d[h]
        S.dma(zr.rearrange("(p u) -> p u", u=384), frep, reads=["afrep"], writes=[("azr", h)])
        for slot, off in ((0, 127), (1, 255)):
            src = bass.AP(zr.tensor, zr.offset + off, [[383, 128], [1, 128]])
            S.dma(BMT[:, h, slot, :], src, reads=[("azr", h)], writes=[("aBMT", h)])
    if dbg is not None:
        S.dma(dbg.rearrange("p (a b c) -> p a b c", a=12, b=2), BMT, reads=[("aBMT", h) for h in range(12)], writes=["dbg"])
        if dbg.shape[-1] == 3072:
            C.barrier()
            return
    PADM = 64 * 16
    qb = [sb.alloc((SEG,), BF16) for _ in range(2)]
    kb = [sb.alloc((PADM + SEG + PADM,), BF16) for _ in range(2)]
    qd = [sb.alloc((SEG,), BF16) for _ in range(2)]
    kd = [sb.alloc((PADM + SEG + PADM,), BF16) for _ in range(2)]
    NV = 3
    v4 = [sb.alloc((4, 65), BF16) for _ in range(NV)]
    for i in range(NV):
        S.dve(lambda e, i=i: e.memset(v4[i], 1.0), reads=[], writes=[("av4", i)])
    lg = [sb.alloc((256,), F32) for _ in range(3)]
    PT = [[sb.alloc((256,), BF16) for _ in range(4)] for _ in range(2)]
    ost = [sb.alloc((264,), BF16) for _ in range(2)]
    lrr = [0]
    vrr = [0]
    orr = [0]
    for si, T in enumerate(C.seqs):
        g0 = C.offs[si]
        for so in range(0, T, SEG):
            for g, (win, d) in enumerate(GROUPS):
                if (ATT_DBG & 4) and g > 0:
                    continue
                if (ATT_DBG & 32) and g != 2:
                    continue
                if (ATT_DBG & 8) and g != 1:
                    continue
                pad = 64 * d
                for pt in range(2):
                    rq = g * 256 + pt * 128
                    S.dma(qb[pt], qkT[rq:rq + 128, g0 + so:g0 + so + SEG], reads=[], writes=[("aqb", pt)])
                    lo, hi = so - pad, so + SEG + pad
                    dlo, dhi = max(lo, 0), min(hi, T)
                    if dlo > lo:
                        S.pool(lambda e, pt=pt, n=dlo - lo: e.memset(kb[pt][:, 0:n], 0.0), reads=[], writes=[("akb", pt)])
                    if dhi < hi:
                        S.pool(lambda e, pt=pt, a=dhi - lo, b=hi - lo: e.memset(kb[pt][:, a:b], 0.0), reads=[], writes=[("akb", pt)])
                    S.dma(kb[pt][:, dlo - lo:dhi - lo], qkT[768 + rq:768 + rq + 128, g0 + dlo:g0 + dhi], reads=[], writes=[("akb", pt)])
                Ls = SEG // d
                nq = Ls // 128
                assert nq >= 2
                if True:
                    for pt in range(2):
                        S.dve(lambda e, pt=pt, d=d: e.tensor_copy(
                            out=qd[pt][:, 0:SEG].rearrange("p (r s) -> p r s", r=d),
                            in_=qb[pt][:, 0:SEG].rearrange("p (s r) -> p r s", r=d)), reads=[("aqb", pt)], writes=[("aqd", pt)])
                        nk = SEG + 2 * pad
                        S.pool(lambda e, pt=pt, d=d, nk=nk: e.tensor_copy(
                            out=kd[pt][:, 0:nk].rearrange("p (r s) -> p r s", r=d),
                            in_=kb[pt][:, 0:nk].rearrange("p (s r) -> p r s", r=d)), reads=[("akb", pt)], writes=[("akd", pt)])
                L = T // d
                for r in range(d):
                    prev = None
                    for m in range(nq + 1):
                        if (ATT_DBG >> 8) and m >= (ATT_DBG >> 8):
                            break
                        gen = m % 2
                        sg0 = (so // d) + 128 * m - 64
                        jlo = 0 if sg0 >= 0 else 64
                        jhi = 128 if sg0 + 128 <= L else 64
                        vi = vrr[0] % NV
                        vrr[0] += 1
                        trow = g0 + (sg0 + jlo) * d + r
                        nrow = jhi - jlo
                        vsrc = bass.AP(vtok_a.tensor, vtok_a.offset + trow * 768 + g * 256, [[768 * d, nrow], [64, 4], [1, 64]])
                        if not (ATT_DBG & 1):
                            S.dma(v4[vi][jlo:jhi, :, 0:64], vsrc, reads=[], writes=[("av4", vi)])
                        qt_lo = max(m - 1, 0)
                        qt_hi = min(m, nq - 1)
                        nqc = (qt_hi - qt_lo + 1) * 128
                        bslot0 = 0 if m - 1 >= 0 else 1
                        kc0 = (128 * m - 64) * d + r + pad
                        qc0 = (128 * qt_lo) * d + r
                        for hh in range(4):
                            pt, prt = hh // 2, 64 * (hh % 2)
                            h = g * 4 + hh
                            bk = C.bank()
                            if False:
                                kop = kb[pt][prt:prt + 64, kc0:kc0 + 128]
                                qop = qb[pt][prt:prt + 64, qc0:qc0 + nqc]
                                rk = [("aqb", pt), ("akb", pt)]
                            else:
                                kop = kd[pt][prt:prt + 64, r * (Ls + 128) + 128 * m:r * (Ls + 128) + 128 * m + 128]
                                qop = qd[pt][prt:prt + 64, r * Ls + 128 * qt_lo:r * Ls + 128 * qt_lo + nqc]
                                rk = [("aqd", pt), ("akd", pt)]
                            S.pe(lambda e, bk=bk, kop=kop, qop=qop, nqc=nqc: e.matmul(
                                C.ps[bk][:, 0:nqc], lhsT=kop, rhs=qop, start=True, stop=True),
                                reads=rk, writes=[("ps", bk)])
                            li = lrr[0] % 3
                            lrr[0] += 1
                            bsl = BMT[:, h, bslot0:bslot0 + nqc // 128, :]
                            S.dve(lambda e, bk=bk, li=li, nqc=nqc, bsl=bsl: e.scalar_tensor_tensor(
                                out=lg[li][:, 0:nqc], in0=C.ps[bk][:, 0:nqc], scalar=0.125, in1=bsl.rearrange("p a b -> p (a b)"),
                                op0=ALU.mult, op1=ALU.add), reads=[("ps", bk), ("aBMT", h)], writes=[("alg", li)])
                            S.act(lambda e, li=li, nqc=nqc, gen=gen, hh=hh: e.activation(out=PT[gen][hh][:, 0:nqc], in_=lg[li][:, 0:nqc], func=AF.Exp),
                                  reads=[("alg", li)], writes=[("aPT", gen, hh)])
                        if m >= 1:
                            qi = m - 1
                            pjlo, pjhi, pvi = prev
                            oi = orr[0] % 2
                            orr[0] += 1
                            bkO = C.bank()
                            pc0 = 128 if qi >= 1 else 0
                            for hh in range(4):
                                oap = C.ps[bkO][:, hh * 128:hh * 128 + 65]
                                S.pe(lambda e, oap=oap, gen=gen, hh=hh, pc0=pc0, pjlo=pjlo, pjhi=pjhi, pvi=pvi: e.matmul(
                                    oap, lhsT=PT[1 - gen][hh][pjlo:pjhi, pc0:pc0 + 128], rhs=v4[pvi][pjlo:pjhi, hh, :], start=True, stop=False),
                                    reads=[("aPT", 1 - gen, hh), ("av4", pvi)], writes=[("ps", bkO)])
                                S.pe(lambda e, oap=oap, gen=gen, hh=hh, jlo=jlo, jhi=jhi, vi=vi: e.matmul(
                                    oap, lhsT=PT[gen][hh][jlo:jhi, 0:128], rhs=v4[vi][jlo:jhi, hh, :], start=False, stop=True),
                                    reads=[("aPT", gen, hh), ("av4", vi)], writes=[("ps", bkO)])
                            o4 = C.ps[bkO].rearrange("p (h c) -> p h c", h=4)
                            S.act(lambda e, o4=o4, oi=oi: e.activation(out=ost[oi][:, 0:256].rearrange("p (h c) -> p h c", h=4),
                                                                       in_=o4[:, :, 0:64], func=AF.Copy),
                                  reads=[("ps", bkO)], writes=[("aost", oi)])
                            S.dve(lambda e, o4=o4, oi=oi: e.tensor_copy(out=ost[oi][:, 256:260], in_=o4[:, :, 64]),
                                  reads=[("ps", bkO)], writes=[("adst", oi)])
                            S.dve(lambda e, o4=o4, oi=oi: e.tensor_tensor(out=ost[oi][:, 260:264], in0=o4[:, :, 64], in1=ost[oi][:, 256:260],
                                                                          op=ALU.subtract),
                                  reads=[("ps", bkO), ("adst", oi)], writes=[("adst2", oi)])
                            trow = g0 + so + (128 * qi) * d + r
                            odst = bass.AP(ao_un.tensor, ao_un.offset + trow * 792 + g * 264, [[792 * d, 128], [1, 264]])
                            if not (ATT_DBG & 2):
                                S.dma(odst, ost[oi], reads=[("aost", oi), ("adst", oi), ("adst2", oi)], writes=[("ao_d", trow, g)])
                        prev = (jlo, jhi, vi)
    C.barrier()


def phase_attnorm(C, ao_un, ao_n):
    S, sb = C.S, C.sb
    sb.reset()
    at = [sb.alloc((4, 792), BF16) for _ in range(2)]
    ao = [sb.alloc((4, 768), BF16) for _ in range(2)]
    tot = [sb.alloc((4, 4), F32) for _ in range(2)]
    nt = C.T // 512
    for i in range(nt):
        b = i % 2
        t0 = i * 512
        S.dma(at[b], ao_un[t0:t0 + 512, :].rearrange("(s p) c -> p s c", p=128), reads=[], writes=[("nat", b)])
        dvs = [at[b][:, :, g * 264 + 256 + 4 * k:g * 264 + 260 + 4 * k] for g in range(3) for k in range(2)]
        S.dve(lambda e, b=b, dvs=dvs: e.tensor_tensor(out=tot[b], in0=dvs[0], in1=dvs[1], op=ALU.add),
              reads=[("nat", b)], writes=[("ntot", b)])
        for kk in range(2, 6):
            S.dve(lambda e, b=b, dvs=dvs, kk=kk: e.tensor_tensor(out=tot[b], in0=tot[b], in1=dvs[kk], op=ALU.add),
                  reads=[("nat", b), ("ntot", b)], writes=[("ntot", b)])
        S.dve(lambda e, b=b: e.reciprocal(out=tot[b], in_=tot[b]), reads=[("ntot", b)], writes=[("ntot", b)])
        for s_ in range(4):
            for g in range(3):
                eng = S.dve if (s_ * 3 + g) % 2 == 0 else S.pool
                eng(lambda e, b=b, s_=s_, g=g: e.tensor_tensor(
                    out=ao[b][:, s_, g * 256:(g + 1) * 256].rearrange("p (j c) -> p j c", j=4),
                    in0=at[b][:, s_, g * 264:g * 264 + 256].rearrange("p (j c) -> p j c", j=4),
                    in1=tot[b][:, s_, :].unsqueeze(2).to_broadcast([128, 4, 64]), op=ALU.mult),
                    reads=[("nat", b), ("ntot", b)], writes=[("nao", b, s_, g)])
        S.dma(ao_n[t0:t0 + 512, :].rearrange("(s p) c -> p s c", p=128), ao[b],
              reads=[("nao", b, s_, g) for s_ in range(4) for g in range(3)], writes=[("aon_d", i)])
    C.barrier()


def host_consts():
    idx = np.arange(128)
    same = (idx[:, None] // 64) == (idx[None, :] // 64)
    UF = (same & (idx[:, None] <= idx[None, :])).astype(np.float32)
    UB = (same & (idx[:, None] >= idx[None, :])).astype(np.float32)
    CH0 = np.repeat((idx < 64).astype(np.float32)[:, None], 128, 1)
    CH1 = np.repeat((idx >= 64).astype(np.float32)[:, None], 128, 1)
    MA_f = np.where(same & (idx[None, :] < idx[:, None]), 0.0, NEG).astype(np.float32)
    MA_b = np.where(same & (idx[None, :] > idx[:, None]), 0.0, NEG).astype(np.float32)
    SEL = np.zeros((16, 16, 128), np.float32)
    for h in range(16):
        SEL[h, h, :] = 1.0
    c32 = np.concatenate([np.eye(128, dtype=np.float32), UF, UB, CH0, CH1], axis=1)
    cbf = np.concatenate([np.eye(128, dtype=np.float32), np.ones((128, 128), np.float32),
                          -np.ones((128, 128), np.float32), MA_f, MA_b], axis=1).astype(ml_dtypes.bfloat16)
    return {"c32": c32, "cbf": cbf, "csel": SEL.reshape(16, 2048)}


def load_consts(C):
    S, sb = C.S, C.sb
    c32_d = C.inp("c32", [128, 640], F32)
    cbf_d = C.inp("cbf", [128, 640], BF16)
    c32 = sb.alloc((5, 128), F32)
    cbf = sb.alloc((5, 128), BF16)
    S.dma(c32, c32_d.rearrange("p (a b) -> p a b", a=5), reads=[], writes=["c32", "ident32"])
    S.dma(cbf, cbf_d.rearrange("p (a b) -> p a b", a=5), reads=[], writes=["cbf", "ident_bf"])
    K_ = {"ident32": c32[:, 0, :], "UF32": c32[:, 1, :], "UB32": c32[:, 2, :], "CH0": c32[:, 3, :], "CH1": c32[:, 4, :],
          "ident_bf": cbf[:, 0, :], "ones_bf": cbf[:, 1, :], "negones_bf": cbf[:, 2, :], "MA_f": cbf[:, 3, :],
          "MA_b": cbf[:, 4, :], "keys": ["c32", "cbf"]}
    sb.set_mark()
    return K_


W_NAMES = ["norm_mix", "norm_ffn", "norm_final", "gdn_w_in", "gdn_conv", "gdn_a_log", "gdn_dt_bias", "gdn_norm",
           "gdn_w_out", "att_w_in", "att_w_out", "rel_bias", "ffn_w_up", "ffn_conv", "ffn_conv_b", "ffn_w_down"]

WIN = 6144


def declare_weights(C):
    W = {}
    W["norm_mix"] = C.inp("norm_mix", [2, D]); W["norm_ffn"] = C.inp("norm_ffn", [2, D]); W["norm_final"] = C.inp("norm_final", [1, D])
    W["gdn_w_in"] = C.inp("gdn_w_in", [D, GDN_IN]); W["gdn_conv"] = C.inp("gdn_conv", [5, 3072])
    W["gdn_a_log"] = C.inp("gdn_a_log", [1, 16]); W["gdn_dt_bias"] = C.inp("gdn_dt_bias", [1, 16]); W["gdn_norm"] = C.inp("gdn_norm", [1, 128])
    W["gdn_w_out"] = C.inp("gdn_w_out", [D, D]); W["att_w_in"] = C.inp("att_w_in", [D, 2304]); W["att_w_out"] = C.inp("att_w_out", [768, D])
    W["rel_bias"] = C.inp("rel_bias", [32, 12]); W["ffn_w_up"] = C.inp("ffn_w_up", [2, D, 2 * D_FF]); W["ffn_conv"] = C.inp("ffn_conv", [2, 3, 2 * D_FF])
    W["ffn_conv_b"] = C.inp("ffn_conv_b", [2, 2 * D_FF]); W["ffn_w_down"] = C.inp("ffn_w_down", [2, D_FF, D])
    return W


def chain_gdn_layer(C, W, K_, x_in, og, pfx):
    T = C.T
    xn0 = C.scratch(pfx + "xn0", [T, D], BF16)
    projT = C.scratch(pfx + "projT", [3072, T], BF16)
    ztok = C.scratch(pfx + "ztok", [T, 1024], BF16)
    gates = C.scratch(pfx + "gates", [T, 32], F32)
    of_s = C.scratch(pfx + "of_s", [T, 1024], BF16)
    phase_norm0(C, x_in, W["norm_mix"][0:1, :], xn0)
    phase_proj(C, xn0, W["gdn_w_in"], GDN_IN, fm=[(0, 3072, projT)],
               tm=[(3072, 512, ztok[:, 0:512], BF16), (3584, 512, ztok[:, 512:1024], BF16), (4096, 32, gates, F32)], tag=pfx + "gp")
    phase_gdn(C, projT, ztok, gates, of_s, og, K_, W["gdn_conv"], W["gdn_a_log"], W["gdn_dt_bias"], W["gdn_norm"])


def chain_rest(C, W, K_, x_in, og, y_out, pfx):
    S = C.S
    T, TP = C.T, C.TP
    xrp = C.scratch(pfx + "xrp", [TP, D], F32)
    xnp = C.scratch(pfx + "xnp", [TP, D], BF16)
    x1 = C.scratch(pfx + "x1", [T, D], F32)
    xn1 = C.scratch(pfx + "xn1", [T, D], BF16)
    qkT = C.scratch(pfx + "qkT", [1536, T], BF16)
    vtok = C.scratch(pfx + "vtok", [T, 768], BF16)
    ao_un = C.scratch(pfx + "ao_un", [T, 792], BF16)
    ao_n = C.scratch(pfx + "ao_n", [T, 768], BF16)
    zr = C.scratch(pfx + "zr", [12, 128 * 384], F32)
    phase_outproj(C, og, D, W["gdn_w_out"], x_in, W["norm_ffn"][0:1, :], xrp, xnp, pfx + "op0", "og_d")

    def outs0(si, t_lo, n, xt_ap, xo_ap, kx, ko):
        g = C.offs[si] + t_lo
        S.dma(x1[g:g + n, :], xt_ap, reads=kx, writes=[("x1d", g)])
        S.dma(xn1[g:g + n, :], xo_ap, reads=ko, writes=[("xn1d", g)])
    phase_ffn(C, xnp, xrp, W["ffn_w_up"][0], W["ffn_w_down"][0], W["ffn_conv"][0], W["ffn_conv_b"][0], W["norm_mix"][1:2, :],
              K_["ident32"], outs0, pfx + "f0", final=False)
    phase_proj(C, xn1, W["att_w_in"], 2304, fm=[(0, 1536, qkT)],
               tm=[(1536, 512, vtok[:, 0:512], BF16), (2048, 256, vtok[:, 512:768], BF16)], tag=pfx + "ap")
    phase_att(C, qkT, vtok, ao_un, W["rel_bias"], zr, K_)
    phase_attnorm(C, ao_un, ao_n)
    phase_outproj(C, ao_n, 768, W["att_w_out"], x1, W["norm_ffn"][1:2, :], xrp, xnp, pfx + "op1", "aon_d")

    def outs1(si, t_lo, n, xt_ap, xo_ap, kx, ko):
        g = C.offs[si] + t_lo
        S.dma(y_out[g:g + n, :], xo_ap, reads=ko, writes=[("yd", g)])
    phase_ffn(C, xnp, xrp, W["ffn_w_up"][1], W["ffn_w_down"][1], W["ffn_conv"][1], W["ffn_conv_b"][1], W["norm_final"],
              K_["ident32"], outs1, pfx + "f1", final=True)


def build_program(seqs, debug=False):
    import contextlib
    nc = bass.Bass("TRN2", target_bir_lowering=False)
    C = Ctx(nc, seqs, debug=debug)
    x_all = C.inp("x_all", [C.T, D])
    W = declare_weights(C)
    y_all = C.scratch("y_all", [C.T, D], F32, out=True)
    og = C.scratch("og", [C.T, 1024], BF16)
    K_ = load_consts(C)
    chain_gdn_layer(C, W, K_, x_all, og, "")
    chain_rest(C, W, K_, x_all, og, y_all, "")
    stack = contextlib.ExitStack()
    C.S.emit(stack)
    return nc, C, stack


def build_program_A(sample_seqs, Tp, debug=False):
    import contextlib
    nc = bass.Bass("TRN2", target_bir_lowering=False)
    C = Ctx(nc, sample_seqs, debug=debug)
    x_all = C.inp("x_all", [C.T, D])
    W = declare_weights(C)
    y_all = C.scratch("y_all", [C.T, D], F32, out=True)
    og = C.scratch("og", [C.T, 1024], BF16)
    K_ = load_consts(C)
    Cp = C.with_seqs([Tp])
    x_p = C.inp("x_p", [Tp, D])
    w_h = C.inp("gdn_w_in_h", [D, 516]); conv_h = C.inp("gdn_conv_h", [5, 384])
    alog_h = C.inp("gdn_a_log_h", [1, 2]); dtb_h = C.inp("gdn_dt_bias_h", [1, 2])
    og_p = C.scratch("og_p", [Tp, 128], BF16, out=True)
    xn0p = C.scratch("p_xn0", [Tp, D], BF16)
    projTp = C.scratch("p_projT", [384, Tp], BF16)
    ztokp = C.scratch("p_ztok", [Tp, 128], BF16)
    gatesp = C.scratch("p_gates", [Tp, 4], F32)
    ofp = C.scratch("p_of", [Tp, 128], BF16)
    phase_norm0(Cp, x_p, W["norm_mix"][0:1, :], xn0p)
    phase_proj(Cp, xn0p, w_h, 516, fm=[(0, 384, projTp)], tm=[(384, 128, ztokp, BF16), (512, 4, gatesp, F32)], tag="pgp")
    phase_gdn(Cp, projTp, ztokp, gatesp, ofp, og_p, K_, conv_h, alog_h, dtb_h, W["gdn_norm"], NH=1, HB=1)
    chain_gdn_layer(C, W, K_, x_all, og, "")
    chain_rest(C, W, K_, x_all, og, y_all, "")
    stack = contextlib.ExitStack()
    C.S.emit(stack)
    return nc, C, stack


def build_program_B(win, debug=False):
    import contextlib
    nc = bass.Bass("TRN2", target_bir_lowering=False)
    C = Ctx(nc, [win], debug=debug)
    x_w = C.inp("x_w", [win, D])
    og_w = C.inp("og_w", [win, 1024], BF16)
    W = declare_weights(C)
    y_w = C.scratch("y_w", [win, D], F32, out=True)
    K_ = load_consts(C)
    chain_rest(C, W, K_, x_w, og_w, y_w, "")
    stack = contextlib.ExitStack()
    C.S.emit(stack)
    return nc, C, stack


def weight_map(w):
    m = {}
    m["norm_mix"] = np.asarray(w["norm_mix"], np.float32)
    m["norm_ffn"] = np.asarray(w["norm_ffn"], np.float32)
    m["norm_final"] = np.asarray(w["norm_final"], np.float32).reshape(1, D)
    m["gdn_w_in"] = np.asarray(w["gdn_w_in"], np.float32)[0]
    m["gdn_conv"] = np.asarray(w["gdn_conv"], np.float32)[0]
    m["gdn_a_log"] = np.asarray(w["gdn_a_log"], np.float32)[0].reshape(1, 16)
    m["gdn_dt_bias"] = np.asarray(w["gdn_dt_bias"], np.float32)[0].reshape(1, 16)
    m["gdn_norm"] = np.asarray(w["gdn_norm"], np.float32)[0].reshape(1, 128)
    m["gdn_w_out"] = np.asarray(w["gdn_w_out"], np.float32)[0]
    m["att_w_in"] = np.asarray(w["att_w_in"], np.float32)[0]
    m["att_w_out"] = np.asarray(w["att_w_out"], np.float32)[0]
    m["rel_bias"] = np.asarray(w["rel_bias"], np.float32)
    m["ffn_w_up"] = np.asarray(w["ffn_w_up"], np.float32)
    m["ffn_conv"] = np.asarray(w["ffn_conv"], np.float32)
    m["ffn_conv_b"] = np.asarray(w["ffn_conv_b"], np.float32)
    m["ffn_w_down"] = np.asarray(w["ffn_w_down"], np.float32)
    m.update(host_consts())
    m.update(host_att_consts())
    return m


def make_in_map(x_rows, w):
    m = weight_map(w)
    m["x_all"] = np.ascontiguousarray(x_rows, dtype=np.float32)
    return m


def head_slices(wm, h):
    w_in = wm["gdn_w_in"]
    cols = ([h * 128 + i for i in range(128)] + [1024 + h * 128 + i for i in range(128)] + [2048 + h * 128 + i for i in range(128)]
            + [3072 + h * 128 + i for i in range(128)] + [4096 + k * 8 + h for k in range(4)])
    ccols = [h * 128 + i for i in range(128)] + [1024 + h * 128 + i for i in range(128)] + [2048 + h * 128 + i for i in range(128)]
    return {"gdn_w_in_h": np.ascontiguousarray(w_in[:, cols]),
            "gdn_conv_h": np.ascontiguousarray(wm["gdn_conv"][:, ccols]),
            "gdn_a_log_h": np.ascontiguousarray(wm["gdn_a_log"][:, [h, 8 + h]]),
            "gdn_dt_bias_h": np.ascontiguousarray(wm["gdn_dt_bias"][:, [h, 8 + h]])}


def kernel(**inputs):
    x_prompt = np.asarray(inputs["x_prompt"], np.float32)
    x_sample = np.asarray(inputs["x_sample"], np.float32)
    ncore = 8
    ns = x_sample.shape[0] // ncore
    Ts = x_sample.shape[1]
    Tp = x_prompt.shape[1]
    wm = weight_map(inputs)
    nc, C, stack = build_program_A([Ts] * ns, Tp)
    with stack:
        in_maps = []
        for c in range(ncore):
            m = dict(wm)
            m["x_all"] = np.ascontiguousarray(x_sample[c * ns:(c + 1) * ns].reshape(ns * Ts, D))
            m["x_p"] = x_prompt[0]
            m.update(head_slices(wm, c))
            in_maps.append({k: m[k] for k in C.inputs})
        resA = run_bass_kernel_spmd(nc, in_maps, core_ids=list(range(ncore)))
    y_sample = np.empty_like(x_sample)
    og_full = np.empty((Tp, 1024), dtype=ml_dtypes.bfloat16)
    for c in range(ncore):
        y_sample[c * ns:(c + 1) * ns] = resA.results[c]["y_all"].reshape(ns, Ts, D)
        og_full[:, c * 128:(c + 1) * 128] = resA.results[c]["og_p"]
    share = Tp // ncore
    nc2, C2, stack2 = build_program_B(WIN)
    starts = [min(max(c * share - (WIN - share) // 2, 0), Tp - WIN) for c in range(ncore)]
    with stack2:
        in_maps = []
        for c in range(ncore):
            m = dict(wm)
            m["x_w"] = np.ascontiguousarray(x_prompt[0, starts[c]:starts[c] + WIN])
            m["og_w"] = np.ascontiguousarray(og_full[starts[c]:starts[c] + WIN])
            in_maps.append({k: m[k] for k in C2.inputs})
        resB = run_bass_kernel_spmd(nc2, in_maps, core_ids=list(range(ncore)))
    y_prompt = np.empty_like(x_prompt)
    for c in range(ncore):
        o = c * share - starts[c]
        y_prompt[0, c * share:(c + 1) * share] = resB.results[c]["y_w"][o:o + share]
    return (y_prompt, y_sample)
```

```python
import math
import numpy as np
import ml_dtypes
import concourse.bass as bass
import concourse.mybir as mybir
from concourse.bass_utils import run_bass_kernel_spmd

F32 = mybir.dt.float32
BF16 = mybir.dt.bfloat16
AF = mybir.ActivationFunctionType
ALU = mybir.AluOpType
AX = mybir.AxisListType

D = 1024
EPS = 1e-6
GDN_H = 8
GDN_IN = 4128
D_FF = 2816
ATT_H = 12
ATT_DH = 64
ATT_W = 768
GROUPS = ((128, 1), (512, 4), (2048, 16))
NEG = -30000.0


class Sched:
    ENGS = ("pe", "dve", "act", "pool", "sp")
    EPOCH = 20000
    FUSE_WAIT = True
    NDMA = {"sp": 24, "pool": 12, "act": 6}

    def __init__(self, nc):
        self.nc = nc
        self.ops = []
        self.res_w = {}
        self.res_r = {}

    def add(self, eng, fn, reads=(), writes=(), dma=False):
        deps = set()
        for k in reads:
            w = self.res_w.get(k)
            if w is not None:
                deps.add(w)
            if isinstance(k, tuple) and k[0] == "ps":
                for r in self.res_r.get(k, ()):
                    if self.ops[r][0] != eng:
                        deps.add(r)
        for k in writes:
            w = self.res_w.get(k)
            if w is not None:
                deps.add(w)
            for r in self.res_r.get(k, ()):
                deps.add(r)
        oid = len(self.ops)
        deps.discard(oid)
        self.ops.append([eng, fn, deps, dma])
        for k in reads:
            lst = self.res_r.setdefault(k, [])
            if not dma:
                lst[:] = [r for r in lst if self.ops[r][3] or self.ops[r][0] != eng]
            lst.append(oid)
        for k in writes:
            self.res_w[k] = oid
            self.res_r[k] = []
        return oid

    def pe(self, fn, reads=(), writes=()):
        return self.add("pe", fn, reads, writes)

    def dve(self, fn, reads=(), writes=()):
        return self.add("dve", fn, reads, writes)

    def act(self, fn, reads=(), writes=()):
        return self.add("act", fn, reads, writes)

    def pool(self, fn, reads=(), writes=()):
        return self.add("pool", fn, reads, writes)

    def dma(self, out, in_, reads=(), writes=(), eng="sp", transpose=False):
        if transpose:
            fn = lambda e: e.dma_start_transpose(out=out, in_=in_)
        else:
            fn = lambda e: e.dma_start(out=out, in_=in_)
        return self.add(eng, fn, reads, writes, dma=True)

    def emit(self, stack):
        nc = self.nc
        ops = self.ops
        n = len(ops)
        flagged = [False] * n
        for i, (eng, fn, deps, dma) in enumerate(ops):
            for d in deps:
                de, _, _, ddma = ops[d]
                if ddma:
                    continue
                if de == "pe" and eng == "pe" and not dma:
                    continue
                flagged[d] = True
        cnt = {e: 0 for e in self.ENGS}
        fidx = [None] * n
        for i, (eng, fn, deps, dma) in enumerate(ops):
            if flagged[i] and not dma:
                fidx[i] = cnt[eng]
                cnt[eng] += 1
        self.flag_counts = dict(cnt)
        csems = {}
        for e in self.ENGS:
            ne = cnt[e] // self.EPOCH + 1
            csems[e] = [stack.enter_context(nc.semaphore(f"c_{e}_{j}")) for j in range(ne)]
        dcnt = {e: 0 for e in self.ENGS}
        dinfo = [None] * n
        for i, (eng, fn, deps, dma) in enumerate(ops):
            if dma:
                dinfo[i] = (eng, dcnt[eng])
                dcnt[eng] += 1
        dsems = {}
        for e in self.ENGS:
            if dcnt[e]:
                nd = self.NDMA.get(e, 8)
                dsems[e] = [stack.enter_context(nc.semaphore(f"d_{e}_{j}")) for j in range(nd)]

        def dma_target(i):
            e, j = dinfo[i]
            nd = len(dsems[e])
            return dsems[e][j % nd], 16 * (j // nd + 1), (e, j % nd)

        per_eng = {e: [] for e in self.ENGS}
        for i, op in enumerate(ops):
            per_eng[op[0]].append(i)

        block = stack.enter_context(nc.Block())
        EPOCH = self.EPOCH

        def run_engine(ename, engobj):
            waited_c = {}
            waited_d = {}
            for i in per_eng[ename]:
                eng, fn, deps, dma = ops[i]
                need_c = {}
                need_d = {}
                for d in deps:
                    de, _, _, ddma = ops[d]
                    if ddma:
                        sem, val, key = dma_target(d)
                        if waited_d.get(key, 0) < val and need_d.get(key, (None, 0))[1] < val:
                            need_d[key] = (sem, val)
                    else:
                        if de == "pe" and eng == "pe" and not dma:
                            continue
                        fi = fidx[d]
                        if waited_c.get(de, -1) < fi and need_c.get(de, -1) < fi:
                            need_c[de] = fi
                if dma:
                    e, j = dinfo[i]
                    nd = len(dsems[e])
                    if j >= nd:
                        key = (e, j % nd)
                        val = 16 * (j // nd)
                        if waited_d.get(key, 0) < val and need_d.get(key, (None, 0))[1] < val:
                            need_d[key] = (dsems[e][j % nd], val)
                wl = []
                for de, fi in need_c.items():
                    wl.append((csems[de][fi // EPOCH], fi % EPOCH + 1))
                    waited_c[de] = fi
                for key, (sem, val) in need_d.items():
                    wl.append((sem, val))
                    waited_d[key] = val
                fuse = None
                if wl and not dma and self.FUSE_WAIT:
                    fuse = wl.pop()
                for sem, val in wl:
                    engobj.wait_ge(sem, val)
                ins = fn(engobj)
                if fuse is not None:
                    ins._wait_ge(fuse[0], fuse[1])
                if dma:
                    sem, val, key = dma_target(i)
                    ins.then_inc(sem, 16)
                elif flagged[i]:
                    fi = fidx[i]
                    ins.then_inc(csems[eng][fi // EPOCH], 1)
            if ename == "sp":
                for e in self.ENGS:
                    if dcnt[e]:
                        nd = len(dsems[e])
                        for slot in range(min(nd, dcnt[e])):
                            last_j = ((dcnt[e] - 1 - slot) // nd) * nd + slot
                            val = 16 * (last_j // nd + 1)
                            if waited_d.get((e, slot), 0) < val:
                                engobj.wait_ge(dsems[e][slot], val)

        @block.tensor
        def _(e):
            run_engine("pe", e)

        @block.vector
        def _(e):
            run_engine("dve", e)

        @block.scalar
        def _(e):
            run_engine("act", e)

        @block.gpsimd
        def _(e):
            run_engine("pool", e)

        @block.sync
        def _(e):
            run_engine("sp", e)


class Rec(Sched):
    def __init__(self):
        self.calls = []

    def add(self, eng, fn, reads=(), writes=(), dma=False):
        self.calls.append((eng, fn, list(reads), list(writes), dma))


def replay_merged(S, *streams):
    lists = []
    for st in streams:
        if st is None:
            continue
        if isinstance(st, (list, tuple)):
            calls = [c for r in st for c in r.calls]
        else:
            calls = st.calls
        if calls:
            lists.append(calls)
    pos = [0] * len(lists)
    total = sum(len(l) for l in lists)
    for _ in range(total):
        best, bestf = None, None
        for k, l in enumerate(lists):
            if pos[k] < len(l):
                f = pos[k] / len(l)
                if best is None or f < bestf:
                    best, bestf = k, f
        S.add(*lists[best][pos[best]])
        pos[best] += 1


class SBAlloc:
    def __init__(self, nc, nbytes=212480):
        self.t = nc.alloc_sbuf_tensor("sb_all", [128, nbytes // 2], BF16)
        self.nbytes = nbytes
        self.off = 0
        self.mark = 0

    def alloc(self, free_shape, dtype):
        esz = 4 if dtype == F32 else 2
        nel = int(np.prod(free_shape))
        nb = nel * esz
        self.off = (self.off + 63) // 64 * 64
        assert self.off + nb <= self.nbytes, f"SBUF overflow {self.off + nb}"
        v = self.t[:, self.off // 2:(self.off + nb) // 2]
        if dtype == F32:
            v = v.bitcast(F32)
        self.off += nb
        if len(free_shape) == 2:
            v = v.rearrange("p (a b) -> p a b", a=free_shape[0])
        elif len(free_shape) == 3:
            v = v.rearrange("p (a b c) -> p a b c", a=free_shape[0], b=free_shape[1])
        return v

    def set_mark(self):
        self.mark = self.off

    def reset(self):
        self.off = self.mark


class Ctx:
    def __init__(self, nc, seqs, debug=False):
        self.nc = nc
        self.S = Sched(nc)
        self.seqs = list(seqs)
        self.offs = [int(x) for x in np.cumsum([0] + self.seqs[:-1])]
        self.T = int(sum(self.seqs))
        self.debug = debug
        self.sb = SBAlloc(nc)
        self.ps = [nc.alloc_psum_tensor(f"psb{i}", [128, 512], F32).ap() for i in range(8)]
        self.ps_rr = 0
        self.ev_rr = 0
        self.dram = {}
        self.inputs = {}
        self.nw = [-(-t // 254) for t in self.seqs]
        self.pstride = [254 * n + 2 for n in self.nw]
        self.poffs = [int(x) for x in np.cumsum([0] + self.pstride[:-1])]
        self.TP = int(sum(self.pstride))
        self.bar_deps = {}

    def with_seqs(self, seqs):
        import copy
        c2 = copy.copy(self)
        c2.seqs = list(seqs)
        c2.offs = [int(x) for x in np.cumsum([0] + c2.seqs[:-1])]
        c2.T = int(sum(c2.seqs))
        c2.nw = [-(-t // 254) for t in c2.seqs]
        c2.pstride = [254 * n + 2 for n in c2.nw]
        c2.poffs = [int(x) for x in np.cumsum([0] + c2.pstride[:-1])]
        c2.TP = int(sum(c2.pstride))
        return c2

    def inp(self, name, shape, dtype=F32):
        if name in self.inputs:
            return self.inputs[name]
        t = self.nc.dram_tensor(name, list(shape), dtype, kind="ExternalInput").ap()
        self.inputs[name] = t
        return t

    def scratch(self, name, shape, dtype, out=False):
        kind = "ExternalOutput" if (out or self.debug) else "Internal"
        t = self.nc.dram_tensor(name, list(shape), dtype, kind=kind).ap()
        self.dram[name] = t
        return t

    bank_pool = None

    def bank(self):
        pool = self.bank_pool or (0, 1, 2, 3, 4, 5, 6, 7)
        i = pool[self.ps_rr % len(pool)]
        self.ps_rr += 1
        return i

    def evac(self, out, in_, reads, writes, scale=None):
        S = self.S
        self.ev_rr += 1
        if self.ev_rr % 2 == 0:
            if scale is None:
                S.act(lambda e: e.activation(out=out, in_=in_, func=AF.Copy), reads, writes)
            else:
                S.act(lambda e: e.activation(out=out, in_=in_, func=AF.Copy, scale=scale), reads, writes)
        else:
            if scale is None:
                S.dve(lambda e: e.tensor_copy(out=out, in_=in_), reads, writes)
            else:
                S.dve(lambda e: e.tensor_scalar(out=out, in0=in_, scalar1=scale, scalar2=None, op0=ALU.mult),
                      reads, writes)

    def barrier(self):
        S = self.S
        last = {}
        dmas = []
        for i, op in enumerate(S.ops):
            if op[3]:
                dmas.append(i)
            else:
                last[op[0]] = i
        deps = set(last.values())
        for e in ("sp", "pool", "act"):
            de = [i for i in dmas if S.ops[i][0] == e]
            deps |= set(de[-S.NDMA.get(e, 8):])
        for e in ("pe", "dve", "act", "pool"):
            S.ops.append([e, (lambda en: en.nop()) if e != "pe" else (lambda en: en.nop()), set(deps), False])
        S.ops.append(["sp", lambda en: en.nop(), set(deps), False])
        S.res_w = {}
        S.res_r = {}


def load_w_bf16(C, dst, src, key, kt):
    for k in range(kt):
        C.S.dma(dst[:, k, :], src[k * 128:(k + 1) * 128, :], reads=(), writes=[(key, k)], eng="pool")


def load_bcast(C, dst, src_row, key):
    n = src_row.shape[-1]
    C.S.dma(dst, src_row.to_broadcast([128, n]), reads=(), writes=[key])


def norm_tile(C, x_t, xkeys, wbc, wkey, out_t, okeys, ss, sskey, junk, jkey, ns, func_out=None):
    S = C.S
    for s in range(ns):
        S.act(lambda e, s=s: e.activation(out=junk, in_=x_t[:, s, :], func=AF.Square, accum_out=ss[:, s:s + 1]),
              reads=[xkeys[s]], writes=[jkey, (sskey, s)])
    sk = [(sskey, s) for s in range(ns)]
    S.dve(lambda e: e.tensor_scalar(out=ss[:, 0:ns], in0=ss[:, 0:ns], scalar1=1.0 / D, scalar2=EPS,
                                    op0=ALU.mult, op1=ALU.add), reads=sk, writes=sk)
    S.act(lambda e: e.activation(out=ss[:, 0:ns], in_=ss[:, 0:ns], func=AF.Sqrt), reads=sk, writes=sk)
    S.dve(lambda e: e.reciprocal(out=ss[:, 0:ns], in_=ss[:, 0:ns]), reads=sk, writes=sk)
    for s in range(ns):
        S.dve(lambda e, s=s: e.scalar_tensor_tensor(out=out_t[:, s, :], in0=x_t[:, s, :], scalar=ss[:, s:s + 1],
                                                    in1=wbc, op0=ALU.mult, op1=ALU.mult),
              reads=[xkeys[s], (sskey, s), wkey], writes=[okeys[s]])


def phase_norm0(C, x_src, w_row, xn_dst):
    S, sb = C.S, C.sb
    sb.reset()
    wbc = sb.alloc((D,), F32)
    load_bcast(C, wbc, w_row, "n0w")
    xt = [sb.alloc((4, D), F32) for _ in range(2)]
    xn = [sb.alloc((4, D), BF16) for _ in range(2)]
    ss = [sb.alloc((4,), F32) for _ in range(2)]
    junk = sb.alloc((D,), BF16)
    nt = C.T // 512
    xs = x_src.rearrange("(n s p) d -> n p s d", s=4, p=128)
    xd = xn_dst.rearrange("(n s p) d -> n p s d", s=4, p=128)
    for i in range(nt):
        b = i % 2
        S.dma(xt[b], xs[i], reads=[], writes=[("n0x", b)])
        norm_tile(C, xt[b], [("n0x", b)] * 4, wbc, "n0w", xn[b], [("n0o", b, s) for s in range(4)], ss[b],
                  ("n0ss", b), junk, "n0j", 4)
        S.dma(xd[i], xn[b], reads=[("n0o", b, s) for s in range(4)], writes=[("xn_d", i)])
    C.barrier()


def phase_proj(C, xn_src, W_src, nout, fm, tm, tag):
    S, sb = C.S, C.sb
    sb.reset()
    W = sb.alloc((8, nout), BF16)
    load_w_bf16(C, W, W_src, tag + "W", 8)
    wkeys = [(tag + "W", k) for k in range(8)]
    xnT = [sb.alloc((8, 512), BF16) for _ in range(2)]
    st_fm = [sb.alloc((4, 512), BF16) for _ in range(2)]
    st_tm = {}
    for j, (c0, n, dst, dt) in enumerate(tm):
        st_tm[j] = [sb.alloc((4, n), dt) for _ in range(2)]
    nt = C.T // 512
    fm_rr = 0
    for i in range(nt):
        b = i % 2
        t0 = i * 512
        for k in range(8):
            S.dma(xnT[b][:, k, :], xn_src[t0:t0 + 512, k * 128:(k + 1) * 128], reads=[("xn_d", i)],
                  writes=[(tag + "xT", b, k)], transpose=True)
        xkeys = [(tag + "xT", b, k) for k in range(8)]
        for (c0, n, dst) in fm:
            nm = n // 128
            for m0 in range(0, nm, 4):
                sbuf = fm_rr % 2
                fm_rr += 1
                mm = min(4, nm - m0)
                for j in range(mm):
                    m = m0 + j
                    bk = C.bank()
                    for k in range(8):
                        S.pe(lambda e, bk=bk, k=k, m=m, c0=c0, b=b: e.matmul(
                            C.ps[bk], lhsT=W[:, k, c0 + m * 128:c0 + (m + 1) * 128], rhs=xnT[b][:, k, :],
                            start=(k == 0), stop=(k == 7)),
                            reads=[wkeys[k], xkeys[k]], writes=[("ps", bk)])
                    C.evac(st_fm[sbuf][:, j, :], C.ps[bk], reads=[("ps", bk)], writes=[(tag + "sf", sbuf, j)])
                d = dst[m0 * 128:(m0 + mm) * 128, t0:t0 + 512].rearrange("(m p) t -> p m t", p=128)
                S.dma(d, st_fm[sbuf][:, 0:mm, :], reads=[(tag + "sf", sbuf, j) for j in range(mm)],
                      writes=[(tag + "fm_d", id(dst), i)])
        for j, (c0, n, dst, dt) in enumerate(tm):
            stg = st_tm[j][b]
            for s in range(4):
                bk = C.bank()
                for k in range(8):
                    S.pe(lambda e, bk=bk, k=k, s=s, c0=c0, n=n, b=b: e.matmul(
                        C.ps[bk][:, 0:n], lhsT=xnT[b][:, k, s * 128:(s + 1) * 128], rhs=W[:, k, c0:c0 + n],
                        start=(k == 0), stop=(k == 7)),
                        reads=[wkeys[k], xkeys[k]], writes=[("ps", bk)])
                C.evac(stg[:, s, :], C.ps[bk][:, 0:n], reads=[("ps", bk)], writes=[(tag + "st", j, b, s)])
            d = dst[t0:t0 + 512, :].rearrange("(s p) c -> p s c", p=128)
            S.dma(d, stg, reads=[(tag + "st", j, b, s) for s in range(4)], writes=[(tag + "tm_d", j, i)])
    C.barrier()


def load_cols(C, dst, src_rows, nrow, ident32, tag):
    S, sb = C.S, C.sb
    tmp = sb.alloc((128,), F32)
    S.dma(tmp[0:nrow, :], src_rows, reads=[], writes=[("tmpc", tag)])
    bk = C.bank()
    S.pe(lambda e: e.transpose(C.ps[bk][:, 0:nrow], tmp[0:nrow, :], ident32[0:nrow, 0:nrow]),
         reads=[("tmpc", tag), "ident32"], writes=[("ps", bk)])
    S.dve(lambda e: e.tensor_copy(out=dst, in_=C.ps[bk][:, 0:nrow]), reads=[("ps", bk)], writes=[tag])


def phase_outproj(C, mix_src, kdim, Wo_src, x_src, w_row, xr_dst, xn_dst, tag, mix_key):
    S, sb = C.S, C.sb
    sb.reset()
    kt = kdim // 128
    Wo = sb.alloc((kt, D), BF16)
    load_w_bf16(C, Wo, Wo_src, tag + "W", kt)
    wbc = sb.alloc((D,), F32)
    load_bcast(C, wbc, w_row, tag + "nw")
    mixT = [sb.alloc((kt, 512), BF16) for _ in range(2)]
    xt = [sb.alloc((4, D), F32) for _ in range(2)]
    xn = [sb.alloc((4, D), BF16) for _ in range(2)]
    ss = [sb.alloc((4,), F32) for _ in range(2)]
    junk = sb.alloc((D,), BF16)
    zero = sb.alloc((D,), BF16)
    S.dve(lambda e: e.memset(zero, 0.0), reads=[], writes=[tag + "zero"])
    for si, T in enumerate(C.seqs):
        p0 = C.poffs[si]
        S.dma(xn_dst[p0:p0 + 1, :], zero[0:1, :], reads=[tag + "zero"], writes=[("xnp_pad", si, 0)])
        r0 = p0 + 1 + T
        r1 = p0 + C.pstride[si]
        while r0 < r1:
            n = min(128, r1 - r0)
            S.dma(xn_dst[r0:r0 + n, :], zero[0:n, :], reads=[tag + "zero"], writes=[("xnp_pad", si, r0)])
            r0 += n
    ti = 0
    for si, T in enumerate(C.seqs):
        for i in range(T // 512):
            b = ti % 2
            g0 = C.offs[si] + i * 512
            pr0 = C.poffs[si] + 1 + i * 512
            for k in range(kt):
                S.dma(mixT[b][:, k, :], mix_src[g0:g0 + 512, k * 128:(k + 1) * 128], reads=[(mix_key, g0 // 512)],
                      writes=[(tag + "mT", b, k)], transpose=True)
            S.dma(xt[b], x_src[g0:g0 + 512, :].rearrange("(s p) d -> p s d", p=128), reads=[("xres_d", g0 // 512)],
                  writes=[(tag + "x", b, s) for s in range(4)])
            for s_ in range(4):
                for h in range(2):
                    bk = C.bank()
                    for k in range(kt):
                        S.pe(lambda e, bk=bk, k=k, s_=s_, h=h, b=b: e.matmul(
                            C.ps[bk], lhsT=mixT[b][:, k, s_ * 128:(s_ + 1) * 128], rhs=Wo[:, k, h * 512:(h + 1) * 512],
                            start=(k == 0), stop=(k == kt - 1)),
                            reads=[(tag + "W", k), (tag + "mT", b, k)], writes=[("ps", bk)])
                    S.dve(lambda e, bk=bk, s_=s_, h=h, b=b: e.tensor_tensor(
                        out=xt[b][:, s_, h * 512:(h + 1) * 512], in0=C.ps[bk], in1=xt[b][:, s_, h * 512:(h + 1) * 512],
                        op=ALU.add), reads=[("ps", bk), (tag + "x", b, s_)], writes=[(tag + "x", b, s_)])
            norm_tile(C, xt[b], [(tag + "x", b, s) for s in range(4)], wbc, tag + "nw", xn[b],
                      [(tag + "xn", b, s) for s in range(4)], ss[b], (tag + "ss", b), junk, tag + "j", 4)
            S.dma(xr_dst[pr0:pr0 + 512, :].rearrange("(s p) d -> p s d", p=128), xt[b],
                  reads=[(tag + "x", b, s) for s in range(4)], writes=[("xrp_d", si, i)])
            S.dma(xn_dst[pr0:pr0 + 512, :].rearrange("(s p) d -> p s d", p=128), xn[b],
                  reads=[(tag + "xn", b, s) for s in range(4)], writes=[("xnp_d", si, i)])
            ti += 1
    C.barrier()


def phase_ffn(C, xnp_src, xrp_src, Wu_src, Wd_src, cw_src, cb_src, w_row, ident32, out_specs, tag, final):
    S, sb = C.S, C.sb
    sb.reset()
    Wu = sb.alloc((8, 2 * D_FF), BF16)
    Wd = sb.alloc((22, D), BF16)
    load_w_bf16(C, Wu, Wu_src, tag + "Wu", 8)
    load_w_bf16(C, Wd, Wd_src, tag + "Wd", 22)
    wbc = sb.alloc((D,), F32)
    load_bcast(C, wbc, w_row, tag + "nw")
    cw = sb.alloc((3, 44), F32)
    cb = sb.alloc((44,), F32)
    for i in range(3):
        load_cols(C, cw[:, i, :], cw_src[i].rearrange("(m p) -> m p", p=128), 44, ident32, (tag + "cw", i))
    load_cols(C, cb, cb_src.rearrange("(m p) -> m p", p=128), 44, ident32, tag + "cb")
    cwk = [(tag + "cw", i) for i in range(3)] + [tag + "cb"]
    xnT = [sb.alloc((8, 256), BF16) for _ in range(2)]
    xt = [sb.alloc((2, D), F32) for _ in range(2)]
    if final:
        xo1 = sb.alloc((2, D), F32)
        xo = [xo1, xo1]
    else:
        xo = [sb.alloc((2, D), BF16) for _ in range(2)]
    a_t1 = sb.alloc((22, 256), BF16)
    a_t = [a_t1, a_t1]
    S.pool(lambda e: e.memset(a_t1, 0.0), reads=[], writes=[(tag + "a", 0, m) for m in range(22)])
    tv = [sb.alloc((256,), F32) for _ in range(2)]
    tg = [sb.alloc((256,), F32) for _ in range(2)]
    ss = [sb.alloc((2,), F32) for _ in range(2)]
    junk = sb.alloc((D,), BF16)
    wi = 0
    for si, T in enumerate(C.seqs):
        for w in range(C.nw[si]):
            b = wi % 2
            u0 = 254 * w
            pr0 = C.poffs[si] + u0
            for k in range(8):
                S.dma(xnT[b][:, k, :], xnp_src[pr0:pr0 + 256, k * 128:(k + 1) * 128],
                      reads=[("xnp_d", si, j) for j in range(u0 // 512, min(T // 512, (u0 + 255) // 512 + 1))] +
                      [("xnp_pad", si, 0)], writes=[(tag + "xT", b, k)], transpose=True)
            S.dma(xt[b], xrp_src[pr0:pr0 + 256, :].rearrange("(s p) d -> p s d", p=128),
                  reads=[("xrp_d", si, j) for j in range(u0 // 512, min(T // 512, (u0 + 255) // 512 + 1))],
                  writes=[(tag + "x", b, s) for s in range(2)])
            xk = [(tag + "xT", b, k) for k in range(8)]
            for m in range(22):
                tb = m % 2
                bv = C.bank()
                for k in range(8):
                    S.pe(lambda e, bk=bv, k=k, m=m, b=b: e.matmul(
                        C.ps[bk][:, 0:256], lhsT=Wu[:, k, m * 128:(m + 1) * 128], rhs=xnT[b][:, k, :],
                        start=(k == 0), stop=(k == 7)), reads=[(tag + "Wu", k), xk[k]], writes=[("ps", bv)])
                bg = C.bank()
                for k in range(8):
                    S.pe(lambda e, bk=bg, k=k, m=m, b=b: e.matmul(
                        C.ps[bk][:, 0:256], lhsT=Wu[:, k, D_FF + m * 128:D_FF + (m + 1) * 128], rhs=xnT[b][:, k, :],
                        start=(k == 0), stop=(k == 7)), reads=[(tag + "Wu", k), xk[k]], writes=[("ps", bg)])
                for (bk, tt, mm, key) in ((bv, tv[tb], m, (tag + "tv", tb)), (bg, tg[tb], 22 + m, (tag + "tg", tb))):
                    S.act(lambda e, bk=bk, tt=tt, mm=mm: e.activation(
                        out=tt[:, 1:255], in_=C.ps[bk][:, 1:255], func=AF.Identity, scale=cw[:, 1, mm:mm + 1],
                        bias=cb[:, mm:mm + 1]), reads=[("ps", bk)] + cwk, writes=[key])
                    S.dve(lambda e, bk=bk, tt=tt, mm=mm: e.scalar_tensor_tensor(
                        out=tt[:, 1:255], in0=C.ps[bk][:, 0:254], scalar=cw[:, 0, mm:mm + 1], in1=tt[:, 1:255],
                        op0=ALU.mult, op1=ALU.add), reads=[("ps", bk), key] + cwk, writes=[key])
                    S.dve(lambda e, bk=bk, tt=tt, mm=mm: e.scalar_tensor_tensor(
                        out=tt[:, 1:255], in0=C.ps[bk][:, 2:256], scalar=cw[:, 2, mm:mm + 1], in1=tt[:, 1:255],
                        op0=ALU.mult, op1=ALU.add), reads=[("ps", bk), key] + cwk, writes=[key])
                S.act(lambda e, tb=tb: e.activation(out=tg[tb][:, 1:255], in_=tg[tb][:, 1:255], func=AF.Silu),
                      reads=[(tag + "tg", tb)], writes=[(tag + "tg", tb)])
                S.pool(lambda e, tb=tb, m=m, b=b: e.tensor_tensor(
                    out=a_t[b][:, m, 1:255], in0=tg[tb][:, 1:255], in1=tv[tb][:, 1:255], op=ALU.mult),
                    reads=[(tag + "tg", tb), (tag + "tv", tb)], writes=[(tag + "a", 0, m)])
            for s_ in range(2):
                for h in range(2):
                    bk = C.bank()
                    for kk in range(22):
                        S.pe(lambda e, bk=bk, kk=kk, s_=s_, h=h, b=b: e.matmul(
                            C.ps[bk], lhsT=a_t[b][:, kk, s_ * 128:(s_ + 1) * 128], rhs=Wd[:, kk, h * 512:(h + 1) * 512],
                            start=(kk == 0), stop=(kk == 21)),
                            reads=[(tag + "Wd", kk), (tag + "a", 0, kk)], writes=[("ps", bk)])
                    S.dve(lambda e, bk=bk, s_=s_, h=h, b=b: e.tensor_tensor(
                        out=xt[b][:, s_, h * 512:(h + 1) * 512], in0=C.ps[bk], in1=xt[b][:, s_, h * 512:(h + 1) * 512],
                        op=ALU.add), reads=[("ps", bk), (tag + "x", b, s_)], writes=[(tag + "x", b, s_)])
            norm_tile(C, xt[b], [(tag + "x", b, s) for s in range(2)], wbc, tag + "nw", xo[b],
                      [(tag + "xo", b if not final else 0, s) for s in range(2)], ss[b], (tag + "ss", b), junk, tag + "j", 2)
            jhi = min(254, T - u0)
            for s_ in range(2):
                plo = max(0, 1 - 128 * s_)
                phi = min(127, jhi - 128 * s_)
                if phi < plo:
                    continue
                t_lo = u0 + 128 * s_ + plo - 1
                n = phi - plo + 1
                out_specs(si, t_lo, n, xt[b][plo:plo + n, s_, :], xo[b][plo:plo + n, s_, :],
                          [(tag + "x", b, s_)], [(tag + "xo", b if not final else 0, s_)])
            wi += 1
    C.barrier()


def phase_gdn(C, projT, ztok, gates, of_s, og, K_, conv_src, alog_src, dtb_src, gnorm_src, NH=8, HB=4):
    S, sb = C.S, C.sb
    S_real = S
    sb.reset()
    NC2 = 2 * NH
    ident_bf, ident32 = K_["ident_bf"], K_["ident32"]
    ck = list(K_["keys"])
    A16 = sb.alloc((NC2,), F32)
    DTB = sb.alloc((NC2,), F32)
    load_bcast(C, A16, alog_src, "gA16")
    load_bcast(C, DTB, dtb_src, "gDTB")
    S.act(lambda e: e.activation(out=A16, in_=A16, func=AF.Exp), reads=["gA16"], writes=["gA16"])
    S.dve(lambda e: e.tensor_scalar(out=A16, in0=A16, scalar1=-1.0, scalar2=None, op0=ALU.mult),
          reads=["gA16"], writes=["gA16"])
    sel_d = C.inp("csel", [16, 2048], F32)
    SELt = sb.alloc((16, 128), F32)
    S.dma(SELt[0:16, :, :], sel_d.rearrange("p (a b) -> p a b", a=16), reads=[], writes=["csel"])
    onec = sb.alloc((1,), F32)
    epsc = sb.alloc((1,), F32)
    S.dve(lambda e: e.memset(onec, 1.0), reads=[], writes=["gonec"])
    S.dve(lambda e: e.memset(epsc, EPS), reads=[], writes=["gepsc"])
    gnw = sb.alloc((128,), F32)
    load_bcast(C, gnw, gnorm_src, "gnw")
    cwg = sb.alloc((5, 3 * NH), F32)
    for i in range(5):
        load_cols(C, cwg[:, i, :], conv_src[i].rearrange("(m p) -> m p", p=128), 3 * NH, ident32, ("gcw", i))
    DG = sb.alloc((3 * HB * 5, 128), BF16)

    def build_DG(hg):
        for w3 in range(3):
            for hh in range(HB):
                ct = w3 * NH + hg * HB + hh
                lc = w3 * HB + hh
                for i in range(5):
                    S.dve(lambda e, ct=ct, lc=lc, i=i: e.tensor_scalar(out=DG[:, lc * 5 + i, :], in0=ident_bf,
                                                                      scalar1=cwg[:, i, ct:ct + 1], scalar2=None, op0=ALU.mult),
                          reads=[("gcw", i), "ident_bf"], writes=[("gDG", lc)])
    SEGT = 32
    NTmax = SEGT
    GA = sb.alloc((NTmax, 2 * NC2), F32)
    G = sb.alloc((NTmax, NC2), F32)
    GH = sb.alloc((NTmax, NC2), BF16)
    GLo = sb.alloc((NTmax, NC2), BF16)
    BETA = sb.alloc((NTmax, NC2), F32)
    NBETA = sb.alloc((NTmax, NC2), F32)
    GC = sb.alloc((NTmax, NC2), F32)
    EGC = sb.alloc((NTmax, NC2), F32)
    EKD = sb.alloc((NTmax, NC2), F32)
    DEC = sb.alloc((NTmax, 2, NC2), F32)
    X = [[sb.alloc((516,), BF16) for _ in range(3 * HB)] for _ in range(1)]
    qc = [sb.alloc((512,), F32) for _ in range(1)] * 2
    sq = [sb.alloc((512,), BF16) for _ in range(1)] * 2
    rn = [sb.alloc((512,), F32) for _ in range(1)] * 2
    qT = [sb.alloc((HB, 512), BF16) for _ in range(2)]
    kT = [sb.alloc((HB, 512), BF16) for _ in range(2)]
    vT = [sb.alloc((HB, 512), BF16) for _ in range(2)]
    qdT = [sb.alloc((HB, 512), BF16) for _ in range(2)]
    ktok = [sb.alloc((HB, 4, 128), BF16) for _ in range(2)]
    vtok = [sb.alloc((HB, 4, 128), BF16) for _ in range(2)]
    EGT = [sb.alloc((512,), F32) for _ in range(2)]
    NB2 = 4
    gUh = [sb.alloc((HB, 128), BF16) for _ in range(NB2)]
    gUl = [sb.alloc((HB, 128), BF16) for _ in range(NB2)]
    Dm = [sb.alloc((HB, 128), BF16) for _ in range(NB2)]
    DmTi = [sb.alloc((HB, 128), BF16) for _ in range(NB2)]
    attnT = [sb.alloc((HB, 128), BF16) for _ in range(NB2)]
    Pm = [[sb.alloc((HB, 128), BF16) for _ in range(2)] for _ in range(NB2)]
    Qm = [[sb.alloc((HB, 128), BF16) for _ in range(2)] for _ in range(NB2)]
    Ym = [[sb.alloc((HB, 128), BF16) for _ in range(2)] for _ in range(NB2)]
    TbT = [sb.alloc((HB, 128), BF16) for _ in range(NB2)]
    keg = [sb.alloc((HB, 128), BF16) for _ in range(NB2)]
    kdec = [sb.alloc((HB, 128), BF16) for _ in range(NB2)]
    negwT = [sb.alloc((HB, 128), BF16) for _ in range(NB2)]
    vnew = [sb.alloc((HB, 128), BF16) for _ in range(NB2)]
    S32 = sb.alloc((HB, 128), F32)
    S16 = sb.alloc((HB, 128), BF16)
    ob = [sb.alloc((HB, 128), BF16) for _ in range(2)]
    o32 = [sb.alloc((HB, 128), F32) for _ in range(2)]
    ofl = [sb.alloc((HB, 128), BF16) for _ in range(2)]
    zt = [sb.alloc((HB, 128), BF16) for _ in range(2)]
    ogt = [sb.alloc((HB, 128), BF16) for _ in range(2)]
    o2 = sb.alloc((HB, 128), F32)
    ssq = [sb.alloc((HB,), F32) for _ in range(2)]
    junkg = sb.alloc((128,), BF16)

    def psb(bk):
        return C.ps[bk].bitcast(BF16)

    def ps4(bk):
        return C.ps[bk][:, 0:HB * 128].rearrange("p (h c) -> p h c", h=HB)

    def ps4b(bk):
        return psb(bk)[:, 0:HB * 128].rearrange("p (h c) -> p h c", h=HB)

    tcount = [0]
    for si, T in enumerate(C.seqs):
        NTall = T // 128
        NBk = T // 512
        g0 = C.offs[si]
        def gates_prep(tile0, NT, g0=g0):
            S.dma(GA[:, 0:NT, :], gates[g0 + tile0 * 128:g0 + (tile0 + NT) * 128, :].rearrange("(n p) c -> p n c", p=128), reads=[], writes=["GA"])
            Ga = GA[:, 0:NT, 0:NC2]
            Gb = GA[:, 0:NT, NC2:2 * NC2]
            Gv = G[:, 0:NT, :]
            S.dve(lambda e, Ga=Ga, Gv=Gv, NT=NT: e.tensor_tensor(out=Gv, in0=Ga, in1=DTB.unsqueeze(1).to_broadcast([128, NT, NC2]),
                                                                 op=ALU.add), reads=["GA", "gDTB"], writes=["G"])
            S.act(lambda e, Gv=Gv: e.activation(out=Gv, in_=Gv, func=AF.Exp), reads=["G"], writes=["G"])
            S.act(lambda e, Gv=Gv: e.activation(out=Gv, in_=Gv, func=AF.Ln, bias=onec[:, 0:1]), reads=["G", "gonec"], writes=["G"])
            S.dve(lambda e, Gv=Gv, NT=NT: e.tensor_tensor(out=Gv, in0=Gv, in1=A16.unsqueeze(1).to_broadcast([128, NT, NC2]),
                                                         op=ALU.mult), reads=["G", "gA16"], writes=["G"])
            S.act(lambda e, Gv=Gv, NT=NT: e.activation(out=GH[:, 0:NT, :], in_=Gv, func=AF.Copy), reads=["G"], writes=["GH"])
            S.dve(lambda e, Gv=Gv, NT=NT: e.tensor_tensor(out=GLo[:, 0:NT, :], in0=Gv, in1=GH[:, 0:NT, :], op=ALU.subtract),
                  reads=["G", "GH"], writes=["GLo"])
            Bv = BETA[:, 0:NT, :]
            S.act(lambda e, Gb=Gb, Bv=Bv: e.activation(out=Bv, in_=Gb, func=AF.Exp, scale=-1.0), reads=["GA"], writes=["BETA"])
            S.dve(lambda e, Bv=Bv: e.tensor_scalar(out=Bv, in0=Bv, scalar1=1.0, scalar2=None, op0=ALU.add),
                  reads=["BETA"], writes=["BETA"])
            S.dve(lambda e, Bv=Bv: e.reciprocal(out=Bv, in_=Bv), reads=["BETA"], writes=["BETA"])
            S.dve(lambda e, Bv=Bv, NT=NT: e.tensor_scalar(out=NBETA[:, 0:NT, :], in0=Bv, scalar1=-1.0, scalar2=None,
                                                          op0=ALU.mult), reads=["BETA"], writes=["NBETA"])
            for t0 in range(0, NT, 32):
                n = min(32, NT - t0)
                bk = C.bank()
                for t in range(t0, t0 + n):
                    c0 = (t - t0) * NC2
                    S.pe(lambda e, bk=bk, t=t, c0=c0: e.matmul(C.ps[bk][:, c0:c0 + NH], lhsT=K_["UF32"], rhs=G[:, t, 0:NH],
                                                               start=True, stop=True), reads=["G"] + ck, writes=[("ps", bk)])
                    S.pe(lambda e, bk=bk, t=t, c0=c0: e.matmul(C.ps[bk][:, c0 + NH:c0 + NC2], lhsT=K_["UB32"], rhs=G[:, t, NH:NC2],
                                                               start=True, stop=True), reads=["G"] + ck, writes=[("ps", bk)])
                S.dve(lambda e, bk=bk, t0=t0, n=n: e.tensor_copy(
                    out=GC[:, t0:t0 + n, :], in_=C.ps[bk][:, 0:n * NC2].rearrange("p (n c) -> p n c", c=NC2)),
                    reads=[("ps", bk)], writes=["GC"])
            S.act(lambda e, NT=NT: e.activation(out=EGC[:, 0:NT, :], in_=GC[:, 0:NT, :], func=AF.Exp), reads=["GC"], writes=["EGC"])
            for t0 in range(0, NT, 16):
                n = min(16, NT - t0)
                bk = C.bank()
                for t in range(t0, t0 + n):
                    for j in range(2):
                        c0 = ((t - t0) * 2 + j) * NC2
                        S.pe(lambda e, bk=bk, t=t, c0=c0, j=j: e.matmul(C.ps[bk][:, c0:c0 + NC2], lhsT=K_["CH%d" % j],
                                                                        rhs=G[:, t, :], start=True, stop=True),
                             reads=["G"] + ck, writes=[("ps", bk)])
                pv = C.ps[bk][:, 0:n * 2 * NC2].rearrange("p (n j c) -> p n j c", j=2, c=NC2)
                S.act(lambda e, pv=pv, t0=t0, n=n: e.activation(out=DEC[:, t0:t0 + n, :, :], in_=pv, func=AF.Exp),
                      reads=[("ps", bk)], writes=["DEC"])
                for j in range(2):
                    S.dve(lambda e, pv=pv, t0=t0, n=n, j=j: e.tensor_tensor(
                        out=EKD[64 * j:64 * j + 64, t0:t0 + n, :], in0=pv[64 * j:64 * j + 64, :, j, :],
                        in1=GC[64 * j:64 * j + 64, t0:t0 + n, :], op=ALU.subtract),
                        reads=[("ps", bk), "GC"], writes=["EKD"])
            S.act(lambda e, NT=NT: e.activation(out=EKD[:, 0:NT, :], in_=EKD[:, 0:NT, :], func=AF.Exp), reads=["EKD"], writes=["EKD"])
            gk = ["G", "GH", "GLo", "BETA", "NBETA", "GC", "EGC", "EKD", "DEC"]

        for hg in range(NH // HB):
            for di in range(2):
                Ud = K_["UF32"] if di == 0 else K_["UB32"]
                MA = K_["MA_f"] if di == 0 else K_["MA_b"]
                S.dve(lambda e: e.memset(S32, 0.0), reads=[], writes=["S32"])
                S.dve(lambda e: e.memset(S16, 0.0), reads=[], writes=["S16"])
                if di == 0:
                    build_DG(hg)
                blocks = list(range(NBk)) if di == 0 else list(range(NBk - 1, -1, -1))
                cur_seg = None
                pending_rc = []
                pair = []
                for bi, blk in enumerate(blocks):
                    bb = bi % 2
                    t0 = blk * 512
                    seg = (blk * 4) // SEGT
                    if seg != cur_seg:
                        cur_seg = seg
                        replay_merged(S_real, pending_rc, *[p_[0] for p_ in pair])
                        replay_merged(S_real, [p_[1] for p_ in pair])
                        pending_rc, pair = [], []
                        gates_prep(seg * SEGT, min(SEGT, NTall - seg * SEGT))
                    tl0 = seg * SEGT
                    S = C.S = Rec()
                    C.bank_pool = (0, 1, 2, 3, 4, 5)
                    assert len(pair) == 0
                    for hh in range(HB):
                        h = hg * HB + hh
                        for w3 in range(3):
                            ct = w3 * NH + h
                            xb = X[0][w3 * HB + hh]
                            xkey = ("gX", 0, w3, hh)
                            lo, hi = t0 - 2, t0 + 514
                            dlo, dhi = max(lo, 0), min(hi, T)
                            if dlo > lo:
                                S.pool(lambda e, xb=xb: e.memset(xb[:, 0:2], 0.0), reads=[], writes=[xkey])
                            if dhi < hi:
                                S.pool(lambda e, xb=xb: e.memset(xb[:, 514:516], 0.0), reads=[], writes=[xkey])
                            S.dma(xb[:, dlo - lo:dhi - lo], projT[ct * 128:(ct + 1) * 128, g0 + dlo:g0 + dhi],
                                  reads=[], writes=[xkey])
                            bk = C.bank()
                            for i in range(5):
                                S.pe(lambda e, bk=bk, i=i, xb=xb, w3=w3, hh=hh: e.matmul(
                                    C.ps[bk], lhsT=DG[:, (w3 * HB + hh) * 5 + i, :], rhs=xb[:, i:i + 512], start=(i == 0), stop=(i == 4)),
                                    reads=[xkey, ("gDG", w3 * HB + hh)], writes=[("ps", bk)])
                            if w3 == 2:
                                S.act(lambda e, bk=bk, bb=bb, hh=hh: e.activation(out=vT[bb][:, hh, :], in_=C.ps[bk], func=AF.Silu),
                                      reads=[("ps", bk)], writes=[("gvT", bb, hh)])
                                continue
                            q2 = 0
                            tcount[0] += 1
                            S.act(lambda e, bk=bk, q2=q2: e.activation(out=qc[q2], in_=C.ps[bk], func=AF.Silu),
                                  reads=[("ps", bk)], writes=[("gqc", q2)])
                            S.act(lambda e, q2=q2: e.activation(out=sq[q2], in_=qc[q2], func=AF.Square),
                                  reads=[("gqc", q2)], writes=[("gsq", q2)])
                            bk2 = C.bank()
                            S.pe(lambda e, bk2=bk2, q2=q2: e.matmul(C.ps[bk2], lhsT=K_["ones_bf"], rhs=sq[q2], start=True, stop=True),
                                 reads=[("gsq", q2)] + ck, writes=[("ps", bk2)])
                            S.act(lambda e, bk2=bk2, q2=q2: e.activation(out=rn[q2], in_=C.ps[bk2], func=AF.Sqrt, bias=epsc[:, 0:1]),
                                  reads=[("ps", bk2), "gepsc"], writes=[("grn", q2)])
                            S.dve(lambda e, q2=q2: e.reciprocal(out=rn[q2], in_=rn[q2]), reads=[("grn", q2)], writes=[("grn", q2)])
                            dst = qT[bb][:, hh, :] if w3 == 0 else kT[bb][:, hh, :]
                            dkey = ("gqT", bb, hh) if w3 == 0 else ("gkT", bb, hh)
                            sc = (128.0 ** -0.5) if w3 == 0 else 1.0
                            S.dve(lambda e, q2=q2, dst=dst, sc=sc: e.scalar_tensor_tensor(
                                out=dst, in0=qc[q2], scalar=sc, in1=rn[q2], op0=ALU.mult, op1=ALU.mult),
                                reads=[("gqc", q2), ("grn", q2)], writes=[dkey])
                        for (srcT, dstk, skey, dkey) in ((kT[bb], ktok[bb], ("gkT", bb, hh), ("gktok", bb, hh)),
                                                         (vT[bb], vtok[bb], ("gvT", bb, hh), ("gvtok", bb, hh))):
                            bk = C.bank()
                            for t in range(4):
                                S.pe(lambda e, bk=bk, t=t, srcT=srcT, hh=hh: e.transpose(
                                    psb(bk)[:, t * 128:(t + 1) * 128], srcT[:, hh, t * 128:(t + 1) * 128], ident_bf),
                                    reads=[skey, "ident_bf"], writes=[("ps", bk)])
                            C.evac(dstk[:, hh, :, :], psb(bk)[:, 0:512].rearrange("p (t c) -> p t c", t=4),
                                   reads=[("ps", bk)], writes=[dkey])
                    bk = C.bank()
                    for t in range(4):
                        S.pe(lambda e, bk=bk, t=t, blk=blk, tl0=tl0: e.transpose(
                            C.ps[bk][0:NC2, t * 128:(t + 1) * 128], EGC[:, blk * 4 + t - tl0, :], ident32),
                            reads=["EGC", "ident32"], writes=[("ps", bk)])
                    S.dve(lambda e, bk=bk, bb=bb: e.tensor_copy(out=EGT[bb][0:NC2, :], in_=C.ps[bk][0:NC2, :]),
                          reads=[("ps", bk)], writes=[("gEGT", bb)])
                    for hh in range(HB):
                        hd = di * NH + hg * HB + hh
                        bk = C.bank()
                        S.pe(lambda e, bk=bk, hd=hd, bb=bb: e.matmul(C.ps[bk], lhsT=SELt[0:NC2, hd, :], rhs=EGT[bb][0:NC2, :],
                                                                     start=True, stop=True),
                             reads=[("gEGT", bb), "csel"] + ck, writes=[("ps", bk)])
                        S.dve(lambda e, bk=bk, hh=hh, bb=bb: e.tensor_tensor(out=qdT[bb][:, hh, :], in0=C.ps[bk], in1=qT[bb][:, hh, :],
                                                                             op=ALU.mult),
                              reads=[("ps", bk), ("gqT", bb, hh)], writes=[("gqdT", bb, hh)])
                    tiles = list(range(4)) if di == 0 else [3, 2, 1, 0]
                    rec_bp = S
                    S = C.S = S_real
                    replay_merged(S_real, pending_rc, rec_bp)
                    pending_rc = []
                    for tl in tiles:
                        S = C.S = Rec()
                        C.bank_pool = (0, 1, 2) if len(pair) == 0 else (3, 4, 5)
                        tt = blk * 4 + tl
                        lt = tt - tl0
                        wb = tt % NB2
                        ts_ = slice(tl * 128, (tl + 1) * 128)
                        hd0 = di * NH + hg * HB
                        Ub = Ud.unsqueeze(1).to_broadcast([128, HB, 128])
                        for (dst, src, key, skey) in ((gUh[wb], GH, ("gUh", wb), "GH"), (gUl[wb], GLo, ("gUl", wb), "GLo")):
                            S.pool(lambda e, dst=dst, src=src, lt=lt, hd0=hd0, Ub=Ub: e.tensor_tensor(
                                out=dst, in0=Ub, in1=src[:, lt, hd0:hd0 + HB].unsqueeze(2).to_broadcast([128, HB, 128]),
                                op=ALU.mult), reads=[skey] + ck, writes=[key])
                        bkE = C.bank()
                        for hh in range(HB):
                            o_ = ps4(bkE)[:, hh, :]
                            S.pe(lambda e, o_=o_, wb=wb, hh=hh: e.matmul(o_, lhsT=gUh[wb][:, hh, :], rhs=K_["ones_bf"], start=True, stop=False),
                                 reads=[("gUh", wb)] + ck, writes=[("ps", bkE)])
                            S.pe(lambda e, o_=o_, wb=wb, hh=hh: e.matmul(o_, lhsT=gUl[wb][:, hh, :], rhs=K_["ones_bf"], start=False, stop=False),
                                 reads=[("gUl", wb)] + ck, writes=[("ps", bkE)])
                            S.pe(lambda e, o_=o_, wb=wb, hh=hh: e.matmul(o_, lhsT=K_["negones_bf"], rhs=gUh[wb][:, hh, :], start=False, stop=False),
                                 reads=[("gUh", wb)] + ck, writes=[("ps", bkE)])
                            S.pe(lambda e, o_=o_, wb=wb, hh=hh: e.matmul(o_, lhsT=K_["negones_bf"], rhs=gUl[wb][:, hh, :], start=False, stop=False),
                                 reads=[("gUl", wb)] + ck, writes=[("ps", bkE)])
                            S.pe(lambda e, o_=o_, MA=MA: e.matmul(o_, lhsT=ident_bf, rhs=MA, start=False, stop=True),
                                 reads=ck, writes=[("ps", bkE)])
                        S.act(lambda e, bkE=bkE, wb=wb: e.activation(out=Dm[wb], in_=ps4(bkE), func=AF.Exp),
                              reads=[("ps", bkE)], writes=[("gDm", wb)])
                        bkT = C.bank()
                        for hh in range(HB):
                            S.pe(lambda e, bkT=bkT, wb=wb, hh=hh: e.transpose(ps4b(bkT)[:, hh, :], Dm[wb][:, hh, :], ident_bf),
                                 reads=[("gDm", wb), "ident_bf"], writes=[("ps", bkT)])
                        S.dve(lambda e, bkT=bkT, wb=wb: e.tensor_tensor(
                            out=DmTi[wb], in0=ps4b(bkT), in1=ident_bf.unsqueeze(1).to_broadcast([128, HB, 128]), op=ALU.add),
                            reads=[("ps", bkT), "ident_bf"], writes=[("gDmTi", wb)])
                        bkK = C.bank()
                        bkQ = C.bank()
                        for hh in range(HB):
                            S.pe(lambda e, bkK=bkK, hh=hh, bb=bb, ts_=ts_: e.matmul(ps4(bkK)[:, hh, :], lhsT=kT[bb][:, hh, ts_], rhs=kT[bb][:, hh, ts_],
                                                                                   start=True, stop=True),
                                 reads=[("gkT", bb, hh)], writes=[("ps", bkK)])
                            S.pe(lambda e, bkQ=bkQ, hh=hh, bb=bb, ts_=ts_: e.matmul(ps4(bkQ)[:, hh, :], lhsT=kT[bb][:, hh, ts_], rhs=qT[bb][:, hh, ts_],
                                                                                   start=True, stop=True),
                                 reads=[("gkT", bb, hh), ("gqT", bb, hh)], writes=[("ps", bkQ)])
                        Q0 = Qm[wb][0]
                        for hh in range(HB):
                            S.dve(lambda e, bkK=bkK, hh=hh, wb=wb, lt=lt, hd0=hd0, Q0=Q0: e.scalar_tensor_tensor(
                                out=Q0[:, hh, :], in0=ps4(bkK)[:, hh, :], scalar=NBETA[:, lt, hd0 + hh:hd0 + hh + 1], in1=Dm[wb][:, hh, :],
                                op0=ALU.mult, op1=ALU.mult), reads=[("ps", bkK), "NBETA", ("gDm", wb)], writes=[("gQ", wb, 0)])
                        S.dve(lambda e, bkQ=bkQ, wb=wb: e.tensor_tensor(out=attnT[wb], in0=ps4(bkQ), in1=DmTi[wb], op=ALU.mult),
                              reads=[("ps", bkQ), ("gDmTi", wb)], writes=[("gattnT", wb)])
                        bkN = C.bank()
                        for hh in range(HB):
                            S.pe(lambda e, bkN=bkN, hh=hh, Q0=Q0: e.transpose(ps4b(bkN)[:, hh, :], Q0[:, hh, :], ident_bf),
                                 reads=[("gQ", wb, 0), "ident_bf"], writes=[("ps", bkN)])
                        P0 = Pm[wb][0]
                        C.evac(P0, ps4b(bkN), reads=[("ps", bkN)], writes=[("gP", wb, 0)])
                        Y0 = Ym[wb][0]
                        S.dve(lambda e, bkN=bkN, Y0=Y0: e.tensor_tensor(
                            out=Y0, in0=ps4b(bkN), in1=ident_bf.unsqueeze(1).to_broadcast([128, HB, 128]), op=ALU.add),
                            reads=[("ps", bkN), "ident_bf"], writes=[("gY", wb, 0)])
                        for k in range(1, 6):
                            pp, pc = (k - 1) % 2, k % 2
                            Pp, Qp, Yp = Pm[wb][pp], Qm[wb][pp], Ym[wb][pp]
                            Pc, Qc, Yc = Pm[wb][pc], Qm[wb][pc], Ym[wb][pc]
                            if k <= 4:
                                bkP = C.bank()
                                for hh in range(HB):
                                    S.pe(lambda e, bkP=bkP, hh=hh, Qp=Qp, Pp=Pp: e.matmul(ps4(bkP)[:, hh, :], lhsT=Qp[:, hh, :], rhs=Pp[:, hh, :],
                                                                                         start=True, stop=True),
                                         reads=[("gQ", wb, pp), ("gP", wb, pp)], writes=[("ps", bkP)])
                            bkQ2 = C.bank()
                            for hh in range(HB):
                                S.pe(lambda e, bkQ2=bkQ2, hh=hh, Qp=Qp, Pp=Pp: e.matmul(ps4(bkQ2)[:, hh, :], lhsT=Pp[:, hh, :], rhs=Qp[:, hh, :],
                                                                                       start=True, stop=True),
                                     reads=[("gQ", wb, pp), ("gP", wb, pp)], writes=[("ps", bkQ2)])
                            if k <= 4:
                                C.evac(Pc, ps4(bkP), reads=[("ps", bkP)], writes=[("gP", wb, pc)])
                            C.evac(Qc, ps4(bkQ2), reads=[("ps", bkQ2)], writes=[("gQ", wb, pc)])
                            bkY = C.bank()
                            for hh in range(HB):
                                S.pe(lambda e, bkY=bkY, hh=hh, Qc=Qc, Yp=Yp: e.matmul(ps4(bkY)[:, hh, :], lhsT=Qc[:, hh, :], rhs=Yp[:, hh, :],
                                                                                     start=True, stop=True),
                                     reads=[("gQ", wb, pc), ("gY", wb, pp)], writes=[("ps", bkY)])
                            S.dve(lambda e, bkY=bkY, Yc=Yc, Yp=Yp: e.tensor_tensor(out=Yc, in0=ps4(bkY), in1=Yp, op=ALU.add),
                                  reads=[("ps", bkY), ("gY", wb, pp)], writes=[("gY", wb, pc)])
                        Y5 = Ym[wb][1]
                        for hh in range(HB):
                            sc_b = BETA[:, lt, hd0 + hh:hd0 + hh + 1]
                            sc_e = EGC[:, lt, hd0 + hh:hd0 + hh + 1]
                            sc_k = EKD[:, lt, hd0 + hh:hd0 + hh + 1]
                            S.act(lambda e, hh=hh, wb=wb, sc_b=sc_b, Y5=Y5: e.activation(out=TbT[wb][:, hh, :], in_=Y5[:, hh, :], func=AF.Copy, scale=sc_b),
                                  reads=[("gY", wb, 1), "BETA"], writes=[("gTbT", wb)])
                            S.pool(lambda e, hh=hh, wb=wb, sc_e=sc_e, bb=bb, tl=tl: e.tensor_scalar(
                                out=keg[wb][:, hh, :], in0=ktok[bb][:, hh, tl, :], scalar1=sc_e, scalar2=None, op0=ALU.mult),
                                reads=[("gktok", bb, hh), "EGC"], writes=[("gkeg", wb)])
                            S.pool(lambda e, hh=hh, wb=wb, sc_k=sc_k, bb=bb, tl=tl: e.tensor_scalar(
                                out=kdec[wb][:, hh, :], in0=ktok[bb][:, hh, tl, :], scalar1=sc_k, scalar2=None, op0=ALU.mult),
                                reads=[("gktok", bb, hh), "EKD"], writes=[("gkdec", wb)])
                        bkW = C.bank()
                        for hh in range(HB):
                            S.pe(lambda e, bkW=bkW, hh=hh, wb=wb: e.matmul(ps4(bkW)[:, hh, :], lhsT=keg[wb][:, hh, :], rhs=TbT[wb][:, hh, :],
                                                                           start=True, stop=True),
                                 reads=[("gkeg", wb), ("gTbT", wb)], writes=[("ps", bkW)])
                        S.act(lambda e, bkW=bkW, wb=wb: e.activation(out=negwT[wb], in_=ps4(bkW), func=AF.Copy, scale=-1.0),
                              reads=[("ps", bkW)], writes=[("gnegwT", wb)])
                        rec_pi = S
                        S = C.S = Rec()
                        C.bank_pool = (6, 7)
                        ob_i = tt % 2
                        if di == 1:
                            grow = g0 + tt * 128
                            S.dma(ofl[ob_i], of_s[grow:grow + 128, hg * HB * 128:(hg + 1) * HB * 128].rearrange("p (h c) -> p h c", h=HB),
                                  reads=[("of_d", si, hg, tt)], writes=[("gofl", ob_i)])
                            S.dma(zt[ob_i], ztok[grow:grow + 128, hg * HB * 128:(hg + 1) * HB * 128].rearrange("p (h c) -> p h c", h=HB),
                                  reads=[], writes=[("gzt", ob_i)])
                        for j in ((0, 1) if di == 0 else (1, 0)):
                            pr = slice(64 * j, 64 * j + 64)
                            bkV = C.bank()
                            for hh in range(HB):
                                S.pe(lambda e, bkV=bkV, hh=hh, wb=wb, bb=bb, tl=tl: e.matmul(ps4(bkV)[:, hh, :], lhsT=TbT[wb][:, hh, :], rhs=vtok[bb][:, hh, tl, :],
                                                                                             start=True, stop=False),
                                     reads=[("gTbT", wb), ("gvtok", bb, hh)], writes=[("ps", bkV)])
                                S.pe(lambda e, bkV=bkV, hh=hh, wb=wb: e.matmul(ps4(bkV)[:, hh, :], lhsT=negwT[wb][:, hh, :], rhs=S16[:, hh, :],
                                                                               start=False, stop=True),
                                     reads=[("gnegwT", wb), "S16"], writes=[("ps", bkV)])
                            S.dve(lambda e, bkV=bkV, wb=wb, pr=pr: e.tensor_copy(out=vnew[wb][pr, :, :], in_=ps4(bkV)[pr, :, :]),
                                  reads=[("ps", bkV)], writes=[("gvnew", wb, j)])
                            bkO = C.bank()
                            for hh in range(HB):
                                S.pe(lambda e, bkO=bkO, hh=hh, bb=bb, ts_=ts_: e.matmul(ps4(bkO)[:, hh, :], lhsT=qdT[bb][:, hh, ts_], rhs=S16[:, hh, :],
                                                                                       start=True, stop=False),
                                     reads=[("gqdT", bb, hh), "S16"], writes=[("ps", bkO)])
                                S.pe(lambda e, bkO=bkO, hh=hh, wb=wb, pr=pr: e.matmul(ps4(bkO)[:, hh, :], lhsT=attnT[wb][pr, hh, :], rhs=vnew[wb][pr, hh, :],
                                                                                     start=False, stop=True),
                                     reads=[("gattnT", wb), ("gvnew", wb, j)], writes=[("ps", bkO)])
                            if di == 0:
                                S.act(lambda e, bkO=bkO, ob_i=ob_i, pr=pr: e.activation(out=ob[ob_i][pr, :, :], in_=ps4(bkO)[pr, :, :], func=AF.Copy),
                                      reads=[("ps", bkO)], writes=[("gob", ob_i, j)])
                            else:
                                S.dve(lambda e, bkO=bkO, ob_i=ob_i, pr=pr: e.tensor_tensor(out=o32[ob_i][pr, :, :], in0=ps4(bkO)[pr, :, :],
                                                                                          in1=ofl[ob_i][pr, :, :], op=ALU.add),
                                      reads=[("ps", bkO), ("gofl", ob_i)], writes=[("go32", ob_i, j)])
                            bkS = C.bank()
                            for hh in range(HB):
                                S.pe(lambda e, bkS=bkS, hh=hh, wb=wb, pr=pr: e.matmul(ps4(bkS)[:, hh, :], lhsT=kdec[wb][pr, hh, :], rhs=vnew[wb][pr, hh, :],
                                                                                     start=True, stop=True),
                                     reads=[("gkdec", wb), ("gvnew", wb, j)], writes=[("ps", bkS)])
                            for hh in range(HB):
                                S.dve(lambda e, bkS=bkS, hh=hh, lt=lt, j=j, hd0=hd0: e.scalar_tensor_tensor(
                                    out=S32[:, hh, :], in0=S32[:, hh, :], scalar=DEC[:, lt, j, hd0 + hh:hd0 + hh + 1], in1=ps4(bkS)[:, hh, :],
                                    op0=ALU.mult, op1=ALU.add), reads=[("ps", bkS), "S32", "DEC"], writes=["S32"])
                            S.act(lambda e: e.activation(out=S16, in_=S32, func=AF.Copy), reads=["S32"], writes=["S16"])
                        grow = g0 + tt * 128
                        if di == 0:
                            S.dma(of_s[grow:grow + 128, hg * HB * 128:(hg + 1) * HB * 128].rearrange("p (h c) -> p h c", h=HB), ob[ob_i],
                                  reads=[("gob", ob_i, 0), ("gob", ob_i, 1)], writes=[("of_d", si, hg, tt)])
                        else:
                            o_in = o32[ob_i]
                            okeys = [("go32", ob_i, 0), ("go32", ob_i, 1)]
                            for hh in range(HB):
                                S.act(lambda e, hh=hh, o_in=o_in, ob_i=ob_i: e.activation(out=junkg, in_=o_in[:, hh, :], func=AF.Square,
                                                                                         accum_out=ssq[ob_i][:, hh:hh + 1]),
                                      reads=okeys, writes=["gjunk", ("gssq", ob_i)])
                            S.dve(lambda e, ob_i=ob_i: e.tensor_scalar(out=ssq[ob_i], in0=ssq[ob_i], scalar1=1.0 / 128, scalar2=EPS,
                                                                       op0=ALU.mult, op1=ALU.add), reads=[("gssq", ob_i)], writes=[("gssq", ob_i)])
                            S.act(lambda e, ob_i=ob_i: e.activation(out=ssq[ob_i], in_=ssq[ob_i], func=AF.Sqrt),
                                  reads=[("gssq", ob_i)], writes=[("gssq", ob_i)])
                            S.dve(lambda e, ob_i=ob_i: e.reciprocal(out=ssq[ob_i], in_=ssq[ob_i]), reads=[("gssq", ob_i)], writes=[("gssq", ob_i)])
                            S.dve(lambda e, o_in=o_in, ob_i=ob_i: e.tensor_tensor(
                                out=o2, in0=o_in, in1=ssq[ob_i].unsqueeze(2).to_broadcast([128, HB, 128]), op=ALU.mult),
                                reads=okeys + [("gssq", ob_i)], writes=["go2"])
                            S.pool(lambda e: e.tensor_tensor(out=o2, in0=o2, in1=gnw.unsqueeze(1).to_broadcast([128, HB, 128]), op=ALU.mult),
                                   reads=["go2", "gnw"], writes=["go2"])
                            S.act(lambda e, ob_i=ob_i: e.activation(out=zt[ob_i], in_=zt[ob_i], func=AF.Silu),
                                  reads=[("gzt", ob_i)], writes=[("gzt", ob_i)])
                            S.dve(lambda e, ob_i=ob_i: e.tensor_tensor(out=ogt[ob_i], in0=o2, in1=zt[ob_i], op=ALU.mult),
                                  reads=["go2", ("gzt", ob_i)], writes=[("gogt", ob_i)])
                            S.dma(og[grow:grow + 128, hg * HB * 128:(hg + 1) * HB * 128].rearrange("p (h c) -> p h c", h=HB), ogt[ob_i],
                                  reads=[("gogt", ob_i)], writes=[("og_d", grow // 512, hg, tt)])
                        rec_rc = S
                        S = C.S = S_real
                        C.bank_pool = None
                        pair.append((rec_pi, rec_rc))
                        if len(pair) == 2:
                            replay_merged(S_real, pending_rc, pair[0][0], pair[1][0])
                            pending_rc = [pair[0][1], pair[1][1]]
                            pair = []
                replay_merged(S_real, pending_rc, *[p_[0] for p_ in pair])
                replay_merged(S_real, [p_[1] for p_ in pair])
                pending_rc, pair = [], []
    C.barrier()


ATT_DBG = 0


def t5_bucket_np(rel):
    nb = 16
    max_exact = 8
    n = np.abs(rel)
    large = max_exact + (np.log(np.maximum(n, max_exact) / max_exact) / math.log(1024 / max_exact) * (nb - max_exact)).astype(np.int32)
    large = np.minimum(large, nb - 1)
    return (np.where(rel > 0, nb, 0) + np.where(n < max_exact, n, large)).astype(np.int32)


def host_att_consts():
    ohr = np.zeros((3, 33, 384), np.float32)
    for g, (win, d) in enumerate(GROUPS):
        for u in range(383):
            rel = (382 - u) - 191
            if abs(rel) <= 64:
                ohr[g, t5_bucket_np(np.array(rel * d)), u] = 1.0
            else:
                ohr[g, 32, u] = NEG
        ohr[g, 32, 383] = NEG
    return {"cohr": ohr.reshape(99, 384)}


def phase_att(C, qkT, vtok_a, ao_un, relb_src, zr_d, K_, dbg=None):
    S, sb = C.S, C.sb
    sb.reset()
    SEG = 4096
    if all(t % 4096 for t in C.seqs):
        SEG = min(C.seqs)
    assert all(t % SEG == 0 for t in C.seqs)
    ohr_d = C.inp("cohr", [99, 384], F32)
    tab = sb.alloc((12,), F32)
    S.dma(tab[0:32, :], relb_src, reads=[], writes=["atab"])
    TABB = sb.alloc((12, 128), F32)
    S.dve(lambda e: e.memset(TABB[32:33, :, :], 1.0), reads=[], writes=["aTABBo"])
    S.dve(lambda e: e.tensor_copy(out=TABB[0:32, :, :], in_=tab[0:32, :].unsqueeze(2).to_broadcast([32, 12, 128])),
          reads=["atab"], writes=["aTABB"])
    OHR = sb.alloc((3, 384), F32)
    S.dma(OHR[0:33, :, :], ohr_d.rearrange("(g b) u -> b g u", g=3), reads=[], writes=["aOHR"])
    BMT = sb.alloc((12, 2, 128), F32)
    frep = sb.alloc((384,), F32)
    for h in range(12):
        g = h // 4
        bk = C.bank()
        S.pe(lambda e, bk=bk, h=h, g=g: e.matmul(C.ps[bk][:, 0:384], lhsT=TABB[0:33, h, :], rhs=OHR[0:33, g, :], start=True, stop=True),
             reads=["aTABB", "aTABBo", "aOHR"], writes=[("ps", bk)])
        S.dve(lambda e, bk=bk: e.tensor_copy(out=frep, in_=C.ps[bk][:, 0:384]), reads=[("ps", bk)], writes=["afrep"])
        zr = zr_d[h]
        S.dma(zr.rearrange("(p u) -> p u", u=384), frep, reads=["afrep"], writes=[("azr", h)])
        for slot, off in ((0, 127), (1, 255)):
            src = bass.AP(zr.tensor, zr.offset + off, [[383, 128], [1, 128]])
            S.dma(BMT[:, h, slot, :], src, reads=[("azr", h)], writes=[("aBMT", h)])
    if dbg is not None:
        S.dma(dbg.rearrange("p (a b c) -> p a b c", a=12, b=2), BMT, reads=[("aBMT", h) for h in range(12)], writes=["dbg"])
        if dbg.shape[-1] == 3072:
            C.barrier()
            return
    PADM = 64 * 16
    qb = [sb.alloc((SEG,), BF16) for _ in range(2)]
    kb = [sb.alloc((PADM + SEG + PADM,), BF16) for _ in range(2)]
    qd = [sb.alloc((SEG,), BF16) for _ in range(2)]
    kd = [sb.alloc((PADM + SEG + PADM,), BF16) for _ in range(2)]
    NV = 3
    v4 = [sb.alloc((4, 65), BF16) for _ in range(NV)]
    for i in range(NV):
        S.dve(lambda e, i=i: e.memset(v4[i], 1.0), reads=[], writes=[("av4", i)])
    lg = [sb.alloc((256,), F32) for _ in range(3)]
    PT = [[sb.alloc((256,), BF16) for _ in range(4)] for _ in range(2)]
    ost = [sb.alloc((264,), BF16) for _ in range(2)]
    lrr = [0]
    vrr = [0]
    orr = [0]
    for si, T in enumerate(C.seqs):
        g0 = C.offs[si]
        for so in range(0, T, SEG):
            for g, (win, d) in enumerate(GROUPS):
                if (ATT_DBG & 4) and g > 0:
                    continue
                if (ATT_DBG & 32) and g != 2:
                    continue
                if (ATT_DBG & 8) and g != 1:
                    continue
                pad = 64 * d
                for pt in range(2):
                    rq = g * 256 + pt * 128
                    S.dma(qb[pt], qkT[rq:rq + 128, g0 + so:g0 + so + SEG], reads=[], writes=[("aqb", pt)])
                    lo, hi = so - pad, so + SEG + pad
                    dlo, dhi = max(lo, 0), min(hi, T)
                    if dlo > lo:
                        S.pool(lambda e, pt=pt, n=dlo - lo: e.memset(kb[pt][:, 0:n], 0.0), reads=[], writes=[("akb", pt)])
                    if dhi < hi:
                        S.pool(lambda e, pt=pt, a=dhi - lo, b=hi - lo: e.memset(kb[pt][:, a:b], 0.0), reads=[], writes=[("akb", pt)])
                    S.dma(kb[pt][:, dlo - lo:dhi - lo], qkT[768 + rq:768 + rq + 128, g0 + dlo:g0 + dhi], reads=[], writes=[("akb", pt)])
                Ls = SEG // d
                nq = Ls // 128
                assert nq >= 2
                if True:
                    for pt in range(2):
                        S.dve(lambda e, pt=pt, d=d: e.tensor_copy(
                            out=qd[pt][:, 0:SEG].rearrange("p (r s) -> p r s", r=d),
                            in_=qb[pt][:, 0:SEG].rearrange("p (s r) -> p r s", r=d)), reads=[("aqb", pt)], writes=[("aqd", pt)])
                        nk = SEG + 2 * pad
                        S.pool(lambda e, pt=pt, d=d, nk=nk: e.tensor_copy(
                            out=kd[pt][:, 0:nk].rearrange("p (r s) -> p r s", r=d),
                            in_=kb[pt][:, 0:nk].rearrange("p (s r) -> p r s", r=d)), reads=[("akb", pt)], writes=[("akd", pt)])
                L = T // d
                for r in range(d):
                    prev = None
                    for m in range(nq + 1):
                        if (ATT_DBG >> 8) and m >= (ATT_DBG >> 8):
                            break
                        gen = m % 2
                        sg0 = (so // d) + 128 * m - 64
                        jlo = 0 if sg0 >= 0 else 64
                        jhi = 128 if sg0 + 128 <= L else 64
                        vi = vrr[0] % NV
                        vrr[0] += 1
                        trow = g0 + (sg0 + jlo) * d + r
                        nrow = jhi - jlo
                        vsrc = bass.AP(vtok_a.tensor, vtok_a.offset + trow * 768 + g * 256, [[768 * d, nrow], [64, 4], [1, 64]])
                        if not (ATT_DBG & 1):
                            S.dma(v4[vi][jlo:jhi, :, 0:64], vsrc, reads=[], writes=[("av4", vi)])
                        qt_lo = max(m - 1, 0)
                        qt_hi = min(m, nq - 1)
                        nqc = (qt_hi - qt_lo + 1) * 128
                        bslot0 = 0 if m - 1 >= 0 else 1
                        kc0 = (128 * m - 64) * d + r + pad
                        qc0 = (128 * qt_lo) * d + r
                        for hh in range(4):
                            pt, prt = hh // 2, 64 * (hh % 2)
                            h = g * 4 + hh
                            bk = C.bank()
                            if False:
                                kop = kb[pt][prt:prt + 64, kc0:kc0 + 128]
                                qop = qb[pt][prt:prt + 64, qc0:qc0 + nqc]
                                rk = [("aqb", pt), ("akb", pt)]
                            else:
                                kop = kd[pt][prt:prt + 64, r * (Ls + 128) + 128 * m:r * (Ls + 128) + 128 * m + 128]
                                qop = qd[pt][prt:prt + 64, r * Ls + 128 * qt_lo:r * Ls + 128 * qt_lo + nqc]
                                rk = [("aqd", pt), ("akd", pt)]
                            S.pe(lambda e, bk=bk, kop=kop, qop=qop, nqc=nqc: e.matmul(
                                C.ps[bk][:, 0:nqc], lhsT=kop, rhs=qop, start=True, stop=True),
                                reads=rk, writes=[("ps", bk)])
                            li = lrr[0] % 3
                            lrr[0] += 1
                            bsl = BMT[:, h, bslot0:bslot0 + nqc // 128, :]
                            S.dve(lambda e, bk=bk, li=li, nqc=nqc, bsl=bsl: e.scalar_tensor_tensor(
                                out=lg[li][:, 0:nqc], in0=C.ps[bk][:, 0:nqc], scalar=0.125, in1=bsl.rearrange("p a b -> p (a b)"),
                                op0=ALU.mult, op1=ALU.add), reads=[("ps", bk), ("aBMT", h)], writes=[("alg", li)])
                            S.act(lambda e, li=li, nqc=nqc, gen=gen, hh=hh: e.activation(out=PT[gen][hh][:, 0:nqc], in_=lg[li][:, 0:nqc], func=AF.Exp),
                                  reads=[("alg", li)], writes=[("aPT", gen, hh)])
                        if m >= 1:
                            qi = m - 1
                            pjlo, pjhi, pvi = prev
                            oi = orr[0] % 2
                            orr[0] += 1
                            bkO = C.bank()
                            pc0 = 128 if qi >= 1 else 0
                            for hh in range(4):
                                oap = C.ps[bkO][:, hh * 128:hh * 128 + 65]
                                S.pe(lambda e, oap=oap, gen=gen, hh=hh, pc0=pc0, pjlo=pjlo, pjhi=pjhi, pvi=pvi: e.matmul(
                                    oap, lhsT=PT[1 - gen][hh][pjlo:pjhi, pc0:pc0 + 128], rhs=v4[pvi][pjlo:pjhi, hh, :], start=True, stop=False),
                                    reads=[("aPT", 1 - gen, hh), ("av4", pvi)], writes=[("ps", bkO)])
                                S.pe(lambda e, oap=oap, gen=gen, hh=hh, jlo=jlo, jhi=jhi, vi=vi: e.matmul(
                                    oap, lhsT=PT[gen][hh][jlo:jhi, 0:128], rhs=v4[vi][jlo:jhi, hh, :], start=False, stop=True),
                                    reads=[("aPT", gen, hh), ("av4", vi)], writes=[("ps", bkO)])
                            o4 = C.ps[bkO].rearrange("p (h c) -> p h c", h=4)
                            S.act(lambda e, o4=o4, oi=oi: e.activation(out=ost[oi][:, 0:256].rearrange("p (h c) -> p h c", h=4),
                                                                       in_=o4[:, :, 0:64], func=AF.Copy),
                                  reads=[("ps", bkO)], writes=[("aost", oi)])
                            S.dve(lambda e, o4=o4, oi=oi: e.tensor_copy(out=ost[oi][:, 256:260], in_=o4[:, :, 64]),
                                  reads=[("ps", bkO)], writes=[("adst", oi)])
                            S.dve(lambda e, o4=o4, oi=oi: e.tensor_tensor(out=ost[oi][:, 260:264], in0=o4[:, :, 64], in1=ost[oi][:, 256:260],
                                                                          op=ALU.subtract),
                                  reads=[("ps", bkO), ("adst", oi)], writes=[("adst2", oi)])
                            trow = g0 + so + (128 * qi) * d + r
                            odst = bass.AP(ao_un.tensor, ao_un.offset + trow * 792 + g * 264, [[792 * d, 128], [1, 264]])
                            if not (ATT_DBG & 2):
                                S.dma(odst, ost[oi], reads=[("aost", oi), ("adst", oi), ("adst2", oi)], writes=[("ao_d", trow, g)])
                        prev = (jlo, jhi, vi)
    C.barrier()


def phase_attnorm(C, ao_un, ao_n):
    S, sb = C.S, C.sb
    sb.reset()
    at = [sb.alloc((4, 792), BF16) for _ in range(2)]
    ao = [sb.alloc((4, 768), BF16) for _ in range(2)]
    tot = [sb.alloc((4, 4), F32) for _ in range(2)]
    nt = C.T // 512
    for i in range(nt):
        b = i % 2
        t0 = i * 512
        S.dma(at[b], ao_un[t0:t0 + 512, :].rearrange("(s p) c -> p s c", p=128), reads=[], writes=[("nat", b)])
        dvs = [at[b][:, :, g * 264 + 256 + 4 * k:g * 264 + 260 + 4 * k] for g in range(3) for k in range(2)]
        S.dve(lambda e, b=b, dvs=dvs: e.tensor_tensor(out=tot[b], in0=dvs[0], in1=dvs[1], op=ALU.add),
              reads=[("nat", b)], writes=[("ntot", b)])
        for kk in range(2, 6):
            S.dve(lambda e, b=b, dvs=dvs, kk=kk: e.tensor_tensor(out=tot[b], in0=tot[b], in1=dvs[kk], op=ALU.add),
                  reads=[("nat", b), ("ntot", b)], writes=[("ntot", b)])
        S.dve(lambda e, b=b: e.reciprocal(out=tot[b], in_=tot[b]), reads=[("ntot", b)], writes=[("ntot", b)])
        for s_ in range(4):
            for g in range(3):
                eng = S.dve if (s_ * 3 + g) % 2 == 0 else S.pool
                eng(lambda e, b=b, s_=s_, g=g: e.tensor_tensor(
                    out=ao[b][:, s_, g * 256:(g + 1) * 256].rearrange("p (j c) -> p j c", j=4),
                    in0=at[b][:, s_, g * 264:g * 264 + 256].rearrange("p (j c) -> p j c", j=4),
                    in1=tot[b][:, s_, :].unsqueeze(2).to_broadcast([128, 4, 64]), op=ALU.mult),
                    reads=[("nat", b), ("ntot", b)], writes=[("nao", b, s_, g)])
        S.dma(ao_n[t0:t0 + 512, :].rearrange("(s p) c -> p s c", p=128), ao[b],
              reads=[("nao", b, s_, g) for s_ in range(4) for g in range(3)], writes=[("aon_d", i)])
    C.barrier()


def host_consts():
    idx = np.arange(128)
    same = (idx[:, None] // 64) == (idx[None, :] // 64)
    UF = (same & (idx[:, None] <= idx[None, :])).astype(np.float32)
    UB = (same & (idx[:, None] >= idx[None, :])).astype(np.float32)
    CH0 = np.repeat((idx < 64).astype(np.float32)[:, None], 128, 1)
    CH1 = np.repeat((idx >= 64).astype(np.float32)[:, None], 128, 1)
    MA_f = np.where(same & (idx[None, :] < idx[:, None]), 0.0, NEG).astype(np.float32)
    MA_b = np.where(same & (idx[None, :] > idx[:, None]), 0.0, NEG).astype(np.float32)
    SEL = np.zeros((16, 16, 128), np.float32)
    for h in range(16):
        SEL[h, h, :] = 1.0
    c32 = np.concatenate([np.eye(128, dtype=np.float32), UF, UB, CH0, CH1], axis=1)
    cbf = np.concatenate([np.eye(128, dtype=np.float32), np.ones((128, 128), np.float32),
                          -np.ones((128, 128), np.float32), MA_f, MA_b], axis=1).astype(ml_dtypes.bfloat16)
    return {"c32": c32, "cbf": cbf, "csel": SEL.reshape(16, 2048)}


def load_consts(C):
    S, sb = C.S, C.sb
    c32_d = C.inp("c32", [128, 640], F32)
    cbf_d = C.inp("cbf", [128, 640], BF16)
    c32 = sb.alloc((5, 128), F32)
    cbf = sb.alloc((5, 128), BF16)
    S.dma(c32, c32_d.rearrange("p (a b) -> p a b", a=5), reads=[], writes=["c32", "ident32"])
    S.dma(cbf, cbf_d.rearrange("p (a b) -> p a b", a=5), reads=[], writes=["cbf", "ident_bf"])
    K_ = {"ident32": c32[:, 0, :], "UF32": c32[:, 1, :], "UB32": c32[:, 2, :], "CH0": c32[:, 3, :], "CH1": c32[:, 4, :],
          "ident_bf": cbf[:, 0, :], "ones_bf": cbf[:, 1, :], "negones_bf": cbf[:, 2, :], "MA_f": cbf[:, 3, :],
          "MA_b": cbf[:, 4, :], "keys": ["c32", "cbf"]}
    sb.set_mark()
    return K_


W_NAMES = ["norm_mix", "norm_ffn", "norm_final", "gdn_w_in", "gdn_conv", "gdn_a_log", "gdn_dt_bias", "gdn_norm",
           "gdn_w_out", "att_w_in", "att_w_out", "rel_bias", "ffn_w_up", "ffn_conv", "ffn_conv_b", "ffn_w_down"]

WIN = 6144


def declare_weights(C):
    W = {}
    W["norm_mix"] = C.inp("norm_mix", [2, D]); W["norm_ffn"] = C.inp("norm_ffn", [2, D]); W["norm_final"] = C.inp("norm_final", [1, D])
    W["gdn_w_in"] = C.inp("gdn_w_in", [D, GDN_IN]); W["gdn_conv"] = C.inp("gdn_conv", [5, 3072])
    W["gdn_a_log"] = C.inp("gdn_a_log", [1, 16]); W["gdn_dt_bias"] = C.inp("gdn_dt_bias", [1, 16]); W["gdn_norm"] = C.inp("gdn_norm", [1, 128])
    W["gdn_w_out"] = C.inp("gdn_w_out", [D, D]); W["att_w_in"] = C.inp("att_w_in", [D, 2304]); W["att_w_out"] = C.inp("att_w_out", [768, D])
    W["rel_bias"] = C.inp("rel_bias", [32, 12]); W["ffn_w_up"] = C.inp("ffn_w_up", [2, D, 2 * D_FF]); W["ffn_conv"] = C.inp("ffn_conv", [2, 3, 2 * D_FF])
    W["ffn_conv_b"] = C.inp("ffn_conv_b", [2, 2 * D_FF]); W["ffn_w_down"] = C.inp("ffn_w_down", [2, D_FF, D])
    return W


def chain_gdn_layer(C, W, K_, x_in, og, pfx):
    T = C.T
    xn0 = C.scratch(pfx + "xn0", [T, D], BF16)
    projT = C.scratch(pfx + "projT", [3072, T], BF16)
    ztok = C.scratch(pfx + "ztok", [T, 1024], BF16)
    gates = C.scratch(pfx + "gates", [T, 32], F32)
    of_s = C.scratch(pfx + "of_s", [T, 1024], BF16)
    phase_norm0(C, x_in, W["norm_mix"][0:1, :], xn0)
    phase_proj(C, xn0, W["gdn_w_in"], GDN_IN, fm=[(0, 3072, projT)],
               tm=[(3072, 512, ztok[:, 0:512], BF16), (3584, 512, ztok[:, 512:1024], BF16), (4096, 32, gates, F32)], tag=pfx + "gp")
    phase_gdn(C, projT, ztok, gates, of_s, og, K_, W["gdn_conv"], W["gdn_a_log"], W["gdn_dt_bias"], W["gdn_norm"])


def chain_rest(C, W, K_, x_in, og, y_out, pfx):
    S = C.S
    T, TP = C.T, C.TP
    xrp = C.scratch(pfx + "xrp", [TP, D], F32)
    xnp = C.scratch(pfx + "xnp", [TP, D], BF16)
    x1 = C.scratch(pfx + "x1", [T, D], F32)
    xn1 = C.scratch(pfx + "xn1", [T, D], BF16)
    qkT = C.scratch(pfx + "qkT", [1536, T], BF16)
    vtok = C.scratch(pfx + "vtok", [T, 768], BF16)
    ao_un = C.scratch(pfx + "ao_un", [T, 792], BF16)
    ao_n = C.scratch(pfx + "ao_n", [T, 768], BF16)
    zr = C.scratch(pfx + "zr", [12, 128 * 384], F32)
    phase_outproj(C, og, D, W["gdn_w_out"], x_in, W["norm_ffn"][0:1, :], xrp, xnp, pfx + "op0", "og_d")

    def outs0(si, t_lo, n, xt_ap, xo_ap, kx, ko):
        g = C.offs[si] + t_lo
        S.dma(x1[g:g + n, :], xt_ap, reads=kx, writes=[("x1d", g)])
        S.dma(xn1[g:g + n, :], xo_ap, reads=ko, writes=[("xn1d", g)])
    phase_ffn(C, xnp, xrp, W["ffn_w_up"][0], W["ffn_w_down"][0], W["ffn_conv"][0], W["ffn_conv_b"][0], W["norm_mix"][1:2, :],
              K_["ident32"], outs0, pfx + "f0", final=False)
    phase_proj(C, xn1, W["att_w_in"], 2304, fm=[(0, 1536, qkT)],
               tm=[(1536, 512, vtok[:, 0:512], BF16), (2048, 256, vtok[:, 512:768], BF16)], tag=pfx + "ap")
    phase_att(C, qkT, vtok, ao_un, W["rel_bias"], zr, K_)
    phase_attnorm(C, ao_un, ao_n)
    phase_outproj(C, ao_n, 768, W["att_w_out"], x1, W["norm_ffn"][1:2, :], xrp, xnp, pfx + "op1", "aon_d")

    def outs1(si, t_lo, n, xt_ap, xo_ap, kx, ko):
        g = C.offs[si] + t_lo
        S.dma(y_out[g:g + n, :], xo_ap, reads=ko, writes=[("yd", g)])
    phase_ffn(C, xnp, xrp, W["ffn_w_up"][1], W["ffn_w_down"][1], W["ffn_conv"][1], W["ffn_conv_b"][1], W["norm_final"],
              K_["ident32"], outs1, pfx + "f1", final=True)


def build_program(seqs, debug=False):
    import contextlib
    nc = bass.Bass("TRN2", target_bir_lowering=False)
    C = Ctx(nc, seqs, debug=debug)
    x_all = C.inp("x_all", [C.T, D])
    W = declare_weights(C)
    y_all = C.scratch("y_all", [C.T, D], F32, out=True)
    og = C.scratch("og", [C.T, 1024], BF16)
    K_ = load_consts(C)
    chain_gdn_layer(C, W, K_, x_all, og, "")
    chain_rest(C, W, K_, x_all, og, y_all, "")
    stack = contextlib.ExitStack()
    C.S.emit(stack)
    return nc, C, stack


def build_program_A(sample_seqs, Tp, debug=False):
    import contextlib
    nc = bass.Bass("TRN2", target_bir_lowering=False)
    C = Ctx(nc, sample_seqs, debug=debug)
    x_all = C.inp("x_all", [C.T, D])
    W = declare_weights(C)
    y_all = C.scratch("y_all", [C.T, D], F32, out=True)
    og = C.scratch("og", [C.T, 1024], BF16)
    K_ = load_consts(C)
    Cp = C.with_seqs([Tp])
    x_p = C.inp("x_p", [Tp, D])
    w_h = C.inp("gdn_w_in_h", [D, 516]); conv_h = C.inp("gdn_conv_h", [5, 384])
    alog_h = C.inp("gdn_a_log_h", [1, 2]); dtb_h = C.inp("gdn_dt_bias_h", [1, 2])
    og_p = C.scratch("og_p", [Tp, 128], BF16, out=True)
    xn0p = C.scratch("p_xn0", [Tp, D], BF16)
    projTp = C.scratch("p_projT", [384, Tp], BF16)
    ztokp = C.scratch("p_ztok", [Tp, 128], BF16)
    gatesp = C.scratch("p_gates", [Tp, 4], F32)
    ofp = C.scratch("p_of", [Tp, 128], BF16)
    phase_norm0(Cp, x_p, W["norm_mix"][0:1, :], xn0p)
    phase_proj(Cp, xn0p, w_h, 516, fm=[(0, 384, projTp)], tm=[(384, 128, ztokp, BF16), (512, 4, gatesp, F32)], tag="pgp")
    phase_gdn(Cp, projTp, ztokp, gatesp, ofp, og_p, K_, conv_h, alog_h, dtb_h, W["gdn_norm"], NH=1, HB=1)
    chain_gdn_layer(C, W, K_, x_all, og, "")
    chain_rest(C, W, K_, x_all, og, y_all, "")
    stack = contextlib.ExitStack()
    C.S.emit(stack)
    return nc, C, stack


def build_program_B(win, debug=False):
    import contextlib
    nc = bass.Bass("TRN2", target_bir_lowering=False)
    C = Ctx(nc, [win], debug=debug)
    x_w = C.inp("x_w", [win, D])
    og_w = C.inp("og_w", [win, 1024], BF16)
    W = declare_weights(C)
    y_w = C.scratch("y_w", [win, D], F32, out=True)
    K_ = load_consts(C)
    chain_rest(C, W, K_, x_w, og_w, y_w, "")
    stack = contextlib.ExitStack()
    C.S.emit(stack)
    return nc, C, stack


def weight_map(w):
    m = {}
    m["norm_mix"] = np.asarray(w["norm_mix"], np.float32)
    m["norm_ffn"] = np.asarray(w["norm_ffn"], np.float32)
    m["norm_final"] = np.asarray(w["norm_final"], np.float32).reshape(1, D)
    m["gdn_w_in"] = np.asarray(w["gdn_w_in"], np.float32)[0]
    m["gdn_conv"] = np.asarray(w["gdn_conv"], np.float32)[0]
    m["gdn_a_log"] = np.asarray(w["gdn_a_log"], np.float32)[0].reshape(1, 16)
    m["gdn_dt_bias"] = np.asarray(w["gdn_dt_bias"], np.float32)[0].reshape(1, 16)
    m["gdn_norm"] = np.asarray(w["gdn_norm"], np.float32)[0].reshape(1, 128)
    m["gdn_w_out"] = np.asarray(w["gdn_w_out"], np.float32)[0]
    m["att_w_in"] = np.asarray(w["att_w_in"], np.float32)[0]
    m["att_w_out"] = np.asarray(w["att_w_out"], np.float32)[0]
    m["rel_bias"] = np.asarray(w["rel_bias"], np.float32)
    m["ffn_w_up"] = np.asarray(w["ffn_w_up"], np.float32)
    m["ffn_conv"] = np.asarray(w["ffn_conv"], np.float32)
    m["ffn_conv_b"] = np.asarray(w["ffn_conv_b"], np.float32)
    m["ffn_w_down"] = np.asarray(w["ffn_w_down"], np.float32)
    m.update(host_consts())
    m.update(host_att_consts())
    return m


def make_in_map(x_rows, w):
    m = weight_map(w)
    m["x_all"] = np.ascontiguousarray(x_rows, dtype=np.float32)
    return m


def head_slices(wm, h):
    w_in = wm["gdn_w_in"]
    cols = ([h * 128 + i for i in range(128)] + [1024 + h * 128 + i for i in range(128)] + [2048 + h * 128 + i for i in range(128)]
            + [3072 + h * 128 + i for i in range(128)] + [4096 + k * 8 + h for k in range(4)])
    ccols = [h * 128 + i for i in range(128)] + [1024 + h * 128 + i for i in range(128)] + [2048 + h * 128 + i for i in range(128)]
    return {"gdn_w_in_h": np.ascontiguousarray(w_in[:, cols]),
            "gdn_conv_h": np.ascontiguousarray(wm["gdn_conv"][:, ccols]),
            "gdn_a_log_h": np.ascontiguousarray(wm["gdn_a_log"][:, [h, 8 + h]]),
            "gdn_dt_bias_h": np.ascontiguousarray(wm["gdn_dt_bias"][:, [h, 8 + h]])}


def kernel(**inputs):
    x_prompt = np.asarray(inputs["x_prompt"], np.float32)
    x_sample = np.asarray(inputs["x_sample"], np.float32)
    ncore = 8
    ns = x_sample.shape[0] // ncore
    Ts = x_sample.shape[1]
    Tp = x_prompt.shape[1]
    wm = weight_map(inputs)
    nc, C, stack = build_program_A([Ts] * ns, Tp)
    with stack:
        in_maps = []
        for c in range(ncore):
            m = dict(wm)
            m["x_all"] = np.ascontiguousarray(x_sample[c * ns:(c + 1) * ns].reshape(ns * Ts, D))
            m["x_p"] = x_prompt[0]
            m.update(head_slices(wm, c))
            in_maps.append({k: m[k] for k in C.inputs})
        resA = run_bass_kernel_spmd(nc, in_maps, core_ids=list(range(ncore)))
    y_sample = np.empty_like(x_sample)
    og_full = np.empty((Tp, 1024), dtype=ml_dtypes.bfloat16)
    for c in range(ncore):
        y_sample[c * ns:(c + 1) * ns] = resA.results[c]["y_all"].reshape(ns, Ts, D)
        og_full[:, c * 128:(c + 1) * 128] = resA.results[c]["og_p"]
    share = Tp // ncore
    nc2, C2, stack2 = build_program_B(WIN)
    starts = [min(max(c * share - (WIN - share) // 2, 0), Tp - WIN) for c in range(ncore)]
    with stack2:
        in_maps = []
        for c in range(ncore):
            m = dict(wm)
            m["x_w"] = np.ascontiguousarray(x_prompt[0, starts[c]:starts[c] + WIN])
            m["og_w"] = np.ascontiguousarray(og_full[starts[c]:starts[c] + WIN])
            in_maps.append({k: m[k] for k in C2.inputs})
        resB = run_bass_kernel_spmd(nc2, in_maps, core_ids=list(range(ncore)))
    y_prompt = np.empty_like(x_prompt)
    for c in range(ncore):
        o = c * share - starts[c]
        y_prompt[0, c * share:(c + 1) * share] = resB.results[c]["y_w"][o:o + share]
    return (y_prompt, y_sample)
```
